# Optimizing a Trainium2 kernel written in Bass

```python
import math
import jax, jax.numpy as jnp
from jax import lax
import numpy as np

D_MODEL = 1024
BATCH = 32
SEQ = 256
DEPTH = 4
DEC_BATCH = 4
DEC_SEQ = 2048
PAST_LEN = 256

GRID_W = 64
N_MIXERS = 2
N_SSM_LAYERS = (DEPTH + 1) // 2
N_POOL_LAYERS = DEPTH // 2
SSM_GROUP = 16
SSM_GROUPS = D_MODEL // SSM_GROUP
SSM_STATE = 64
N_DIRS = 2
POOL_WINDOWS = (2, 4, 8, 16)
N_POOL_GROUPS = len(POOL_WINDOWS)
POOL_GROUP = D_MODEL // N_POOL_GROUPS
D_FF = -(-8 * D_MODEL // (3 * 256)) * 256
DEEPNORM_ALPHA = (2 * DEPTH) ** 0.25
DEEPNORM_BETA = (8 * DEPTH) ** -0.25
LN_EPS = 1e-5
DT_MIN = 0.001
DT_MAX = 0.1

kernel_name = 'hybrid_s5_pool_prefix_diffusion_step'

F32 = jnp.float32


def layer_norm(x, g, b):
    xf = x.astype(F32)
    mu = jnp.mean(xf, axis=-1, keepdims=True)
    var = jnp.mean(jnp.square(xf - mu), axis=-1, keepdims=True)
    y = (xf - mu) * lax.rsqrt(var + LN_EPS)
    return (y * g + b).astype(x.dtype)


def ada_mod(cond, w, b):
    m = (jax.nn.silu(cond) @ w + b)[..., None, :]
    return jnp.split(m, 6, axis=-1)


def modulate(x, shift, scale):
    return x * (1 + scale) + shift


def s5_discretize(a_re, a_im, log_step, b_re, b_im):
    a_re = a_re.astype(F32)
    a_im = a_im.astype(F32)
    b_re = b_re.astype(F32)
    b_im = b_im.astype(F32)
    dt = jnp.exp(log_step.astype(F32))[:, None]
    mag = jnp.exp(a_re * dt)
    ang = a_im * dt
    ab_re = mag * jnp.cos(ang)
    ab_im = mag * jnp.sin(ang)
    den = jnp.square(a_re) + jnp.square(a_im)
    num_re = ab_re - 1.0
    coef_re = (num_re * a_re + ab_im * a_im) / den
    coef_im = (ab_im * a_re - num_re * a_im) / den
    bb_re = coef_re[..., None] * b_re - coef_im[..., None] * b_im
    bb_im = coef_re[..., None] * b_im + coef_im[..., None] * b_re
    return ab_re, ab_im, bb_re, bb_im


def _linear_recurrence_combine(e1, e2):
    a1r, a1i, b1r, b1i = e1
    a2r, a2i, b2r, b2i = e2
    return (a2r * a1r - a2i * a1i,
            a2r * a1i + a2i * a1r,
            a2r * b1r - a2i * b1i + b2r,
            a2r * b1i + a2i * b1r + b2i)


def s5_scan(u, ab_re, ab_im, bb_re, bb_im, h0_re, h0_im, reverse):
    bu_re = jnp.einsum('blgh,gph->blgp', u, bb_re)
    bu_im = jnp.einsum('blgh,gph->blgp', u, bb_im)
    first = -1 if reverse else 0
    h0_re = h0_re.astype(F32)
    h0_im = h0_im.astype(F32)
    bu_re = bu_re.at[:, first].add(ab_re * h0_re - ab_im * h0_im)
    bu_im = bu_im.at[:, first].add(ab_re * h0_im + ab_im * h0_re)
    a_re = jnp.broadcast_to(ab_re, bu_re.shape)
    a_im = jnp.broadcast_to(ab_im, bu_re.shape)
    _, _, s_re, s_im = lax.associative_scan(
        _linear_recurrence_combine, (a_re, a_im, bu_re, bu_im), axis=1, reverse=reverse)
    last = 0 if reverse else -1
    return s_re, s_im, s_re[:, last], s_im[:, last]


def s5_mixer(h, a_re, a_im, log_step, b_re, b_im, c_re, c_im, d_skip, glu_w, glu_b, h0_re, h0_im):
    bsz, seq_len, _ = h.shape
    u = h.astype(F32).reshape(bsz, seq_len, SSM_GROUPS, SSM_GROUP)
    y = u * d_skip.astype(F32).reshape(SSM_GROUPS, SSM_GROUP)
    fin_re, fin_im = [], []
    for d in range(N_DIRS):
        ab_re, ab_im, bb_re, bb_im = s5_discretize(a_re[d], a_im[d], log_step[d], b_re[d], b_im[d])
        s_re, s_im, f_re, f_im = s5_scan(u, ab_re, ab_im, bb_re, bb_im,
                                         h0_re[:, d], h0_im[:, d], reverse=(d == 1))
        y = y + (jnp.einsum('blgp,ghp->blgh', s_re, c_re[d].astype(F32))
                 - jnp.einsum('blgp,ghp->blgh', s_im, c_im[d].astype(F32)))
        fin_re.append(f_re)
        fin_im.append(f_im)
    y = jax.nn.gelu(y.reshape(bsz, seq_len, D_MODEL))
    v = y @ glu_w + glu_b
    val, gate = jnp.split(v, 2, axis=-1)
    out = val * jax.nn.sigmoid(gate)
    return out.astype(h.dtype), jnp.stack(fin_re, axis=1), jnp.stack(fin_im, axis=1)


def _window_bounds(n, k):
    pos = jnp.arange(n)
    lo = jnp.clip(pos - k // 2, 0, n)
    hi = jnp.clip(pos - k // 2 + k, 0, n)
    return lo, hi


def pool_seq(x, k):
    seq_len = x.shape[1]
    s = jnp.pad(jnp.cumsum(x, axis=1), ((0, 0), (1, 0), (0, 0)))
    lo, hi = _window_bounds(seq_len, k)
    cnt = (hi - lo).astype(F32)[None, :, None]
    return (jnp.take(s, hi, axis=1) - jnp.take(s, lo, axis=1)) / cnt


def pool_grid(x, k):
    bsz, seq_len, ch = x.shape
    rows = seq_len // GRID_W
    g = x.reshape(bsz, rows, GRID_W, ch)
    s = jnp.cumsum(jnp.cumsum(g, axis=1), axis=2)
    s = jnp.pad(s, ((0, 0), (1, 0), (1, 0), (0, 0)))
    rlo, rhi = _window_bounds(rows, k)
    clo, chi = _window_bounds(GRID_W, k)
    s_rhi = jnp.take(s, rhi, axis=1)
    s_rlo = jnp.take(s, rlo, axis=1)
    tot = (jnp.take(s_rhi, chi, axis=2) - jnp.take(s_rhi, clo, axis=2)
           - jnp.take(s_rlo, chi, axis=2) + jnp.take(s_rlo, clo, axis=2))
    cnt = ((rhi - rlo)[:, None] * (chi - clo)[None, :]).astype(F32)[None, :, :, None]
    return (tot / cnt).reshape(bsz, seq_len, ch)


def pool_mixer(h, w_groups, scale, grid):
    hf = h.astype(F32)
    outs = []
    for gi, k in enumerate(POOL_WINDOWS):
        xg = hf[..., gi * POOL_GROUP:(gi + 1) * POOL_GROUP]
        pooled = pool_grid(xg, k) if grid else pool_seq(xg, k)
        outs.append((pooled - xg) @ w_groups[gi])
    return (jnp.concatenate(outs, axis=-1) * scale).astype(h.dtype)


def swiglu(h, w_gate, w_up, w_down):
    return (jax.nn.silu(h @ w_gate) * (h @ w_up)) @ w_down


def setup_inputs(seed: int = 0) -> dict:
    key = jax.random.key(seed)
    ks = jax.random.split(key, 32)
    nrm = jax.random.normal
    G, P, HG = SSM_GROUPS, SSM_STATE, SSM_GROUP
    x_prompt = nrm(ks[0], (BATCH, SEQ, D_MODEL), F32)
    x_sample = nrm(ks[1], (DEC_BATCH, DEC_SEQ, D_MODEL), F32)
    state_ssm_re = 0.1 * nrm(ks[2], (DEC_BATCH, N_SSM_LAYERS, N_DIRS, G, P), F32)
    state_ssm_im = 0.1 * nrm(ks[3], (DEC_BATCH, N_SSM_LAYERS, N_DIRS, G, P), F32)
    c = nrm(ks[4], (DEC_BATCH, D_MODEL), F32)
    c_ctx = nrm(ks[5], (D_MODEL,), F32)
    ada_w = nrm(ks[6], (DEPTH, D_MODEL, 6 * D_MODEL), F32) * D_MODEL ** -0.5
    ada_b = 0.01 * nrm(ks[7], (DEPTH, 6 * D_MODEL), F32)
    ln_g = 1.0 + 0.01 * nrm(ks[8], (DEPTH, 2, D_MODEL), F32)
    ln_b = 0.01 * nrm(ks[9], (DEPTH, 2, D_MODEL), F32)
    ssm_a_re = -0.5 + 0.01 * nrm(ks[10], (N_SSM_LAYERS, N_DIRS, G, P), F32)
    ssm_a_im = (math.pi * jnp.arange(P, dtype=F32)
                + 0.01 * nrm(ks[11], (N_SSM_LAYERS, N_DIRS, G, P), F32))
    ssm_log_step = jax.random.uniform(ks[12], (N_SSM_LAYERS, N_DIRS, G), F32,
                                      minval=math.log(DT_MIN), maxval=math.log(DT_MAX))
    ssm_b_re = nrm(ks[13], (N_SSM_LAYERS, N_DIRS, G, P, HG), F32) * (2 * HG) ** -0.5
    ssm_b_im = nrm(ks[14], (N_SSM_LAYERS, N_DIRS, G, P, HG), F32) * (2 * HG) ** -0.5
    ssm_c_re = nrm(ks[15], (N_SSM_LAYERS, N_DIRS, G, HG, P), F32) * (2 * P) ** -0.5
    ssm_c_im = nrm(ks[16], (N_SSM_LAYERS, N_DIRS, G, HG, P), F32) * (2 * P) ** -0.5
    ssm_d = nrm(ks[17], (N_SSM_LAYERS, D_MODEL), F32)
    ssm_glu_w = nrm(ks[18], (N_SSM_LAYERS, D_MODEL, 2 * D_MODEL), F32) * (D_MODEL ** -0.5 * DEEPNORM_BETA)
    ssm_glu_b = 0.01 * nrm(ks[19], (N_SSM_LAYERS, 2 * D_MODEL), F32)
    pool_w = nrm(ks[20], (N_POOL_LAYERS, N_POOL_GROUPS, POOL_GROUP, POOL_GROUP), F32) * (POOL_GROUP ** -0.5 * DEEPNORM_BETA)
    pool_scale = 1.0 + 0.01 * nrm(ks[21], (N_POOL_LAYERS, D_MODEL), F32)
    ffn_w_gate = nrm(ks[22], (DEPTH, D_MODEL, D_FF), F32) * D_MODEL ** -0.5
    ffn_w_up = nrm(ks[23], (DEPTH, D_MODEL, D_FF), F32) * D_MODEL ** -0.5
    ffn_w_down = nrm(ks[24], (DEPTH, D_FF, D_MODEL), F32) * (D_FF ** -0.5 * DEEPNORM_BETA)
    return {'x_prompt': x_prompt, 'x_sample': x_sample,
            'state_ssm_re': state_ssm_re, 'state_ssm_im': state_ssm_im,
            'c': c, 'c_ctx': c_ctx, 'ada_w': ada_w, 'ada_b': ada_b,
            'ln_g': ln_g, 'ln_b': ln_b,
            'ssm_a_re': ssm_a_re, 'ssm_a_im': ssm_a_im, 'ssm_log_step': ssm_log_step,
            'ssm_b_re': ssm_b_re, 'ssm_b_im': ssm_b_im, 'ssm_c_re': ssm_c_re, 'ssm_c_im': ssm_c_im,
            'ssm_d': ssm_d, 'ssm_glu_w': ssm_glu_w, 'ssm_glu_b': ssm_glu_b,
            'pool_w': pool_w, 'pool_scale': pool_scale,
            'ffn_w_gate': ffn_w_gate, 'ffn_w_up': ffn_w_up, 'ffn_w_down': ffn_w_down}


def reference(x_prompt, x_sample, state_ssm_re, state_ssm_im, c, c_ctx, ada_w, ada_b,
              ln_g, ln_b, ssm_a_re, ssm_a_im, ssm_log_step, ssm_b_re, ssm_b_im,
              ssm_c_re, ssm_c_im, ssm_d, ssm_glu_w, ssm_glu_b, pool_w, pool_scale,
              ffn_w_gate, ffn_w_up, ffn_w_down):
    xc = x_prompt
    xl = x_sample
    new_re, new_im = [], []
    for i in range(DEPTH):
        sh1c, sc1c, g1c, sh2c, sc2c, g2c = ada_mod(c_ctx, ada_w[i], ada_b[i])
        sh1l, sc1l, g1l, sh2l, sc2l, g2l = ada_mod(c, ada_w[i], ada_b[i])
        hc = modulate(xc, sh1c, sc1c)
        hl = modulate(xl, sh1l, sc1l)
        j = i // N_MIXERS
        if i % N_MIXERS == 0:
            params = (ssm_a_re[j], ssm_a_im[j], ssm_log_step[j], ssm_b_re[j], ssm_b_im[j],
                      ssm_c_re[j], ssm_c_im[j], ssm_d[j], ssm_glu_w[j], ssm_glu_b[j])
            zeros = jnp.zeros((xc.shape[0], N_DIRS, SSM_GROUPS, SSM_STATE), F32)
            oc, fin_re, fin_im = s5_mixer(hc, *params, zeros, zeros)
            ol, _, _ = s5_mixer(hl, *params, state_ssm_re[:, j], state_ssm_im[:, j])
            new_re.append(fin_re)
            new_im.append(fin_im)
        else:
            oc = pool_mixer(hc, pool_w[j], pool_scale[j], grid=False)
            ol = pool_mixer(hl, pool_w[j], pool_scale[j], grid=True)
        xc = layer_norm(DEEPNORM_ALPHA * xc + g1c * oc, ln_g[i, 0], ln_b[i, 0])
        xl = layer_norm(DEEPNORM_ALPHA * xl + g1l * ol, ln_g[i, 0], ln_b[i, 0])
        fc = swiglu(modulate(xc, sh2c, sc2c), ffn_w_gate[i], ffn_w_up[i], ffn_w_down[i])
        fl = swiglu(modulate(xl, sh2l, sc2l), ffn_w_gate[i], ffn_w_up[i], ffn_w_down[i])
        xc = layer_norm(DEEPNORM_ALPHA * xc + g2c * fc, ln_g[i, 1], ln_b[i, 1])
        xl = layer_norm(DEEPNORM_ALPHA * xl + g2l * fl, ln_g[i, 1], ln_b[i, 1])
    new_state_ssm_re = jnp.stack(new_re, axis=1).astype(state_ssm_re.dtype)
    new_state_ssm_im = jnp.stack(new_im, axis=1).astype(state_ssm_im.dtype)
    return (xc, xl, new_state_ssm_re, new_state_ssm_im)
```

```python
import numpy as np
import concourse.bass as bass
import concourse.mybir as mybir
from concourse.bass_utils import run_bass_kernel_spmd

F32 = mybir.dt.float32
BF16 = mybir.dt.bfloat16
AF = mybir.ActivationFunctionType
ALU = mybir.AluOpType

D = 1024
DFF = 2816
NFT = 22
DEPTH = 4
ALPHA = (2 * DEPTH) ** 0.25
EPS = 1e-5
NCORES = 8
WINDOWS = (2, 4, 8, 16)


class Prog:
    ENG = ("pe", "act", "dve", "pool", "sp")

    def __init__(self, nc, n_dma_sems=6):
        self.nc = nc
        self.ops = {e: [] for e in self.ENG}
        self.count = {}
        self.waited = {e: {} for e in self.ENG}
        self.bufs = {}
        self.n_dma = n_dma_sems
        self.dma_rr = {e: 0 for e in self.ENG}
        self.out_tickets = []

    def _deps(self, reads, writes):
        w = {}

        def add(t):
            if t is None:
                return
            k, v = t
            if w.get(k, 0) < v:
                w[k] = v

        for key in reads:
            b = self.bufs.get(key)
            if b:
                add(b["w"])
        for key in writes:
            b = self.bufs.get(key)
            if b:
                add(b["w"])
                for k, v in b["r"].items():
                    add((k, v))
        return w

    def _filter(self, eng, w, skip_self=False):
        out = []
        for k, v in w.items():
            if skip_self and k == eng:
                continue
            if self.waited[eng].get(k, 0) >= v:
                continue
            self.waited[eng][k] = v
            out.append((k, v))
        return out

    def _commit(self, ticket, reads, writes):
        k, v = ticket
        for key in reads:
            b = self.bufs.setdefault(key, {"w": None, "r": {}})
            if b["r"].get(k, 0) < v:
                b["r"][k] = v
        for key in writes:
            self.bufs[key] = {"w": ticket, "r": {}}

    def op(self, eng, fn, reads=(), writes=(), signal=True, extra_waits=()):
        w = self._deps(reads, writes)
        for t in extra_waits:
            if t is not None and w.get(t[0], 0) < t[1]:
                w[t[0]] = t[1]
        waits = self._filter(eng, w, skip_self=(eng == "pe"))
        inc = None
        ticket = None
        if signal:
            self.count[eng] = self.count.get(eng, 0) + 1
            ticket = (eng, self.count[eng])
            inc = (eng, 1)
            self._commit(ticket, reads, writes)
        self.ops[eng].append((waits, fn, inc))
        return ticket

    def dma(self, eng, fn, reads=(), writes=(), is_output=False):
        i = self.dma_rr[eng]
        self.dma_rr[eng] = (i + 1) % self.n_dma
        sk = "dma_%s_%d" % (eng, i)
        prev = self.count.get(sk, 0)
        w = self._deps(reads, writes)
        if prev > 0 and w.get(sk, 0) < prev:
            w[sk] = prev
        waits = self._filter(eng, w)
        self.count[sk] = prev + 16
        ticket = (sk, prev + 16)
        self._commit(ticket, reads, writes)
        self.ops[eng].append((waits, fn, (sk, 16)))
        if is_output:
            self.out_tickets.append(ticket)
        return ticket

    def barrier(self):
        for e in self.ENG:
            w = {k: v for k, v in self.count.items()}
            waits = self._filter(e, w, skip_self=False)
            waits = [(k, v) for k, v in waits if k != e]
            if waits:
                self.ops[e].append((waits, None, None))

    def finish(self):
        w = {}
        for k, v in self.out_tickets:
            w[k] = max(w.get(k, 0), v)
        waits = self._filter("sp", w)
        if waits:
            self.ops["sp"].append((waits, None, None))

    def replay(self):
        nc = self.nc
        sem_names = sorted(self.count.keys())
        import contextlib

        with contextlib.ExitStack() as st:
            sems = {k: st.enter_context(nc.semaphore("s_" + k)) for k in sem_names}
            block = st.enter_context(nc.Block())
            engs = {"pe": block.tensor, "act": block.scalar, "dve": block.vector,
                    "pool": block.gpsimd, "sp": block.sync}

            def mk(ename):
                ops = self.ops[ename]

                def body(e):
                    for waits, fn, inc in ops:
                        for k, v in waits:
                            e.wait_ge(sems[k], v)
                        if fn is not None:
                            ins = fn(e)
                            if inc is not None:
                                ins.then_inc(sems[inc[0]], inc[1])
                return body

            for ename in self.ENG:
                engs[ename](mk(ename))


class StopBuild(Exception):
    pass


class Buf:
    def __init__(self, tens, width, off, shape, pbase=0):
        self.t = tens
        self.W = width
        self.off = off
        self.shape = tuple(shape)
        st = []
        acc = 1
        for n in reversed(self.shape):
            st.append(acc)
            acc *= n
        self.strides = tuple(reversed(st))
        self.size = acc
        self.pbase = pbase

    def ap(self, dims, off=0, p0=0, np_=128):
        return bass.AP(self.t, (self.pbase + p0) * self.W + self.off + off,
                       [[self.W, np_]] + [list(d) for d in dims])

    def __getitem__(self, idx):
        if not isinstance(idx, tuple):
            idx = (idx,)
        ps = idx[0]
        p0, p1 = (ps.start or 0), (128 if ps.stop is None else ps.stop)
        off = 0
        dims = []
        rest = idx[1:]
        for i, n in enumerate(self.shape):
            s = self.strides[i]
            if i < len(rest):
                r = rest[i]
                if isinstance(r, int):
                    off += r * s
                    continue
                a = r.start or 0
                b = n if r.stop is None else r.stop
                stp = r.step or 1
                off += a * s
                dims.append([s * stp, (b - a + stp - 1) // stp])
            else:
                dims.append([s, n])
        merged = []
        for d in dims:
            if merged and merged[-1][0] == d[0] * d[1]:
                merged[-1] = [d[0], merged[-1][1] * d[1]]
            else:
                merged.append(d)
        if not merged:
            merged = [[1, 1]]
        return self.ap(merged, off, p0, p1 - p0)


class Arena:
    def __init__(self, t32, t16, width32):
        self.t32, self.t16, self.W = t32, t16, width32
        self.top = 0
        self.marks = []
        self.guard = None

    def alloc(self, shape, dt=F32):
        n = int(np.prod(shape))
        if dt == F32:
            b = Buf(self.t32, self.W, self.top, shape)
            self.top += n
        else:
            b = Buf(self.t16, 2 * self.W, 2 * self.top, shape)
            self.top += (n + 1) // 2
        assert self.top <= self.W, ("arena overflow", self.top, self.W)
        assert self.guard is None or self.top <= self.guard, ("arena guard hit", self.top, self.guard)
        return b

    def mark(self):
        self.marks.append(self.top)

    def release(self):
        self.top = self.marks.pop()


def build(cfg):
    nc = bass.Bass("TRN2", target_bir_lowering=False)
    P = Prog(nc)
    depth = cfg.get("depth", DEPTH)
    mixers = cfg.get("mixers", True)

    def din(name, shape):
        return nc.dram_tensor(name, list(shape), F32, kind="ExternalInput").ap()

    x_d = din("x", [2048, D])
    cond_d = din("cond", [128, 8])
    ada_w = din("ada_w", [DEPTH, D, 6 * D])
    ada_b = din("ada_b", [DEPTH, 6 * D])
    ln_g = din("ln_g", [DEPTH, 2, D])
    ln_b = din("ln_b", [DEPTH, 2, D])
    wg_d = din("ffn_w_gate", [DEPTH, D, DFF])
    wu_d = din("ffn_w_up", [DEPTH, D, DFF])
    wd_d = din("ffn_w_down", [DEPTH, DFF, D])
    ident_d = din("ident", [128, 128])
    selT_d = din("selT", [128, 128])
    maskG_d = din("maskG", [128, 8])
    pool_w = din("pool_w", [2, 4, 256, 256])
    pool_scale = din("pool_scale", [2, D])
    pmat_d = din("pmat", [12, 128, 128])
    icnt_d = din("icnt", [128, 16, 4])
    a_re_d = din("ssm_a_re", [2, 2, 64, 64])
    a_im_d = din("ssm_a_im", [2, 2, 64, 64])
    ls_d = din("ssm_log_step", [2, 2, 64])
    b_re_d = din("ssm_b_re", [2, 2, 64, 64, 16])
    b_im_d = din("ssm_b_im", [2, 2, 64, 64, 16])
    c_re_d = din("ssm_c_re", [2, 2, 64, 16, 64])
    c_im_d = din("ssm_c_im", [2, 2, 64, 16, 64])
    dsk_d = din("ssm_d", [2, D])
    gluw_d = din("ssm_glu_w", [2, D, 2 * D])
    glub_d = din("ssm_glu_b", [2, 2 * D])
    h0_d = din("h0", [2, 128, 2, 64])
    mrow_d = din("mrow", [128, 128])
    expo_d = din("expo", [128, 34])
    wmask_d = din("wmask", [128, 6, 256])
    y_d = nc.dram_tensor("y", [2048, D], F32, kind="ExternalOutput").ap()
    st_d = [nc.dram_tensor(nm, [8, 2, 2, 64, 64], F32, kind="ExternalOutput").ap() for nm in ("st_re", "st_im")]
    dbg_d = nc.dram_tensor("dbg", [128, 8192], F32, kind="ExternalOutput").ap() if cfg.get("dbg") else None
    dbg_pos = [0]
    stop_at = cfg.get("s5_stop", None)

    import contextlib
    with contextlib.ExitStack() as es:
        AW = 52900
        a32 = es.enter_context(nc.sbuf_tensor("arena", [128, AW], F32))
        a16 = a32.bitcast(BF16)
        ps32 = es.enter_context(nc.psum_tensor("psum", [128, 4096], F32))
        ps16 = ps32.bitcast(BF16)
        A = Arena(a32, a16, AW)

        def PS(bank, shape, dt=F32, off=0):
            if dt == F32:
                return Buf(ps32, 4096, bank * 512 + off, shape)
            return Buf(ps16, 8192, bank * 1024 + off, shape)

        x = A.alloc([16, D])
        ident = A.alloc([128])
        identb = A.alloc([128], BF16)
        ones = A.alloc([128])
        condc = A.alloc([8])
        scc = A.alloc([8])
        grow = A.alloc([2, D])
        mcol = A.alloc([4, 8])
        LN = {}
        mcolS = A.alloc([2, 64])
        small = A.alloc([64])

        x_R = x_d.rearrange("(k s) c -> k s c", s=16)
        y_R = y_d.rearrange("(k s) c -> k s c", s=16)

        P.dma("sp", lambda e: e.dma_start(out=x[:], in_=x_R), writes=["x"])
        P.dma("sp", lambda e: e.dma_start(out=ident[:], in_=ident_d), writes=["ident"])
        P.dma("sp", lambda e: e.dma_start(out=condc[:], in_=cond_d), writes=["condc"])
        P.op("pool", lambda e: e.memset(ones[:], 1.0), writes=["ones"])
        P.op("dve", lambda e: e.tensor_copy(out=identb[:], in_=ident[:]), reads=["ident"], writes=["identb"])
        P.op("act", lambda e: e.activation(out=scc[:], in_=condc[:], func=AF.Silu), reads=["condc"], writes=["scc"])

        selT = A.alloc([128])
        maskG = A.alloc([8])
        P.dma("sp", lambda e: e.dma_start(out=selT[:], in_=selT_d), writes=["selT"])
        P.dma("sp", lambda e: e.dma_start(out=maskG[:], in_=maskG_d), writes=["maskG"])
        onesb = A.alloc([128], BF16)
        P.op("dve", lambda e: e.tensor_copy(out=onesb[:], in_=ones[:]), reads=["ones"], writes=["onesb"])

        def ada_gen(l, pb0):
            wa = [A.alloc([8, 128], BF16) for _ in range(3)]
            ba = [A.alloc([128], BF16) for _ in range(3)]
            bc = [A.alloc([1]) for _ in range(3)]
            sccb = A.alloc([8, 2], BF16)
            screpb = A.alloc([8, 128], BF16)
            rhsS = [A.alloc([8]) for _ in range(2)]
            grow_n = A.alloc([2, D]); mcol_n = A.alloc([4, 8]); mcolS_n = A.alloc([2, 64])
            AD["bufs"] = (grow_n, mcol_n, mcolS_n)
            P.op("dve", lambda e: e.tensor_copy(out=screpb.ap([[128, 8], [1, 128]]), in_=scc.ap([[1, 8], [0, 128]])),
                 reads=["scc"], writes=["screpb"])
            P.op("dve", lambda e: e.tensor_copy(out=sccb.ap([[2, 8], [1, 2]]), in_=scc.ap([[1, 8], [0, 2]])),
                 reads=["scc"], writes=["sccb"])

            def load(b):
                i3 = b % 3
                v = b // 8
                src = ada_w[l, :, b * 128:(b + 1) * 128].rearrange("(t p) n -> p t n", p=128)
                P.dma("pool", lambda e: e.dma_start(out=wa[i3][:], in_=src), writes=["adaw%d" % i3])
                if v in (2, 5):
                    P.dma("pool", lambda e: e.dma_start(out=ba[i3][0:1, :], in_=ada_b[l:l + 1, b * 128:(b + 1) * 128]),
                          writes=["adab%d" % i3])
                else:
                    P.dma("sp", lambda e: e.dma_start(out=bc[i3][:], in_=ada_b[l, b * 128:(b + 1) * 128].rearrange("(p o) -> p o", o=1)),
                          writes=["adac%d" % i3])

            pending = []
            load(0)
            load(1)
            for b in range(48):
                v, ct = divmod(b, 8)
                i2, i3 = b % 2, b % 3
                if b + 2 < 48:
                    load(b + 2)
                w_ = wa[i3]
                wk = "adaw%d" % i3
                for fn in pending:
                    fn()
                pending = []
                if v in (2, 5):
                    gi = 0 if v == 2 else 1
                    pa = PS(pb0 + i2, [128])
                    pka = "adapa%d" % i2
                    for kt in range(8):
                        P.op("pe", lambda e, kt=kt, w_=w_, pa=pa: e.matmul(pa[:], screpb[:, kt, :], w_[:, kt, :],
                                                                           start=(kt == 0), stop=False),
                             reads=["screpb", wk], writes=[pka], signal=(kt == 0))
                    P.op("pe", lambda e, pa=pa, i3=i3: e.matmul(pa[:], onesb[0:1, :], ba[i3][0:1, :], start=False, stop=True),
                         reads=["onesb", "adab%d" % i3, "screpb", wk], writes=[pka])
                    P.op("act", lambda e, pa=pa, gi=gi, ct=ct: e.activation(
                        out=grow_n[:, gi, ct * 128:(ct + 1) * 128], in_=pa[:], func=AF.Identity),
                         reads=[pka], writes=["grow_n"])
                else:
                    vi = {0: 0, 1: 1, 3: 2, 4: 3}[v]
                    pc = PS(pb0 + i2, [2], off=128)
                    pkc = "adapc%d" % i2
                    for kt in range(8):
                        P.op("pe", lambda e, kt=kt, w_=w_, pc=pc: e.matmul(pc[:], w_[:, kt, :], sccb[:, kt, :],
                                                                           start=(kt == 0), stop=(kt == 7)),
                             reads=["sccb", wk], writes=[pkc], signal=(kt in (0, 7)))
                    addc = 1.0 if vi in (1, 3) else 0.0
                    P.op("dve", lambda e, pc=pc, vi=vi, ct=ct, addc=addc, i3=i3: e.scalar_tensor_tensor(
                        out=mcol_n[:, vi, ct:ct + 1], in0=pc[:, 0:1], scalar=addc, in1=bc[i3][:, 0:1], op0=ALU.add, op1=ALU.add),
                         reads=[pkc, "adac%d" % i3], writes=["mcol_n"])
                    if v in (0, 1):
                        rs = rhsS[ct % 2]
                        rk = "adars%d" % (ct % 2)
                        P.op("dve", lambda e, rs=rs, vi=vi, ct=ct: e.tensor_scalar(
                            out=rs[:], in0=maskG[:], scalar1=mcol_n[:, vi, ct:ct + 1], scalar2=None, op0=ALU.mult),
                             reads=["mcol_n", "maskG"], writes=[rk])

                        def later(rs=rs, rk=rk, v=v, ct=ct, i2=i2):
                            p3 = PS(pb0 + i2, [8], off=256)
                            pk3 = "adaps%d" % i2
                            P.op("pe", lambda e: e.matmul(p3[:], selT[:], rs[:], start=True, stop=True),
                                 reads=[rk, "selT"], writes=[pk3])
                            P.op("dve", lambda e: e.tensor_copy(out=mcolS_n[:, v, ct * 8:(ct + 1) * 8], in_=p3[:]),
                                 reads=[pk3], writes=["mcolS_n"])
                        pending.append(later)
                yield
            for fn in pending:
                fn()

        AD = {}

        def ada_commit():
            grow_n, mcol_n, mcolS_n = AD["bufs"]
            P.op("act", lambda e: e.activation(out=grow[:], in_=grow_n[:], func=AF.Identity),
                 reads=["grow_n"], writes=["grow0", "grow1"])
            P.op("dve", lambda e: e.tensor_copy(out=mcol[:], in_=mcol_n[:]), reads=["mcol_n"], writes=["mcol"])
            P.op("dve", lambda e: e.tensor_copy(out=mcolS[:], in_=mcolS_n[:]), reads=["mcolS_n"], writes=["mcolS"])

        ADA_HI = 4800

        def ada_mod(l):
            saved = A.top
            A.top = A.W - ADA_HI
            assert A.top >= saved
            A.mark()
            for _ in ada_gen(l, 6):
                pass
            ada_commit()
            A.release()
            A.top = saved

        def load_ln(l, which):
            LN["row"] = A.alloc([2, D])
            lnrow = LN["row"]
            for i, src in enumerate((ln_g, ln_b)):
                P.dma("sp", lambda e, i=i, src=src: e.dma_start(
                    out=lnrow[:, i, :], in_=src[l, which:which + 1, :].partition_broadcast(128)), writes=["lnrow%d" % i])

        lnst = A.alloc([16, 2, 6])
        lnmv = A.alloc([16, 2])
        lnr = A.alloc([3, 16])

        def layer_norm_batch(s0, n, pool_assist=False):
            lnrow = LN["row"]
            xk = ["x%d" % s for s in range(s0, s0 + n)]
            for s in range(s0, s0 + n):
                for h in range(2):
                    P.op("dve", lambda e, s=s, h=h: e.bn_stats(out=lnst[:, s, h, :], in_=x[:, s, h * 512:(h + 1) * 512]),
                         reads=["x%d" % s], writes=["lnst%d_%d" % (s, h)])
                P.op("dve", lambda e, s=s: e.bn_aggr(out=lnmv[:, s, :], in_=lnst[:, s, :, :]),
                     reads=["lnst%d_0" % s, "lnst%d_1" % s], writes=["lnmv"])
            var = lnmv.ap([[2, n]], 2 * s0 + 1)
            mean = lnmv.ap([[2, n]], 2 * s0)
            P.op("act", lambda e: e.activation(out=lnr[:, 0, s0:s0 + n], in_=var, func=AF.Sqrt, bias=epsc[:, 0:1], scale=1.0),
                 reads=["lnmv", "epsc"], writes=["lnr0"])
            P.op("dve", lambda e: e.reciprocal(out=lnr[:, 1, s0:s0 + n], in_=lnr[:, 0, s0:s0 + n]), reads=["lnr0"], writes=["lnr1"])
            P.op("dve", lambda e: e.scalar_tensor_tensor(out=lnr[:, 2, s0:s0 + n], in0=mean, scalar=-1.0, in1=lnr[:, 1, s0:s0 + n],
                                                          op0=ALU.mult, op1=ALU.mult), reads=["lnmv", "lnr1"], writes=["lnr2"])
            for s in range(s0, s0 + n):
                P.op("act", lambda e, s=s: e.activation(out=x[:, s, :], in_=x[:, s, :], func=AF.Identity,
                                                        scale=lnr[:, 1, s:s + 1], bias=lnr[:, 2, s:s + 1]),
                     reads=["x%d" % s, "lnr1", "lnr2"], writes=["x%d" % s])
            if pool_assist:
                nd = (2 * n + 2) // 3
                parts = (("dve", s0, nd), ("pool", s0 + nd, n - nd))
            else:
                h2 = n // 2
                parts = (("dve", s0, h2), ("dve", s0 + h2, n - h2))
            for (eng, a, m) in parts:
                ks = ["x%d" % s for s in range(a, a + m)]
                P.op(eng, lambda e, a=a, m=m: e.tensor_tensor(out=x[:, a:a + m, :], in0=x[:, a:a + m, :],
                                                              in1=lnrow.ap([[0, m], [1, D]], 0), op=ALU.mult),
                     reads=ks + ["lnrow0"], writes=ks)
                P.op(eng, lambda e, a=a, m=m: e.tensor_tensor(out=x[:, a:a + m, :], in0=x[:, a:a + m, :],
                                                              in1=lnrow.ap([[0, m], [1, D]], D), op=ALU.add),
                     reads=ks + ["lnrow1"], writes=ks + ["x"])

        epsc = A.alloc([1])
        P.op("pool", lambda e: e.memset(epsc[:], EPS), writes=["epsc"])

        def ffn(l, nxt=None):
            A.mark()
            agen = ada_gen(nxt, 0) if nxt is not None else None

            def ada_step(n=1):
                if agen is None:
                    return
                for _ in range(n):
                    try:
                        next(agen)
                    except StopIteration:
                        return
            aT = A.alloc([8, 1024], BF16)
            hid = A.alloc([NFT, 1024], BF16)
            wgb = [A.alloc([8, 128], BF16) for _ in range(2)]
            wub = [A.alloc([8, 128], BF16) for _ in range(2)]
            wdb = [A.alloc([NFT, 256], BF16) for _ in range(2)]
            sg = [A.alloc([512]) for _ in range(2)]
            load_ln(l, 1)
            nwd = 0
            for half in range(2):
                for ct in range(8):
                    for sq in range(2):
                        pb = 2 + (ct * 2 + sq) % 2
                        pt = PS(pb, [4, 128])
                        for q in range(4):
                            s = half * 8 + sq * 4 + q
                            P.op("pe", lambda e, pt=pt, q=q, s=s, ct=ct: e.transpose(
                                pt[:, q, :], x[:, s, ct * 128:(ct + 1) * 128], ident[:]),
                                 reads=["x%d" % s, "ident"] if q < 3 else ["x%d" % (s - i) for i in range(4)] + ["ident"],
                                 writes=["ps%d" % pb] if q in (0, 3) else [], signal=(q in (0, 3)))
                        P.op("act", lambda e, pt=pt, ct=ct, sq=sq: e.activation(
                            out=aT[:, ct, sq * 512:(sq + 1) * 512], in_=pt[:], func=AF.Identity,
                            scale=mcol[:, 3, ct:ct + 1], bias=mcol[:, 2, ct:ct + 1]),
                             reads=["ps%d" % pb, "mcol"], writes=["aT%d" % sq])
                for ft in range(NFT):
                    ada_step(1)
                    wb_i = ft % 2
                    for (dst, srcw, nm) in ((wgb[wb_i], wg_d, "wg"), (wub[wb_i], wu_d, "wu")):
                        src = srcw[l, :, ft * 128:(ft + 1) * 128].rearrange("(t p) n -> p t n", p=128)
                        P.dma("pool", lambda e, dst=dst, src=src: e.dma_start(out=dst[:], in_=src),
                              writes=["%s%d" % (nm, wb_i)])
                    for blk in range(2):
                        pg, pu = PS(2 + blk * 2, [512]), PS(3 + blk * 2, [512])
                        kg, ku = "ps%d" % (2 + blk * 2), "ps%d" % (3 + blk * 2)
                        for (pp, kk_, wbuf, nm) in ((pg, kg, wgb[wb_i], "wg"), (pu, ku, wub[wb_i], "wu")):
                            for ct in range(8):
                                P.op("pe", lambda e, pp=pp, wbuf=wbuf, ct=ct, blk=blk: e.matmul(
                                    pp[:], wbuf[:, ct, :], aT[:, ct, blk * 512:(blk + 1) * 512],
                                    start=(ct == 0), stop=(ct == 7)),
                                     reads=["%s%d" % (nm, wb_i), "aT%d" % blk], writes=[kk_],
                                     signal=(ct in (0, 7)))
                        sgt = sg[blk]
                        P.op("act", lambda e, sgt=sgt, pg=pg: e.activation(out=sgt[:], in_=pg[:], func=AF.Silu),
                             reads=[kg], writes=["sg%d" % blk])
                        P.op("dve", lambda e, sgt=sgt, pu=pu, ft=ft, blk=blk: e.tensor_tensor(
                            out=hid[:, ft, blk * 512:(blk + 1) * 512], in0=pu[:], in1=sgt[:], op=ALU.mult),
                             reads=[ku, "sg%d" % blk], writes=["hid"])
                for cq in range(4):
                    ada_step(1)
                    wb_i = nwd % 2
                    nwd += 1
                    wdt = wdb[wb_i]
                    src = wd_d[l, :, cq * 256:(cq + 1) * 256].rearrange("(t p) n -> p t n", p=128)
                    P.dma("pool", lambda e, wdt=wdt, src=src: e.dma_start(out=wdt[:], in_=src), writes=["wd%d" % wb_i])
                    for s8 in range(8):
                        s = half * 8 + s8
                        pb = 6 + (s8 % 2)
                        po = PS(pb, [256])
                        for ft in range(NFT):
                            P.op("pe", lambda e, po=po, ft=ft, s8=s8, wdt=wdt: e.matmul(
                                po[:], hid[:, ft, s8 * 128:(s8 + 1) * 128], wdt[:, ft, :],
                                start=(ft == 0), stop=(ft == NFT - 1)),
                                 reads=["hid", "wd%d" % wb_i], writes=["ps%d" % pb], signal=(ft in (0, NFT - 1)))
                        tq = sg[s8 % 2]
                        P.op("dve", lambda e, tq=tq, po=po, cq=cq: e.tensor_tensor(
                            out=tq[:, 0:256], in0=po[:], in1=grow[:, 1, cq * 256:(cq + 1) * 256], op=ALU.mult),
                             reads=["ps%d" % pb, "grow1"], writes=["sg%d" % (s8 % 2)])
                        P.op("dve", lambda e, tq=tq, s=s, cq=cq: e.scalar_tensor_tensor(
                            out=x[:, s, cq * 256:(cq + 1) * 256], in0=x[:, s, cq * 256:(cq + 1) * 256], scalar=ALPHA,
                            in1=tq[:, 0:256], op0=ALU.mult, op1=ALU.add),
                             reads=["sg%d" % (s8 % 2), "x%d" % s], writes=["x%d" % s])
                if half == 1 and agen is not None:
                    ada_step(100)
                    ada_commit()
                layer_norm_batch(half * 8, 8)
            A.release()
            P.barrier()

        def mixer_none(l):
            A.mark()
            load_ln(l, 0)
            for s in range(16):
                P.op("dve", lambda e, s=s: e.tensor_scalar(out=x[:, s, :], in0=x[:, s, :], scalar1=ALPHA, scalar2=None,
                                                           op0=ALU.mult), reads=["x%d" % s], writes=["x%d" % s])
            layer_norm_batch(0, 16)
            A.release()
            P.barrier()

        def mixer_pool(l):
            j = l // 2
            A.mark()
            hT = [A.alloc([8, 128], BF16) for _ in range(2)]
            zb = A.alloc([16, D], BF16)
            pw = A.alloc([4, 2, 256], BF16)
            pm = A.alloc([12, 128], BF16)
            ic = A.alloc([16, 4])
            psrow = A.alloc([D])
            tq = [A.alloc([256]) for _ in range(2)]
            load_ln(l, 0)
            P.dma("pool", lambda e: e.dma_start(out=pw[:], in_=pool_w[j].rearrange("g (t p) n -> p g t n", p=128)),
                  writes=["pw"])
            P.dma("pool", lambda e: e.dma_start(out=pm[:], in_=pmat_d.rearrange("a k m -> k a m")), writes=["pm"])
            P.dma("sp", lambda e: e.dma_start(out=ic[:], in_=icnt_d), writes=["ic"])
            P.dma("sp", lambda e: e.dma_start(out=psrow[:], in_=pool_scale[j:j + 1, :].partition_broadcast(128)),
                  writes=["psrow"])
            P.op("dve", lambda e: e.tensor_tensor(out=psrow[:], in0=psrow[:], in1=grow[:, 0, :], op=ALU.mult),
                 reads=["psrow", "grow0"], writes=["psrow"])
            def pool_tr(s):
                ht = hT[s % 2]
                hk = "hT%d" % (s % 2)
                for ct in range(8):
                    pb = ct % 2
                    pt = PS(pb, [128])
                    P.op("pe", lambda e, pt=pt, s=s, ct=ct: e.transpose(pt[:], x[:, s, ct * 128:(ct + 1) * 128], ident[:]),
                         reads=["x%d" % s, "ident"], writes=["ps%d" % pb])
                    P.op("act", lambda e, pt=pt, ht=ht, ct=ct: e.activation(
                        out=ht[:, ct, :], in_=pt[:], func=AF.Identity, scale=mcol[:, 1, ct:ct + 1],
                        bias=mcol[:, 0, ct:ct + 1]), reads=["ps%d" % pb, "mcol"], writes=[hk])

            def pool_zm(s):
                ht = hT[s % 2]
                hk = "hT%d" % (s % 2)
                for g in range(4):
                    pb = 2 + g % 2
                    pz = PS(pb, [256])
                    for kt in range(2):
                        P.op("pe", lambda e, pz=pz, ht=ht, g=g, kt=kt: e.matmul(
                            pz[:], ht[:, 2 * g + kt, :], pw[:, g, kt, :], start=(kt == 0), stop=(kt == 1)),
                             reads=[hk, "pw"], writes=["ps%d" % pb])
                    P.op("act", lambda e, pz=pz, s=s, g=g: e.activation(out=zb[:, s, g * 256:(g + 1) * 256], in_=pz[:],
                                                                     func=AF.Identity), reads=["ps%d" % pb], writes=["zb"])

            pool_tr(0)
            for s in range(16):
                if s + 1 < 16:
                    pool_tr(s + 1)
                pool_zm(s)
            n = 0
            for sp_ in range(16):
                for g, k in enumerate(WINDOWS):
                    pb = 4 + n % 2
                    tqq = tq[n % 2]
                    tqk = "tq%d" % (n % 2)
                    n += 1
                    pp = PS(pb, [256])
                    terms = []
                    for dlt in range(-(k // 2), k - k // 2):
                        sidx = sp_ + dlt
                        a = 0
                        if sidx < 0:
                            sidx, a = sidx + 16, 2
                        elif sidx >= 16:
                            sidx, a = sidx - 16, 1
                        terms.append((sidx, a))
                    for i, (sidx, a) in enumerate(terms):
                        P.op("pe", lambda e, pp=pp, g=g, a=a, sidx=sidx, i=i, nt=len(terms): e.matmul(
                            pp[:], pm[:, g * 3 + a, :], zb[:, sidx, g * 256:(g + 1) * 256], start=(i == 0), stop=(i == nt - 1)),
                             reads=["zb", "pm"], writes=["ps%d" % pb], signal=(i in (0, len(terms) - 1)))
                    P.op("dve", lambda e, pp=pp, tqq=tqq, sp_=sp_, g=g: e.scalar_tensor_tensor(
                        out=tqq[:], in0=pp[:], scalar=ic[:, sp_, g:g + 1], in1=zb[:, sp_, g * 256:(g + 1) * 256],
                        op0=ALU.mult, op1=ALU.subtract), reads=["ps%d" % pb, "ic", "zb"], writes=[tqk])
                    P.op("dve", lambda e, tqq=tqq, g=g: e.tensor_tensor(out=tqq[:], in0=tqq[:], in1=psrow[:, g * 256:(g + 1) * 256],
                                                                    op=ALU.mult), reads=[tqk, "psrow"], writes=[tqk])
                    P.op("dve", lambda e, tqq=tqq, sp_=sp_, g=g: e.scalar_tensor_tensor(
                        out=x[:, sp_, g * 256:(g + 1) * 256], in0=x[:, sp_, g * 256:(g + 1) * 256], scalar=ALPHA, in1=tqq[:],
                        op0=ALU.mult, op1=ALU.add), reads=[tqk, "x%d" % sp_], writes=["x%d" % sp_])
            layer_norm_batch(0, 16, pool_assist=True)
            A.release()
            P.barrier()

        I32 = mybir.dt.int32
        a32i = a32.bitcast(I32)

        def V(eng, method, reads, writes, **kw):
            return P.op(eng, lambda e: getattr(e, method)(**kw), reads=reads, writes=writes)

        def dump(name, ap_, n, reads):
            if dbg_d is None:
                return
            c0 = dbg_pos[0]
            dbg_pos[0] += n
            print("DBG", name, c0, n)
            P.dma("sp", lambda e: e.dma_start(out=dbg_d[:, c0:c0 + n], in_=ap_), reads=reads, is_output=True)

        def stop(k):
            if stop_at == k:
                raise StopBuild()

        def sincos(ang, n, cosb, sinb, key):
            r = ang; m = A.alloc([n]); rf = m
            ri = Buf(a32i, AW, A.top, [n]); A.top += n
            k = key
            V("dve", "tensor_scalar", [k + "ang"], [k + "r"], out=r[:], in0=ang[:], scalar1=1.0 / (2 * np.pi), scalar2=None, op0=ALU.mult)
            V("dve", "tensor_copy", [k + "r"], [k + "ri"], out=ri[:], in_=r[:])
            V("dve", "tensor_copy", [k + "ri"], [k + "m"], out=rf[:], in_=ri[:])
            V("dve", "tensor_tensor", [k + "r", k + "m"], [k + "r"], out=r[:], in0=r[:], in1=rf[:], op=ALU.subtract)
            for (thr, op, sign) in ((0.5, ALU.is_gt, ALU.subtract), (-0.5, ALU.is_lt, ALU.add)):
                V("dve", "tensor_scalar", [k + "r"], [k + "m"], out=m[:], in0=r[:], scalar1=thr, scalar2=None, op0=op)
                V("dve", "tensor_tensor", [k + "r", k + "m"], [k + "r"], out=r[:], in0=r[:], in1=m[:], op=sign)
            V("act", "activation", [k + "r"], [k + "sin"], out=sinb[:], in_=r[:], func=AF.Sin, scale=6.283185)
            V("dve", "tensor_scalar", [k + "r"], [k + "r"], out=r[:], in0=r[:], scalar1=0.25, scalar2=None, op0=ALU.add)
            V("dve", "tensor_scalar", [k + "r"], [k + "m"], out=m[:], in0=r[:], scalar1=0.5, scalar2=None, op0=ALU.is_gt)
            V("dve", "tensor_tensor", [k + "r", k + "m"], [k + "r"], out=r[:], in0=r[:], in1=m[:], op=ALU.subtract)
            V("act", "activation", [k + "r"], [k + "cos"], out=cosb[:], in_=r[:], func=AF.Sin, scale=6.283185)

        def mixer_s5(l):
            jj = l // 2
            if l == 0:
                A.guard = A.W - ADA_HI
            GQ = 4
            A.mark()
            th = A.alloc([64]); adt = A.alloc([64]); are = A.alloc([64]); aim = A.alloc([64]); dtb = A.alloc([64])
            cre = A.alloc([64]); cim = A.alloc([64])
            expo = A.alloc([34])
            TRE = A.alloc([64 * 34]); TIM = A.alloc([64 * 34])
            BB = [A.alloc([64 * 16]) for _ in range(2)]
            CT = [A.alloc([64 * 16]) for _ in range(2)]
            Dcol = A.alloc([64])
            wmask = A.alloc([6, 256])
            mrow = A.alloc([128])
            Sfin = A.alloc([8 * 2 * 64])
            nat = A.alloc([3, 128])
            for i_, (dst, src, nm) in enumerate(((are, a_re_d, "are"), (aim, a_im_d, "aim"))):
                P.dma("sp", lambda e, src=src, i_=i_: e.dma_start(
                    out=nat.ap([[64, 2], [1, 64]], i_ * 128, 0, 64), in_=src[jj].rearrange("d g n -> g d n")), writes=["nat%d" % i_])
                pt = PS(i_, [128])
                V("pe", "transpose", ["nat%d" % i_, "ident"], ["ps%d" % i_], out=pt[:, 0:64], in_=nat.ap([[1, 128]], i_ * 128, 0, 64),
                  identity=ident[0:64, 0:64])
                V("act", "activation", ["ps%d" % i_], [nm], out=dst[:], in_=pt[:, 0:64], func=AF.Identity)
            for d in range(2):
                P.dma("sp", lambda e, d=d: e.dma_start(out=dtb[64 * d:64 * d + 64, :],
                                                       in_=ls_d[jj, d:d + 1, :].partition_broadcast(64)), writes=["dtb"])
            P.dma("sp", lambda e: e.dma_start(out=expo[:], in_=expo_d), writes=["expo"])
            P.dma("sp", lambda e: e.dma_start(out=wmask[:], in_=wmask_d), writes=["wmask"])
            P.dma("sp", lambda e: e.dma_start(out=mrow[:], in_=mrow_d), writes=["mrow"])
            P.dma("sp", lambda e: e.dma_start(out=nat.ap([[1, 16]], 256, 0, 64), in_=dsk_d[jj].rearrange("(g c) -> g c", c=16)),
                  writes=["nat2"])
            V("dve", "tensor_copy", ["nat2", "nat0"], ["nat0"], out=nat.ap([[16, 8], [1, 16]], 0, 0, 64),
              in_=nat.ap([[0, 8], [1, 16]], 256, 0, 64))
            ptd = PS(2, [128])
            V("pe", "transpose", ["nat0", "ident"], ["ps2"], out=ptd[:, 0:64], in_=nat.ap([[1, 128]], 0, 0, 64), identity=ident[0:64, 0:64])
            V("act", "activation", ["ps2"], ["Dcol"], out=Dcol[:], in_=ptd[:, 0:64], func=AF.Identity)
            A.mark()
            Braw = [A.alloc([64 * 16]) for _ in range(2)]
            CN = [A.alloc([8, 128]) for _ in range(2)]
            for ri, (bd, cd) in enumerate(((b_re_d, c_re_d), (b_im_d, c_im_d))):
                for d in range(2):
                    P.dma("act", lambda e, ri=ri, d=d, bd=bd: e.dma_start(
                        out=Braw[ri].ap([[16, 64], [1, 16]], 0, 64 * d, 64), in_=bd[jj, d].rearrange("g n c -> n g c")),
                          writes=["Braw%d" % ri])
                    P.dma("sp", lambda e, ri=ri, d=d, cd=cd: e.dma_start(
                        out=CN[ri].ap([[128, 8], [1, 64]], 64 * d), in_=cd[jj, d].rearrange("(gt g8) o n -> (g8 o) gt n", g8=8)),
                          writes=["CN%d" % ri])
            V("act", "activation", ["dtb"], ["dtb"], out=dtb[:], in_=dtb[:], func=AF.Exp)
            V("dve", "tensor_tensor", ["are", "dtb"], ["adt"], out=adt[:], in0=are[:], in1=dtb[:], op=ALU.mult)
            V("dve", "tensor_tensor", ["aim", "dtb"], ["th"], out=th[:], in0=aim[:], in1=dtb[:], op=ALU.mult)
            NT = 64 * 34
            ANG = A.alloc([NT]); MAG = A.alloc([NT]); COS = TRE; SIN = TIM
            full = [[34, 64], [1, 34]]
            V("dve", "tensor_tensor", ["th", "expo"], ["s5ang"], out=ANG.ap(full), in0=th.ap([[1, 64], [0, 34]]),
              in1=expo.ap([[0, 64], [1, 34]]), op=ALU.mult)
            V("dve", "tensor_tensor", ["adt", "expo"], ["MAG"], out=MAG.ap(full), in0=adt.ap([[1, 64], [0, 34]]),
              in1=expo.ap([[0, 64], [1, 34]]), op=ALU.mult)
            V("act", "activation", ["MAG"], ["MAG"], out=MAG[:], in_=MAG[:], func=AF.Exp)
            sincos(ANG, NT, COS, SIN, "s5")
            V("dve", "tensor_tensor", ["MAG", "s5cos"], ["TRE", "s5cos"], out=TRE[:], in0=MAG[:], in1=COS[:], op=ALU.mult)
            V("dve", "tensor_tensor", ["MAG", "s5sin"], ["TIM", "s5sin"], out=TIM[:], in0=MAG[:], in1=SIN[:], op=ALU.mult)
            stop(1)
            t1 = A.alloc([64]); t2 = A.alloc([64]); nr = A.alloc([64]); rden = A.alloc([64])
            lre, lim = TRE.ap([[34, 64]], 0), TIM.ap([[34, 64]], 0)
            V("dve", "tensor_scalar", ["TRE"], ["nr"], out=nr[:], in0=lre, scalar1=-1.0, scalar2=None, op0=ALU.add)
            V("dve", "tensor_tensor", ["are"], ["t1"], out=t1[:], in0=are[:], in1=are[:], op=ALU.mult)
            V("dve", "tensor_tensor", ["aim"], ["t2"], out=t2[:], in0=aim[:], in1=aim[:], op=ALU.mult)
            V("dve", "tensor_tensor", ["t1", "t2"], ["t1"], out=t1[:], in0=t1[:], in1=t2[:], op=ALU.add)
            V("dve", "reciprocal", ["t1"], ["rden"], out=rden[:], in_=t1[:])
            V("dve", "tensor_tensor", ["nr", "are"], ["t1"], out=t1[:], in0=nr[:], in1=are[:], op=ALU.mult)
            V("dve", "tensor_tensor", ["TIM", "aim"], ["t2"], out=t2[:], in0=lim, in1=aim[:], op=ALU.mult)
            V("dve", "tensor_tensor", ["t1", "t2"], ["t1"], out=t1[:], in0=t1[:], in1=t2[:], op=ALU.add)
            V("dve", "tensor_tensor", ["t1", "rden"], ["cre"], out=cre[:], in0=t1[:], in1=rden[:], op=ALU.mult)
            V("dve", "tensor_tensor", ["TIM", "are"], ["t1"], out=t1[:], in0=lim, in1=are[:], op=ALU.mult)
            V("dve", "tensor_tensor", ["nr", "aim"], ["t2"], out=t2[:], in0=nr[:], in1=aim[:], op=ALU.mult)
            V("dve", "tensor_tensor", ["t1", "t2"], ["t1"], out=t1[:], in0=t1[:], in1=t2[:], op=ALU.subtract)
            V("dve", "tensor_tensor", ["t1", "rden"], ["cim"], out=cim[:], in0=t1[:], in1=rden[:], op=ALU.mult)
            u1 = A.alloc([1024]); u2 = A.alloc([1024])
            f3 = [[16, 64], [1, 16]]
            cb = lambda c: c.ap([[1, 64], [0, 16]])
            for (o, (ca, xa), (cb_, xb), op) in ((0, (cre, 0), (cim, 1), ALU.subtract), (1, (cre, 1), (cim, 0), ALU.add)):
                V("dve", "tensor_tensor", ["cre", "cim", "Braw0", "Braw1"], ["u1"], out=u1.ap(f3), in0=cb(ca), in1=Braw[xa].ap(f3), op=ALU.mult)
                V("dve", "tensor_tensor", ["cre", "cim", "Braw0", "Braw1"], ["u2"], out=u2.ap(f3), in0=cb(cb_), in1=Braw[xb].ap(f3), op=ALU.mult)
                V("dve", "tensor_tensor", ["u1", "u2"], ["BB%d" % o], out=BB[o][:], in0=u1[:], in1=u2[:], op=op)
            stop(2)
            for ri in range(2):
                for gt in range(8):
                    pt = PS(gt % 2, [128])
                    pk = "ps%d" % (gt % 2)
                    V("pe", "transpose", ["CN%d" % ri, "ident"], [pk], out=pt[:], in_=CN[ri][:, gt, :], identity=ident[:])
                    V("act", "activation", [pk], ["CT%d" % ri], out=CT[ri][:, gt * 128:(gt + 1) * 128], in_=pt[:], func=AF.Identity,
                      scale=(-1.0 if ri == 1 else 1.0))
            A.release()
            P.barrier()
            A.guard = None
            yR = A.alloc([16, D], BF16)
            A.mark()
            GQ = 16
            SB = 1
            RQ = [A.alloc([GQ, 256], BF16) for _ in range(2)]
            Bbuf = A.alloc([128, 2, GQ], BF16)
            Pbuf = A.alloc([2, GQ, 128], BF16)
            G_, W2, W4 = GQ, 2 * GQ, 4 * GQ
            A.mark()
            LQ2 = [[A.alloc([SB, 256], BF16) for _ in range(2)] for _ in range(3)]
            g1 = A.alloc([SB * 256]); g2 = A.alloc([SB * 256])
            h1 = A.alloc([SB * 256]); h2 = A.alloc([SB * 256])
            XT = [A.alloc([2, 128], BF16) for _ in range(3)]
            Wg = [A.alloc([4, 256], BF16) for _ in range(2)]
            LTg = [A.alloc([4, 128], BF16) for _ in range(2)]
            xstg = A.alloc([2, 128])
            topA = A.top
            A.release()
            A.mark()
            UU = [A.alloc([8, W2]) for _ in range(3)]
            T2 = A.alloc([8, W2]); prod = A.alloc([8, W4]); C4 = A.alloc([W4]); C256 = A.alloc([W4])
            VV = [A.alloc([W2]) for _ in range(3)]
            Pin = A.alloc([8, W2]); prodb = A.alloc([W4]); tb = A.alloc([W2])
            Zc = [A.alloc([W2]) for _ in range(2)]; prodz = A.alloc([W4])
            ZC = A.alloc([16, W4])
            corr = [A.alloc([8, W2]) for _ in range(2)]; prodj = [A.alloc([8, W4]) for _ in range(2)]
            topS = A.top
            A.release()
            A.mark()
            ytl = [A.alloc([256]) for _ in range(2)]; yul = [A.alloc([256]) for _ in range(2)]; yvl = [A.alloc([256]) for _ in range(2)]
            topC = A.top
            A.release()
            A.top = max(topA, topS, topC)
            assert A.top <= A.W, ("arena overflow (s5 scratch)", A.top)
            o3 = [[256, SB], [16, 16], [1, 16]]

            def gen(eng, which, g0b, dst, t1_, t2_, kpre, doff=0, ksuf=""):
                col, X = (2, BB) if which == "L" else (18, CT)
                xk = ["BB0", "BB1"] if which == "L" else ["CT0", "CT1"]
                tap = lambda T: T.ap([[34, SB], [1, 16], [0, 16]], g0b * 34 + col)
                xap = lambda Xb: Xb.ap([[16, SB], [0, 16], [1, 16]], g0b * 16)
                k1, k2 = kpre + "1", kpre + "2"
                d0 = dst[0].ap([[1, SB * 256]], doff * 256)
                d1 = dst[1].ap([[1, SB * 256]], doff * 256)
                V(eng, "tensor_tensor", ["TRE"] + xk, [k1], out=t1_.ap(o3), in0=tap(TRE), in1=xap(X[0]), op=ALU.mult)
                V(eng, "tensor_tensor", ["TIM"] + xk, [k2], out=t2_.ap(o3), in0=tap(TIM), in1=xap(X[1]), op=ALU.mult)
                V(eng, "tensor_tensor", [k1, k2], [which + "0" + ksuf], out=d0, in0=t1_[:], in1=t2_[:],
                  op=(ALU.subtract if which == "L" else ALU.add))
                V(eng, "tensor_tensor", ["TRE"] + xk, [k1], out=t1_.ap(o3), in0=tap(TRE), in1=xap(X[1]), op=ALU.mult)
                V(eng, "tensor_tensor", ["TIM"] + xk, [k2], out=t2_.ap(o3), in0=tap(TIM), in1=xap(X[0]), op=ALU.mult)
                V(eng, "tensor_tensor", [k1, k2], [which + "1" + ksuf], out=d1, in0=t1_[:], in1=t2_[:],
                  op=(ALU.add if which == "L" else ALU.subtract))

            for Q in range(64 // GQ):
                g0 = Q * GQ
                stop(3)
                P.barrier()
                def stage0(gl):
                    g = g0 + gl
                    LQ = LQ2[gl % 3]
                    lsuf = "_%d" % (gl % 3)
                    gen("dve", "L", g, LQ, g1, g2, "g", ksuf=lsuf)
                    if False:
                        gen("dve", "R", g, RQ, g1, g2, "g", doff=gl, ksuf="_%d" % gl)
                    else:
                        gen("pool", "R", g, RQ, h1, h2, "h", doff=gl, ksuf="_%d" % gl)

                def stage1(gl):
                    g = g0 + gl
                    LQ = LQ2[gl % 3]
                    lsuf = "_%d" % (gl % 3)
                    lt = LTg[gl % 2]; ltk = "LT%d" % (gl % 2)
                    xt = XT[gl % 3]; xtk = "XT%d" % (gl % 3)
                    p2 = PS(2, [4, 128], BF16)
                    for ri in range(2):
                        for jt in range(2):
                            V("pe", "transpose", ["L%d" % ri + lsuf, "identb"], ["ps2"], out=p2[:, ri * 2 + jt, :],
                              in_=LQ[ri][:, 0, jt * 128:(jt + 1) * 128], identity=identb[:])
                    V("act", "activation", ["ps2"], [ltk], out=lt[:], in_=p2[:], func=AF.Identity)
                    p3 = PS(3, [2, 128])
                    for jt in range(2):
                        V("pool", "tensor_copy", ["x"] + ["x%d" % (8 * jt + i) for i in range(8)], ["xstg%d" % jt],
                          out=xstg.ap([[16, 8], [1, 16]], jt * 128), in_=x.ap([[1024, 8], [1, 16]], 8 * jt * 1024 + g * 16))
                        V("pe", "transpose", ["xstg%d" % jt, "ident"], ["ps3"], out=p3[:, jt, :], in_=xstg[:, jt, :], identity=ident[:])
                    V("act", "activation", ["ps3", "mcolS"], [xtk], out=xt[:], in_=p3[:], func=AF.Identity,
                      scale=mcolS[:, 1, g:g + 1], bias=mcolS[:, 0, g:g + 1])

                def stage2(gl):
                    g = g0 + gl
                    LQ = LQ2[gl % 3]
                    lsuf = "_%d" % (gl % 3)
                    lt = LTg[gl % 2]; ltk = "LT%d" % (gl % 2)
                    xt = XT[gl % 3]; xtk = "XT%d" % (gl % 3)
                    p4 = PS(4, [2, 128])
                    for ri in range(2):
                        for jt in range(2):
                            P.op("pe", lambda e, p4=p4, ri=ri, jt=jt, lt=lt, xt=xt: e.matmul(
                                p4[:, ri, :], lt[:, ri * 2 + jt, :], xt[:, jt, :], start=(jt == 0), stop=(jt == 1)),
                                 reads=[ltk, xtk], writes=["ps4"])
                    V("act", "activation", ["ps4"], ["Bbuf"], out=Bbuf.ap([[GQ, 2], [2 * GQ, 128]], gl, 0, 64),
                      in_=p4.ap([[128, 2], [1, 128]], 0, 0, 64), func=AF.Identity)
                    V("act", "activation", ["ps4"], ["Bbuf"], out=Bbuf.ap([[GQ, 2], [-2 * GQ, 128]], 127 * 2 * GQ + gl, 64, 64),
                      in_=p4.ap([[128, 2], [1, 128]], 0, 64, 64), func=AF.Identity)
                    wg = Wg[gl % 2]; wk = "Wg%d" % (gl % 2)
                    for d in range(2):
                        pw_ = PS(d, [2, 256])
                        for jt in range(2):
                            for ri in range(2):
                                P.op("pe", lambda e, pw_=pw_, d=d, jt=jt, ri=ri, gl=gl, LQ=LQ: e.matmul(
                                    pw_[:, jt, :], LQ[ri].ap([[1, 128]], jt * 128, 64 * d, 64),
                                    RQ[ri].ap([[1, 256]], gl * 256, 64 * d, 64), start=(ri == 0), stop=(ri == 1)),
                                     reads=["L0" + lsuf, "L1" + lsuf, "R0_%d" % gl, "R1_%d" % gl], writes=["ps%d" % d])
                            V("dve", "tensor_tensor", ["ps%d" % d, "wmask"], [wk + "_%d" % (d * 2 + jt)], out=wg[:, d * 2 + jt, :],
                              in0=pw_[:, jt, :], in1=wmask[:, d * 2 + jt, :], op=ALU.mult)
                    for jt in range(2):
                        V("dve", "scalar_tensor_tensor", [wk + "_%d" % jt, "wmask", "Dcol"], [wk + "_%d" % jt], out=wg[:, jt, :],
                          in0=wmask[:, 4 + jt, :], scalar=Dcol[:, g:g + 1], in1=wg[:, jt, :], op0=ALU.mult, op1=ALU.add)

                def stage3(gl):
                    g = g0 + gl
                    xt = XT[gl % 3]; xtk = "XT%d" % (gl % 3)
                    wg = Wg[gl % 2]; wk = "Wg%d" % (gl % 2)
                    pb = 5 + gl % 2
                    py = PS(pb, [256])
                    n_mm = 0
                    for d in range(2):
                        for jt in range(2):
                            P.op("pe", lambda e, py=py, d=d, jt=jt, xt=xt, wg=wg, n_mm=n_mm: e.matmul(
                                py[:], xt[:, jt, :], wg[:, d * 2 + jt, :], start=(n_mm == 0), stop=(n_mm == 3)),
                                 reads=[xtk] + [wk + "_%d" % q for q in range(4)], writes=["ps%d" % pb])
                            n_mm += 1
                    V("act", "activation", ["ps%d" % pb], ["yR"], out=yR.ap([[1024, 16], [1, 16]], g * 16),
                      in_=py.ap([[16, 16], [1, 16]]), func=AF.Identity)

                for t in range(GQ + 3):
                    if 0 <= t - 2 < GQ:
                        stage2(t - 2)
                    if t < GQ:
                        stage0(t)
                    if 0 <= t - 1 < GQ:
                        stage1(t - 1)
                    if 0 <= t - 3 < GQ:
                        stage3(t - 3)
                stop(5)
                P.barrier()
                P.dma("sp", lambda e, Q=Q: e.dma_start(out=VV[0].ap([[GQ, 2], [1, GQ]]),
                                                       in_=h0_d[jj, :, :, Q * GQ:(Q + 1) * GQ]), writes=["VV0"])
                lam = lambda T: T.ap([[34, GQ]], g0 * 34 + 1)
                V("dve", "tensor_copy", ["TRE"], ["C4"], out=C4.ap([[G_, 2], [1, G_]]), in_=TRE.ap([[0, 2], [34, GQ]], g0 * 34 + 1))
                V("dve", "tensor_copy", ["TIM"], ["C4"], out=C4[:, 2 * G_:3 * G_], in_=lam(TIM))
                V("dve", "tensor_scalar", ["TIM"], ["C4"], out=C4[:, 3 * G_:4 * G_], in0=lam(TIM), scalar1=-1.0, scalar2=None, op0=ALU.mult)
                P.op("pool", lambda e: e.memset(ZC[:, 0, 0:W2], 1.0), writes=["ZC0"])
                P.op("pool", lambda e: e.memset(ZC[:, 0, W2:W4], 0.0), writes=["ZC0"])
                P.op("pool", lambda e: e.memset(Zc[0][:, 0:G_], 1.0), writes=["Zc0"])
                P.op("pool", lambda e: e.memset(Zc[0][:, G_:W2], 0.0), writes=["Zc0"])
                for j in range(1, 17):
                    zp, zn = Zc[(j - 1) % 2], Zc[j % 2]
                    kzp, kzn = "Zc%d" % ((j - 1) % 2), "Zc%d" % (j % 2)
                    V("pool", "tensor_tensor", [kzp, "C4"], ["prodz"], out=prodz.ap([[W2, 2], [1, W2]]),
                      in0=zp.ap([[0, 2], [1, W2]]), in1=C4.ap([[W2, 2], [1, W2]]), op=ALU.mult)
                    V("pool", "tensor_tensor", ["prodz"], [kzn], out=zn.ap([[G_, 2], [1, G_]]), in0=prodz.ap([[G_, 2], [1, G_]]),
                      in1=prodz.ap([[-G_, 2], [1, G_]], 3 * G_), op=ALU.add)
                    dst = C256 if j == 16 else None
                    dk = "C256" if j == 16 else "ZC%d" % j
                    oa = (lambda a, n: C256.ap([[1, n]], a)) if j == 16 else (lambda a, n, j=j: ZC.ap([[1, n]], j * W4 + a))
                    V("pool", "tensor_copy", [kzn], [dk], out=(C256.ap([[G_, 2], [1, G_]]) if j == 16 else ZC.ap([[G_, 2], [1, G_]], j * W4)),
                      in_=zn.ap([[0, 2], [1, G_]]))
                    V("pool", "tensor_copy", [kzn], [dk], out=oa(2 * G_, G_), in_=zn[:, G_:W2])
                    V("pool", "tensor_scalar", [kzn], [dk], out=oa(3 * G_, G_), in0=zn[:, G_:W2], scalar1=-1.0, scalar2=0.0,
                      op0=ALU.mult, op1=ALU.add)
                P.op("dve", lambda e: e.memset(UU[0][:], 0.0), writes=["UU0"])
                u3 = [[W2, 8], [G_, 2], [1, G_]]
                for j in range(16):
                    up, un = UU[j % 3], UU[(j + 1) % 3]
                    kp, kn = "UU%d" % (j % 3), "UU%d" % ((j + 1) % 3)
                    V("dve", "tensor_tensor", [kp, "Bbuf"], ["T2"], out=T2.ap([[W2, 8], [1, W2]]), in0=up.ap([[W2, 8], [1, W2]]),
                      in1=Bbuf.ap([[16 * W2, 8], [1, W2]], j * W2), op=ALU.add)
                    V("dve", "tensor_tensor", ["T2", "C4"], ["prod"], out=prod.ap([[W4, 8], [W2, 2], [1, W2]]),
                      in0=T2.ap([[W2, 8], [0, 2], [1, W2]]), in1=C4.ap([[0, 8], [W2, 2], [1, W2]]), op=ALU.mult)
                    V("dve", "tensor_tensor", ["prod"], [kn], out=un.ap(u3), in0=prod.ap([[W4, 8], [G_, 2], [1, G_]]),
                      in1=prod.ap([[W4, 8], [-G_, 2], [1, G_]], 3 * G_), op=ALU.add)
                    if j < 15:
                        V("act", "activation", [kn], ["Pbuf"], out=Pbuf.ap([[16, 8], [GQ * 128, 2], [128, GQ]], j + 1, 0, 64),
                          in_=un.ap(u3, 0, 0, 64), func=AF.Identity)
                        V("act", "activation", [kn], ["Pbuf"], out=Pbuf.ap([[-16, 8], [GQ * 128, 2], [128, GQ]], 127 - (j + 1), 64, 64),
                          in_=un.ap(u3, 0, 64, 64), func=AF.Identity)
                U15, kU15 = UU[16 % 3], "UU%d" % (16 % 3)
                for b in range(8):
                    vp, vn = VV[b % 3], VV[(b + 1) % 3]
                    kvp, kvn = "VV%d" % (b % 3), "VV%d" % ((b + 1) % 3)
                    V("dve", "tensor_scalar", [kvp, "mrow"], ["Pin"], out=Pin[:, b, :], in0=vp[:], scalar1=mrow[:, 16 * b:16 * b + 1],
                      scalar2=None, op0=ALU.mult)
                    V("dve", "tensor_tensor", ["Pin", "C256"], ["prodb"], out=prodb.ap([[W2, 2], [1, W2]]),
                      in0=Pin.ap([[0, 2], [1, W2]], b * W2), in1=C256.ap([[W2, 2], [1, W2]]), op=ALU.mult)
                    V("dve", "tensor_tensor", ["prodb"], ["tb"], out=tb.ap([[G_, 2], [1, G_]]), in0=prodb.ap([[G_, 2], [1, G_]]),
                      in1=prodb.ap([[-G_, 2], [1, G_]], 3 * G_), op=ALU.add)
                    V("dve", "tensor_tensor", ["tb", kU15], [kvn], out=vn[:], in0=tb[:], in1=U15[:, b, :], op=ALU.add)
                    for hh in range(2):
                        p0 = 64 * hh
                        bb = b if hh == 0 else 7 - b
                        V("act", "activation", [kvn], ["Sfin"], out=Sfin.ap([[64, 2], [1, GQ]], bb * 128 + g0, p0, 64),
                          in_=vn.ap([[GQ, 2], [1, GQ]], 0, p0, 64), func=AF.Identity)
                for j in range(16):
                    pj, cj = prodj[j % 2], corr[j % 2]
                    kpj, kcj = "prodj%d" % (j % 2), "corr%d" % (j % 2)
                    V("dve", "tensor_tensor", ["Pin", "ZC%d" % j], [kpj], out=pj.ap([[W4, 8], [W2, 2], [1, W2]]),
                      in0=Pin.ap([[W2, 8], [0, 2], [1, W2]]), in1=ZC.ap([[0, 8], [W2, 2], [1, W2]], j * W4), op=ALU.mult)
                    V("dve", "tensor_tensor", [kpj], [kcj], out=cj.ap(u3), in0=pj.ap([[W4, 8], [G_, 2], [1, G_]]),
                      in1=pj.ap([[W4, 8], [-G_, 2], [1, G_]], 3 * G_), op=ALU.add)
                    for hh in range(2):
                        p0 = 64 * hh
                        if hh == 0:
                            pap = Pbuf.ap([[16, 8], [GQ * 128, 2], [128, GQ]], j, 0, 64)
                        else:
                            pap = Pbuf.ap([[-16, 8], [GQ * 128, 2], [128, GQ]], 127 - j, 64, 64)
                        if j == 0:
                            V("act", "activation", [kcj], ["Pbuf"], out=pap, in_=cj.ap(u3, 0, p0, 64), func=AF.Identity)
                        else:
                            V("pool", "tensor_tensor", [kcj, "Pbuf"], ["Pbuf"], out=pap, in0=pap, in1=cj.ap(u3, 0, p0, 64), op=ALU.add)
                P.barrier()
                stop(6)
                for gl in range(GQ):
                    g = g0 + gl
                    pb = 5 + gl % 2
                    py = PS(pb, [256])
                    yt, yu, yv = ytl[gl % 2], yul[gl % 2], yvl[gl % 2]
                    ks = "_%d" % (gl % 2)
                    for ri in range(2):
                        P.op("pe", lambda e, py=py, ri=ri, gl=gl: e.matmul(
                            py[:], Pbuf[:, ri, gl, :], RQ[ri][:, gl, :], start=(ri == 0), stop=(ri == 1)),
                             reads=["Pbuf", "R0_%d" % gl, "R1_%d" % gl], writes=["ps%d" % pb])
                    V("dve", "tensor_tensor", ["ps%d" % pb, "yR"], ["yu" + ks], out=yu.ap([[16, 16], [1, 16]]),
                      in0=py.ap([[16, 16], [1, 16]]), in1=yR.ap([[1024, 16], [1, 16]], g * 16), op=ALU.add)
                    V("act", "activation", ["yu" + ks], ["yt" + ks], out=yt[:], in_=yu[:], func=AF.Square)
                    V("dve", "tensor_scalar", ["yt" + ks], ["yt" + ks], out=yt[:], in0=yt[:], scalar1=0.044715, scalar2=1.0, op0=ALU.mult, op1=ALU.add)
                    V("dve", "tensor_tensor", ["yt" + ks, "yu" + ks], ["yt" + ks], out=yt[:], in0=yt[:], in1=yu[:], op=ALU.mult)
                    V("act", "activation", ["yt" + ks], ["yv" + ks], out=yv[:], in_=yt[:], func=AF.Sigmoid, scale=1.5957691216057308)
                    V("dve", "tensor_tensor", ["yv" + ks, "yu" + ks], ["yR"], out=yR.ap([[1024, 16], [1, 16]], g * 16),
                      in0=yv.ap([[16, 16], [1, 16]]), in1=yu.ap([[16, 16], [1, 16]]), op=ALU.mult)
            A.release()
            P.barrier()
            stop(7)
            A.mark()
            SfT = A.alloc([16, 128])
            for bq in range(8):
                for ri in range(2):
                    pt = PS(ri, [128])
                    V("pe", "transpose", ["Sfin", "ident"], ["ps%d" % ri], out=pt[0:64, :],
                      in_=Sfin.ap([[1, 64]], bq * 128 + ri * 64), identity=ident[:])
                    V("act", "activation", ["ps%d" % ri], ["SfT"], out=SfT[0:64, bq * 2 + ri, :], in_=pt[0:64, :], func=AF.Identity)
            for ri in range(2):
                for d in range(2):
                    P.dma("sp", lambda e, ri=ri, d=d: e.dma_start(
                        out=st_d[ri][:, jj, d, :, :].rearrange("b g n -> g b n"),
                        in_=SfT.ap([[256, 8], [1, 64]], ri * 128 + d * 64, 0, 64)), reads=["SfT"], is_output=True)
            stop(8)
            gw = A.alloc([2, 8, 512], BF16)
            gb = A.alloc([2, 512], BF16)
            yT = [A.alloc([8, 128], BF16) for _ in range(2)]
            sgm = [A.alloc([512]) for _ in range(2)]
            load_ln(l, 0)
            for cbk in range(2):
                for vg in range(2):
                    c0 = vg * D + cbk * 512
                    P.dma("pool", lambda e, vg=vg, c0=c0: e.dma_start(
                        out=gw[:, vg, :, :], in_=gluw_d[jj, :, c0:c0 + 512].rearrange("(t p) n -> p t n", p=128)), writes=["gw"])
                    P.dma("pool", lambda e, vg=vg, c0=c0: e.dma_start(out=gb[0:1, vg, :], in_=glub_d[jj:jj + 1, c0:c0 + 512]),
                          writes=["gb"])
                def glu_tr(s):
                    yt_ = yT[s % 2]; ytk = "yT%d" % (s % 2)
                    ptb = PS(s % 2, [8, 128], BF16)
                    for ct in range(8):
                        P.op("pe", lambda e, ptb=ptb, ct=ct, s=s: e.transpose(ptb[:, ct, :], yR[:, s, ct * 128:(ct + 1) * 128], identb[:]),
                             reads=["yR", "identb"], writes=["ps%d" % (s % 2)], signal=(ct in (0, 7)))
                    V("act", "activation", ["ps%d" % (s % 2)], [ytk], out=yt_[:], in_=ptb[:], func=AF.Identity)

                def glu_mm(s, cbk=cbk):
                    yt_ = yT[s % 2]; ytk = "yT%d" % (s % 2)
                    pv, pg_ = PS(2 + 2 * (s % 2), [512]), PS(3 + 2 * (s % 2), [512])
                    kv, kg = "ps%d" % (2 + 2 * (s % 2)), "ps%d" % (3 + 2 * (s % 2))
                    for (pp, kq, vg) in ((pv, kv, 0), (pg_, kg, 1)):
                        for ct in range(8):
                            P.op("pe", lambda e, pp=pp, ct=ct, vg=vg, yt_=yt_: e.matmul(
                                pp[:], yt_[:, ct, :], gw[:, vg, ct, :], start=(ct == 0), stop=False),
                                 reads=[ytk, "gw"], writes=[kq], signal=(ct == 0))
                        P.op("pe", lambda e, pp=pp, vg=vg: e.matmul(pp[:], onesb[0:1, :], gb[0:1, vg, :], start=False, stop=True),
                             reads=[ytk, "gw", "gb", "onesb"], writes=[kq])
                    sg_ = sgm[s % 2]; sgk = "sgm%d" % (s % 2)
                    V("act", "activation", [kg], [sgk], out=sg_[:], in_=pg_[:], func=AF.Sigmoid)
                    V("dve", "tensor_tensor", [kv, sgk], [sgk], out=sg_[:], in0=pv[:], in1=sg_[:], op=ALU.mult)
                    V("dve", "tensor_tensor", [sgk, "grow0"], [sgk], out=sg_[:], in0=sg_[:], in1=grow[:, 0, cbk * 512:(cbk + 1) * 512], op=ALU.mult)
                    V("dve", "scalar_tensor_tensor", [sgk, "x%d" % s], ["x%d" % s], out=x[:, s, cbk * 512:(cbk + 1) * 512],
                      in0=x[:, s, cbk * 512:(cbk + 1) * 512], scalar=ALPHA, in1=sg_[:], op0=ALU.mult, op1=ALU.add)

                glu_tr(0)
                for s in range(16):
                    if s + 1 < 16:
                        glu_tr(s + 1)
                    glu_mm(s)
            layer_norm_batch(0, 16, pool_assist=True)
            A.release()
            A.release()
            P.barrier()

        ada_mod(0)
        for l in range(depth):
          try:
            mix = cfg.get("mix", "all")
            if l % 2 == 1 and mix in ("all", "pool"):
                mixer_pool(l)
            elif l % 2 == 0 and mix in ("all", "s5"):
                mixer_s5(l)
            else:
                mixer_none(l)
            ffn(l, l + 1 if l + 1 < depth else None)
          except StopBuild:
            P.barrier()
            break

        P.dma("sp", lambda e: e.dma_start(out=y_R, in_=x[:]), reads=["x"] + ["x%d" % s for s in range(16)],
              is_output=True)
        P.finish()
        P.replay()
    return nc


def _pool_consts(sample):
    pm = np.zeros((4, 3, 128, 128), np.float32)
    ic = np.zeros((128, 16, 4), np.float32)
    for g, k in enumerate(WINDOWS):
        for kq in range(128):
            if sample:
                row, cb = divmod(kq, 4)
                rows = range(max(0, row - k // 2), min(32, row - k // 2 + k))
                for a, off in ((0, 0), (1, 1), (2, -1)):
                    cbm = cb + off
                    if 0 <= cbm < 4:
                        for r in rows:
                            pm[g, a, r * 4 + cbm, kq] = 1.0
                for sq in range(16):
                    col = cb * 16 + sq
                    ccnt = min(64, col - k // 2 + k) - max(0, col - k // 2)
                    ic[kq, sq, g] = 1.0 / (len(rows) * ccnt)
            else:
                seq, cpos = divmod(kq, 16)
                for a, off in ((0, 0), (1, 1), (2, -1)):
                    m = cpos + off
                    if 0 <= m < 16:
                        pm[g, a, seq * 16 + m, kq] = 1.0
                for sq in range(16):
                    t = cpos * 16 + sq
                    cnt = min(256, t - k // 2 + k) - max(0, t - k // 2)
                    ic[kq, sq, g] = 1.0 / cnt
    return pm.reshape(12, 128, 128), ic


def _s5_consts():
    expo = np.zeros((128, 34), np.float32)
    expo[:, 0], expo[:, 1] = 1.0, 16.0
    sidx = np.arange(16, dtype=np.float32)
    expo[:64, 2:18] = -(sidx + 1)
    expo[64:, 2:18] = -(16 - sidx)
    expo[:64, 18:34] = sidx + 1
    expo[64:, 18:34] = 16 - sidx
    wm = np.zeros((128, 6, 256), np.float32)
    for p in range(128):
        s8, c = divmod(p, 16)
        for jt in range(2):
            sv = 8 * jt + s8
            for sp in range(16):
                wm[p, 0 * 2 + jt, sp * 16:(sp + 1) * 16] = 1.0 if sp >= sv else 0.0
                wm[p, 1 * 2 + jt, sp * 16:(sp + 1) * 16] = 1.0 if sp <= sv else 0.0
            wm[p, 4 + jt, sv * 16 + c] = 1.0
    return expo, wm


def kernel(**inputs):
    cfg = inputs.pop("_cfg", {})
    f = lambda a: np.ascontiguousarray(np.asarray(a, dtype=np.float32))
    x_prompt, x_sample = f(inputs["x_prompt"]), f(inputs["x_sample"])
    c, c_ctx = f(inputs["c"]), f(inputs["c_ctx"])
    shared = {k: f(inputs[k]) for k in ("ada_w", "ada_b", "ln_g", "ln_b", "ffn_w_gate", "ffn_w_up", "ffn_w_down",
                                         "pool_w", "pool_scale")}
    shared["ident"] = np.eye(128, dtype=np.float32)
    pidx = np.arange(128)
    shared["selT"] = (pidx[:, None] % 16 == pidx[None, :] % 16).astype(np.float32)
    shared["maskG"] = (pidx[:, None] // 16 == np.arange(8)[None, :]).astype(np.float32)
    in_maps = []
    for core in range(NCORES):
        m = dict(shared)
        if core < 4:
            m["x"] = x_sample[core]
            cond = c[core]
        else:
            m["x"] = x_prompt[(core - 4) * 8:(core - 3) * 8].reshape(2048, D)
            cond = c_ctx
        m["cond"] = np.ascontiguousarray(cond.reshape(8, 128).T)
        m["pmat"], m["icnt"] = _pool_consts(core < 4)
        for k in ("ssm_a_re", "ssm_a_im", "ssm_log_step", "ssm_b_re", "ssm_b_im", "ssm_c_re", "ssm_c_im", "ssm_d",
                  "ssm_glu_w", "ssm_glu_b"):
            m[k] = f(inputs[k])
        h0 = np.zeros((2, 128, 2, 64), np.float32)
        mrow = np.ones((128, 128), np.float32)
        if core < 4:
            for ri, key in enumerate(("state_ssm_re", "state_ssm_im")):
                stt = f(inputs[key])[core]
                h0[:, :, ri, :] = stt.transpose(0, 1, 3, 2).reshape(2, 128, 64)
        else:
            mrow[:, ::16] = 0.0
        m["h0"], m["mrow"] = h0, mrow
        m["expo"], m["wmask"] = _s5_consts()
        in_maps.append(m)
    nc = build(cfg)
    if cfg.get("trace"):
        res = run_bass_kernel_spmd(nc, in_maps, core_ids=list(range(NCORES)), trace=True)
        return res
    if cfg.get("cores"):
        in_maps = [in_maps[i] for i in cfg["cores"]]
        res = run_bass_kernel_spmd(nc, in_maps, core_ids=list(range(len(in_maps))))
        return res
    res = run_bass_kernel_spmd(nc, in_maps, core_ids=list(range(NCORES)))
    ys = [r["y"] for r in res.results]
    y_sample = np.stack(ys[:4], 0)
    y_prompt = np.concatenate(ys[4:], 0).reshape(32, 256, D)
    st_re = np.concatenate([r["st_re"] for r in res.results[4:]], 0)
    st_im = np.concatenate([r["st_im"] for r in res.results[4:]], 0)
    return (y_prompt, y_sample, st_re, st_im)
```

```python
import numpy as np
import concourse.bass as bass
import concourse.mybir as mybir
from concourse.bass_utils import run_bass_kernel_spmd

F32 = mybir.dt.float32
BF16 = mybir.dt.bfloat16
AF = mybir.ActivationFunctionType
ALU = mybir.AluOpType

D = 1024
DFF = 2816
NFT = 22
DEPTH = 4
ALPHA = (2 * DEPTH) ** 0.25
EPS = 1e-5
NCORES = 8
WINDOWS = (2, 4, 8, 16)


class Prog:
    ENG = ("pe", "act", "dve", "pool", "sp")

    def __init__(self, nc, n_dma_sems=6):
        self.nc = nc
        self.ops = {e: [] for e in self.ENG}
        self.count = {}
        self.waited = {e: {} for e in self.ENG}
        self.bufs = {}
        self.n_dma = n_dma_sems
        self.dma_rr = {e: 0 for e in self.ENG}
        self.out_tickets = []

    def _deps(self, reads, writes):
        w = {}

        def add(t):
            if t is None:
                return
            k, v = t
            if w.get(k, 0) < v:
                w[k] = v

        for key in reads:
            b = self.bufs.get(key)
            if b:
                add(b["w"])
        for key in writes:
            b = self.bufs.get(key)
            if b:
                add(b["w"])
                for k, v in b["r"].items():
                    add((k, v))
        return w

    def _filter(self, eng, w, skip_self=False):
        out = []
        for k, v in w.items():
            if skip_self and k == eng:
                continue
            if self.waited[eng].get(k, 0) >= v:
                continue
            self.waited[eng][k] = v
            out.append((k, v))
        return out

    def _commit(self, ticket, reads, writes):
        k, v = ticket
        for key in reads:
            b = self.bufs.setdefault(key, {"w": None, "r": {}})
            if b["r"].get(k, 0) < v:
                b["r"][k] = v
        for key in writes:
            self.bufs[key] = {"w": ticket, "r": {}}

    def op(self, eng, fn, reads=(), writes=(), signal=True, extra_waits=()):
        w = self._deps(reads, writes)
        for t in extra_waits:
            if t is not None and w.get(t[0], 0) < t[1]:
                w[t[0]] = t[1]
        waits = self._filter(eng, w, skip_self=(eng == "pe"))
        inc = None
        ticket = None
        if signal:
            self.count[eng] = self.count.get(eng, 0) + 1
            ticket = (eng, self.count[eng])
            inc = (eng, 1)
            self._commit(ticket, reads, writes)
        self.ops[eng].append((waits, fn, inc))
        return ticket

    def dma(self, eng, fn, reads=(), writes=(), is_output=False):
        i = self.dma_rr[eng]
        self.dma_rr[eng] = (i + 1) % self.n_dma
        sk = "dma_%s_%d" % (eng, i)
        prev = self.count.get(sk, 0)
        w = self._deps(reads, writes)
        if prev > 0 and w.get(sk, 0) < prev:
            w[sk] = prev
        waits = self._filter(eng, w)
        self.count[sk] = prev + 16
        ticket = (sk, prev + 16)
        self._commit(ticket, reads, writes)
        self.ops[eng].append((waits, fn, (sk, 16)))
        if is_output:
            self.out_tickets.append(ticket)
        return ticket

    def barrier(self):
        for e in self.ENG:
            w = {k: v for k, v in self.count.items()}
            waits = self._filter(e, w, skip_self=False)
            waits = [(k, v) for k, v in waits if k != e]
            if waits:
                self.ops[e].append((waits, None, None))

    def finish(self):
        w = {}
        for k, v in self.out_tickets:
            w[k] = max(w.get(k, 0), v)
        waits = self._filter("sp", w)
        if waits:
            self.ops["sp"].append((waits, None, None))

    def replay(self):
        nc = self.nc
        sem_names = sorted(self.count.keys())
        import contextlib

        with contextlib.ExitStack() as st:
            sems = {k: st.enter_context(nc.semaphore("s_" + k)) for k in sem_names}
            block = st.enter_context(nc.Block())
            engs = {"pe": block.tensor, "act": block.scalar, "dve": block.vector,
                    "pool": block.gpsimd, "sp": block.sync}

            def mk(ename):
                ops = self.ops[ename]

                def body(e):
                    for waits, fn, inc in ops:
                        for k, v in waits:
                            e.wait_ge(sems[k], v)
                        if fn is not None:
                            ins = fn(e)
                            if inc is not None:
                                ins.then_inc(sems[inc[0]], inc[1])
                return body

            for ename in self.ENG:
                engs[ename](mk(ename))


class StopBuild(Exception):
    pass


class Buf:
    def __init__(self, tens, width, off, shape, pbase=0):
        self.t = tens
        self.W = width
        self.off = off
        self.shape = tuple(shape)
        st = []
        acc = 1
        for n in reversed(self.shape):
            st.append(acc)
            acc *= n
        self.strides = tuple(reversed(st))
        self.size = acc
        self.pbase = pbase

    def ap(self, dims, off=0, p0=0, np_=128):
        return bass.AP(self.t, (self.pbase + p0) * self.W + self.off + off,
                       [[self.W, np_]] + [list(d) for d in dims])

    def __getitem__(self, idx):
        if not isinstance(idx, tuple):
            idx = (idx,)
        ps = idx[0]
        p0, p1 = (ps.start or 0), (128 if ps.stop is None else ps.stop)
        off = 0
        dims = []
        rest = idx[1:]
        for i, n in enumerate(self.shape):
            s = self.strides[i]
            if i < len(rest):
                r = rest[i]
                if isinstance(r, int):
                    off += r * s
                    continue
                a = r.start or 0
                b = n if r.stop is None else r.stop
                stp = r.step or 1
                off += a * s
                dims.append([s * stp, (b - a + stp - 1) // stp])
            else:
                dims.append([s, n])
        merged = []
        for d in dims:
            if merged and merged[-1][0] == d[0] * d[1]:
                merged[-1] = [d[0], merged[-1][1] * d[1]]
            else:
                merged.append(d)
        if not merged:
            merged = [[1, 1]]
        return self.ap(merged, off, p0, p1 - p0)


class Arena:
    def __init__(self, t32, t16, width32):
        self.t32, self.t16, self.W = t32, t16, width32
        self.top = 0
        self.marks = []
        self.guard = None

    def alloc(self, shape, dt=F32):
        n = int(np.prod(shape))
        if dt == F32:
            b = Buf(self.t32, self.W, self.top, shape)
            self.top += n
        else:
            b = Buf(self.t16, 2 * self.W, 2 * self.top, shape)
            self.top += (n + 1) // 2
        assert self.top <= self.W, ("arena overflow", self.top, self.W)
        assert self.guard is None or self.top <= self.guard, ("arena guard hit", self.top, self.guard)
        return b

    def mark(self):
        self.marks.append(self.top)

    def release(self):
        self.top = self.marks.pop()


def build(cfg):
    nc = bass.Bass("TRN2", target_bir_lowering=False)
    P = Prog(nc)
    depth = cfg.get("depth", DEPTH)
    mixers = cfg.get("mixers", True)

    def din(name, shape):
        return nc.dram_tensor(name, list(shape), F32, kind="ExternalInput").ap()

    x_d = din("x", [2048, D])
    cond_d = din("cond", [128, 8])
    ada_w = din("ada_w", [DEPTH, D, 6 * D])
    ada_b = din("ada_b", [DEPTH, 6 * D])
    ln_g = din("ln_g", [DEPTH, 2, D])
    ln_b = din("ln_b", [DEPTH, 2, D])
    wg_d = din("ffn_w_gate", [DEPTH, D, DFF])
    wu_d = din("ffn_w_up", [DEPTH, D, DFF])
    wd_d = din("ffn_w_down", [DEPTH, DFF, D])
    ident_d = din("ident", [128, 128])
    selT_d = din("selT", [128, 128])
    maskG_d = din("maskG", [128, 8])
    pool_w = din("pool_w", [2, 4, 256, 256])
    pool_scale = din("pool_scale", [2, D])
    pmat_d = din("pmat", [12, 128, 128])
    icnt_d = din("icnt", [128, 16, 4])
    a_re_d = din("ssm_a_re", [2, 2, 64, 64])
    a_im_d = din("ssm_a_im", [2, 2, 64, 64])
    ls_d = din("ssm_log_step", [2, 2, 64])
    b_re_d = din("ssm_b_re", [2, 2, 64, 64, 16])
    b_im_d = din("ssm_b_im", [2, 2, 64, 64, 16])
    c_re_d = din("ssm_c_re", [2, 2, 64, 16, 64])
    c_im_d = din("ssm_c_im", [2, 2, 64, 16, 64])
    dsk_d = din("ssm_d", [2, D])
    gluw_d = din("ssm_glu_w", [2, D, 2 * D])
    glub_d = din("ssm_glu_b", [2, 2 * D])
    h0_d = din("h0", [2, 128, 2, 64])
    mrow_d = din("mrow", [128, 128])
    expo_d = din("expo", [128, 34])
    wmask_d = din("wmask", [128, 6, 256])
    y_d = nc.dram_tensor("y", [2048, D], F32, kind="ExternalOutput").ap()
    st_d = [nc.dram_tensor(nm, [8, 2, 2, 64, 64], F32, kind="ExternalOutput").ap() for nm in ("st_re", "st_im")]
    dbg_d = nc.dram_tensor("dbg", [128, 8192], F32, kind="ExternalOutput").ap() if cfg.get("dbg") else None
    dbg_pos = [0]
    stop_at = cfg.get("s5_stop", None)

    import contextlib
    with contextlib.ExitStack() as es:
        AW = 52900
        a32 = es.enter_context(nc.sbuf_tensor("arena", [128, AW], F32))
        a16 = a32.bitcast(BF16)
        ps32 = es.enter_context(nc.psum_tensor("psum", [128, 4096], F32))
        ps16 = ps32.bitcast(BF16)
        A = Arena(a32, a16, AW)

        def PS(bank, shape, dt=F32, off=0):
            if dt == F32:
                return Buf(ps32, 4096, bank * 512 + off, shape)
            return Buf(ps16, 8192, bank * 1024 + off, shape)

        x = A.alloc([16, D])
        ident = A.alloc([128])
        identb = A.alloc([128], BF16)
        ones = A.alloc([128])
        condc = A.alloc([8])
        scc = A.alloc([8])
        grow = A.alloc([2, D])
        mcol = A.alloc([4, 8])
        LN = {}
        mcolS = A.alloc([2, 64])
        small = A.alloc([64])

        x_R = x_d.rearrange("(k s) c -> k s c", s=16)
        y_R = y_d.rearrange("(k s) c -> k s c", s=16)

        P.dma("sp", lambda e: e.dma_start(out=x[:], in_=x_R), writes=["x"])
        P.dma("sp", lambda e: e.dma_start(out=ident[:], in_=ident_d), writes=["ident"])
        P.dma("sp", lambda e: e.dma_start(out=condc[:], in_=cond_d), writes=["condc"])
        P.op("pool", lambda e: e.memset(ones[:], 1.0), writes=["ones"])
        P.op("dve", lambda e: e.tensor_copy(out=identb[:], in_=ident[:]), reads=["ident"], writes=["identb"])
        P.op("act", lambda e: e.activation(out=scc[:], in_=condc[:], func=AF.Silu), reads=["condc"], writes=["scc"])

        selT = A.alloc([128])
        maskG = A.alloc([8])
        P.dma("sp", lambda e: e.dma_start(out=selT[:], in_=selT_d), writes=["selT"])
        P.dma("sp", lambda e: e.dma_start(out=maskG[:], in_=maskG_d), writes=["maskG"])
        onesb = A.alloc([128], BF16)
        P.op("dve", lambda e: e.tensor_copy(out=onesb[:], in_=ones[:]), reads=["ones"], writes=["onesb"])

        def ada_gen(l, pb0):
            wa = [A.alloc([8, 128], BF16) for _ in range(3)]
            ba = [A.alloc([128], BF16) for _ in range(3)]
            bc = [A.alloc([1]) for _ in range(3)]
            sccb = A.alloc([8, 2], BF16)
            screpb = A.alloc([8, 128], BF16)
            rhsS = [A.alloc([8]) for _ in range(2)]
            grow_n = A.alloc([2, D]); mcol_n = A.alloc([4, 8]); mcolS_n = A.alloc([2, 64])
            AD["bufs"] = (grow_n, mcol_n, mcolS_n)
            P.op("dve", lambda e: e.tensor_copy(out=screpb.ap([[128, 8], [1, 128]]), in_=scc.ap([[1, 8], [0, 128]])),
                 reads=["scc"], writes=["screpb"])
            P.op("dve", lambda e: e.tensor_copy(out=sccb.ap([[2, 8], [1, 2]]), in_=scc.ap([[1, 8], [0, 2]])),
                 reads=["scc"], writes=["sccb"])

            def load(b):
                i3 = b % 3
                v = b // 8
                src = ada_w[l, :, b * 128:(b + 1) * 128].rearrange("(t p) n -> p t n", p=128)
                P.dma("pool", lambda e: e.dma_start(out=wa[i3][:], in_=src), writes=["adaw%d" % i3])
                if v in (2, 5):
                    P.dma("pool", lambda e: e.dma_start(out=ba[i3][0:1, :], in_=ada_b[l:l + 1, b * 128:(b + 1) * 128]),
                          writes=["adab%d" % i3])
                else:
                    P.dma("sp", lambda e: e.dma_start(out=bc[i3][:], in_=ada_b[l, b * 128:(b + 1) * 128].rearrange("(p o) -> p o", o=1)),
                          writes=["adac%d" % i3])

            pending = []
            load(0)
            load(1)
            for b in range(48):
                v, ct = divmod(b, 8)
                i2, i3 = b % 2, b % 3
                if b + 2 < 48:
                    load(b + 2)
                w_ = wa[i3]
                wk = "adaw%d" % i3
                for fn in pending:
                    fn()
                pending = []
                if v in (2, 5):
                    gi = 0 if v == 2 else 1
                    pa = PS(pb0 + i2, [128])
                    pka = "adabk%d" % i2
                    for kt in range(8):
                        P.op("pe", lambda e, kt=kt, w_=w_, pa=pa: e.matmul(pa[:], screpb[:, kt, :], w_[:, kt, :],
                                                                           start=(kt == 0), stop=False),
                             reads=["screpb", wk], writes=[pka], signal=(kt == 0))
                    P.op("pe", lambda e, pa=pa, i3=i3: e.matmul(pa[:], onesb[0:1, :], ba[i3][0:1, :], start=False, stop=True),
                         reads=["onesb", "adab%d" % i3, "screpb", wk], writes=[pka])
                    P.op("act", lambda e, pa=pa, gi=gi, ct=ct: e.activation(
                        out=grow_n[:, gi, ct * 128:(ct + 1) * 128], in_=pa[:], func=AF.Identity),
                         reads=[pka], writes=["grow_n"])
                else:
                    vi = {0: 0, 1: 1, 3: 2, 4: 3}[v]
                    pc = PS(pb0 + i2, [2], off=128)
                    pkc = "adabk%d" % i2
                    for kt in range(8):
                        P.op("pe", lambda e, kt=kt, w_=w_, pc=pc: e.matmul(pc[:], w_[:, kt, :], sccb[:, kt, :],
                                                                           start=(kt == 0), stop=(kt == 7)),
                             reads=["sccb", wk], writes=[pkc], signal=(kt in (0, 7)))
                    addc = 1.0 if vi in (1, 3) else 0.0
                    P.op("dve", lambda e, pc=pc, vi=vi, ct=ct, addc=addc, i3=i3: e.scalar_tensor_tensor(
                        out=mcol_n[:, vi, ct:ct + 1], in0=pc[:, 0:1], scalar=addc, in1=bc[i3][:, 0:1], op0=ALU.add, op1=ALU.add),
                         reads=[pkc, "adac%d" % i3], writes=["mcol_n"])
                    if v in (0, 1):
                        rs = rhsS[ct % 2]
                        rk = "adars%d" % (ct % 2)
                        P.op("dve", lambda e, rs=rs, vi=vi, ct=ct: e.tensor_scalar(
                            out=rs[:], in0=maskG[:], scalar1=mcol_n[:, vi, ct:ct + 1], scalar2=None, op0=ALU.mult),
                             reads=["mcol_n", "maskG"], writes=[rk])

                        def later(rs=rs, rk=rk, v=v, ct=ct, i2=i2):
                            p3 = PS(pb0 + i2, [8], off=256)
                            pk3 = "adabk%d" % i2
                            P.op("pe", lambda e: e.matmul(p3[:], selT[:], rs[:], start=True, stop=True),
                                 reads=[rk, "selT"], writes=[pk3])
                            P.op("dve", lambda e: e.tensor_copy(out=mcolS_n[:, v, ct * 8:(ct + 1) * 8], in_=p3[:]),
                                 reads=[pk3], writes=["mcolS_n"])
                        pending.append(later)
                yield
            for fn in pending:
                fn()

        AD = {}

        def ada_commit():
            grow_n, mcol_n, mcolS_n = AD["bufs"]
            P.op("act", lambda e: e.activation(out=grow[:], in_=grow_n[:], func=AF.Identity),
                 reads=["grow_n"], writes=["grow0", "grow1"])
            P.op("dve", lambda e: e.tensor_copy(out=mcol[:], in_=mcol_n[:]), reads=["mcol_n"], writes=["mcol"])
            P.op("dve", lambda e: e.tensor_copy(out=mcolS[:], in_=mcolS_n[:]), reads=["mcolS_n"], writes=["mcolS"])

        ADA_HI = 4800

        def ada_mod(l):
            saved = A.top
            A.top = A.W - ADA_HI
            assert A.top >= saved
            A.mark()
            for _ in ada_gen(l, 6):
                pass
            ada_commit()
            A.release()
            A.top = saved

        def load_ln(l, which):
            LN["row"] = A.alloc([2, D])
            lnrow = LN["row"]
            for i, src in enumerate((ln_g, ln_b)):
                P.dma("sp", lambda e, i=i, src=src: e.dma_start(
                    out=lnrow[:, i, :], in_=src[l, which:which + 1, :].partition_broadcast(128)), writes=["lnrow%d" % i])

        lnst = A.alloc([16, 2, 6])
        lnmv = A.alloc([16, 2])
        lnr = A.alloc([3, 16])

        def layer_norm_batch(s0, n, pool_assist=False):
            lnrow = LN["row"]
            xk = ["x%d" % s for s in range(s0, s0 + n)]
            for s in range(s0, s0 + n):
                for h in range(2):
                    P.op("dve", lambda e, s=s, h=h: e.bn_stats(out=lnst[:, s, h, :], in_=x[:, s, h * 512:(h + 1) * 512]),
                         reads=["x%d" % s], writes=["lnst%d_%d" % (s, h)])
                P.op("dve", lambda e, s=s: e.bn_aggr(out=lnmv[:, s, :], in_=lnst[:, s, :, :]),
                     reads=["lnst%d_0" % s, "lnst%d_1" % s], writes=["lnmv"])
            var = lnmv.ap([[2, n]], 2 * s0 + 1)
            mean = lnmv.ap([[2, n]], 2 * s0)
            P.op("act", lambda e: e.activation(out=lnr[:, 0, s0:s0 + n], in_=var, func=AF.Sqrt, bias=epsc[:, 0:1], scale=1.0),
                 reads=["lnmv", "epsc"], writes=["lnr0"])
            P.op("dve", lambda e: e.reciprocal(out=lnr[:, 1, s0:s0 + n], in_=lnr[:, 0, s0:s0 + n]), reads=["lnr0"], writes=["lnr1"])
            P.op("dve", lambda e: e.scalar_tensor_tensor(out=lnr[:, 2, s0:s0 + n], in0=mean, scalar=-1.0, in1=lnr[:, 1, s0:s0 + n],
                                                          op0=ALU.mult, op1=ALU.mult), reads=["lnmv", "lnr1"], writes=["lnr2"])
            for s in range(s0, s0 + n):
                P.op("act", lambda e, s=s: e.activation(out=x[:, s, :], in_=x[:, s, :], func=AF.Identity,
                                                        scale=lnr[:, 1, s:s + 1], bias=lnr[:, 2, s:s + 1]),
                     reads=["x%d" % s, "lnr1", "lnr2"], writes=["x%d" % s])
            if pool_assist:
                nd = (2 * n + 2) // 3
                parts = (("dve", s0, nd), ("pool", s0 + nd, n - nd))
            else:
                h2 = n // 2
                parts = (("dve", s0, h2), ("dve", s0 + h2, n - h2))
            for (eng, a, m) in parts:
                ks = ["x%d" % s for s in range(a, a + m)]
                P.op(eng, lambda e, a=a, m=m: e.tensor_tensor(out=x[:, a:a + m, :], in0=x[:, a:a + m, :],
                                                              in1=lnrow.ap([[0, m], [1, D]], 0), op=ALU.mult),
                     reads=ks + ["lnrow0"], writes=ks)
                P.op(eng, lambda e, a=a, m=m: e.tensor_tensor(out=x[:, a:a + m, :], in0=x[:, a:a + m, :],
                                                              in1=lnrow.ap([[0, m], [1, D]], D), op=ALU.add),
                     reads=ks + ["lnrow1"], writes=ks + ["x"])

        epsc = A.alloc([1])
        P.op("pool", lambda e: e.memset(epsc[:], EPS), writes=["epsc"])

        def ffn(l, nxt=None):
            A.mark()
            agen = ada_gen(nxt, 0) if nxt is not None else None

            def ada_step(n=1):
                if agen is None:
                    return
                for _ in range(n):
                    try:
                        next(agen)
                    except StopIteration:
                        return
            aT = A.alloc([8, 1024], BF16)
            hid = A.alloc([NFT, 1024], BF16)
            wgb = [A.alloc([8, 128], BF16) for _ in range(2)]
            wub = [A.alloc([8, 128], BF16) for _ in range(2)]
            wdb = [A.alloc([NFT, 256], BF16) for _ in range(2)]
            sg = [A.alloc([512]) for _ in range(2)]
            load_ln(l, 1)
            nwd = 0
            for half in range(2):
                for ct in range(8):
                    for sq in range(2):
                        pb = 2 + (ct * 2 + sq) % 2
                        pt = PS(pb, [4, 128])
                        for q in range(4):
                            s = half * 8 + sq * 4 + q
                            P.op("pe", lambda e, pt=pt, q=q, s=s, ct=ct: e.transpose(
                                pt[:, q, :], x[:, s, ct * 128:(ct + 1) * 128], ident[:]),
                                 reads=["x%d" % s, "ident"] if q < 3 else ["x%d" % (s - i) for i in range(4)] + ["ident"],
                                 writes=["ps%d" % pb] if q in (0, 3) else [], signal=(q in (0, 3)))
                        P.op("act", lambda e, pt=pt, ct=ct, sq=sq: e.activation(
                            out=aT[:, ct, sq * 512:(sq + 1) * 512], in_=pt[:], func=AF.Identity,
                            scale=mcol[:, 3, ct:ct + 1], bias=mcol[:, 2, ct:ct + 1]),
                             reads=["ps%d" % pb, "mcol"], writes=["aT%d" % sq])
                for ft in range(NFT):
                    ada_step(1)
                    wb_i = ft % 2
                    for (dst, srcw, nm) in ((wgb[wb_i], wg_d, "wg"), (wub[wb_i], wu_d, "wu")):
                        src = srcw[l, :, ft * 128:(ft + 1) * 128].rearrange("(t p) n -> p t n", p=128)
                        P.dma("pool", lambda e, dst=dst, src=src: e.dma_start(out=dst[:], in_=src),
                              writes=["%s%d" % (nm, wb_i)])
                    for blk in range(2):
                        pg, pu = PS(2 + blk * 2, [512]), PS(3 + blk * 2, [512])
                        kg, ku = "ps%d" % (2 + blk * 2), "ps%d" % (3 + blk * 2)
                        for (pp, kk_, wbuf, nm) in ((pg, kg, wgb[wb_i], "wg"), (pu, ku, wub[wb_i], "wu")):
                            for ct in range(8):
                                P.op("pe", lambda e, pp=pp, wbuf=wbuf, ct=ct, blk=blk: e.matmul(
                                    pp[:], wbuf[:, ct, :], aT[:, ct, blk * 512:(blk + 1) * 512],
                                    start=(ct == 0), stop=(ct == 7)),
                                     reads=["%s%d" % (nm, wb_i), "aT%d" % blk], writes=[kk_],
                                     signal=(ct in (0, 7)))
                        sgt = sg[blk]
                        P.op("act", lambda e, sgt=sgt, pg=pg: e.activation(out=sgt[:], in_=pg[:], func=AF.Silu),
                             reads=[kg], writes=["sg%d" % blk])
                        P.op("dve", lambda e, sgt=sgt, pu=pu, ft=ft, blk=blk: e.tensor_tensor(
                            out=hid[:, ft, blk * 512:(blk + 1) * 512], in0=pu[:], in1=sgt[:], op=ALU.mult),
                             reads=[ku, "sg%d" % blk], writes=["hid"])
                for cq in range(4):
                    ada_step(1)
                    wb_i = nwd % 2
                    nwd += 1
                    wdt = wdb[wb_i]
                    src = wd_d[l, :, cq * 256:(cq + 1) * 256].rearrange("(t p) n -> p t n", p=128)
                    P.dma("pool", lambda e, wdt=wdt, src=src: e.dma_start(out=wdt[:], in_=src), writes=["wd%d" % wb_i])
                    for s8 in range(8):
                        s = half * 8 + s8
                        pb = 6 + (s8 % 2)
                        po = PS(pb, [256])
                        for ft in range(NFT):
                            P.op("pe", lambda e, po=po, ft=ft, s8=s8, wdt=wdt: e.matmul(
                                po[:], hid[:, ft, s8 * 128:(s8 + 1) * 128], wdt[:, ft, :],
                                start=(ft == 0), stop=(ft == NFT - 1)),
                                 reads=["hid", "wd%d" % wb_i], writes=["ps%d" % pb], signal=(ft in (0, NFT - 1)))
                        tq = sg[s8 % 2]
                        P.op("dve", lambda e, tq=tq, po=po, cq=cq: e.tensor_tensor(
                            out=tq[:, 0:256], in0=po[:], in1=grow[:, 1, cq * 256:(cq + 1) * 256], op=ALU.mult),
                             reads=["ps%d" % pb, "grow1"], writes=["sg%d" % (s8 % 2)])
                        P.op("dve", lambda e, tq=tq, s=s, cq=cq: e.scalar_tensor_tensor(
                            out=x[:, s, cq * 256:(cq + 1) * 256], in0=x[:, s, cq * 256:(cq + 1) * 256], scalar=ALPHA,
                            in1=tq[:, 0:256], op0=ALU.mult, op1=ALU.add),
                             reads=["sg%d" % (s8 % 2), "x%d" % s], writes=["x%d" % s])
                if half == 1 and agen is not None:
                    ada_step(100)
                    ada_commit()
                layer_norm_batch(half * 8, 8)
            A.release()
            P.barrier()

        def mixer_none(l):
            A.mark()
            load_ln(l, 0)
            for s in range(16):
                P.op("dve", lambda e, s=s: e.tensor_scalar(out=x[:, s, :], in0=x[:, s, :], scalar1=ALPHA, scalar2=None,
                                                           op0=ALU.mult), reads=["x%d" % s], writes=["x%d" % s])
            layer_norm_batch(0, 16)
            A.release()
            P.barrier()

        def mixer_pool(l):
            j = l // 2
            A.mark()
            hT = [A.alloc([8, 128], BF16) for _ in range(2)]
            zb = A.alloc([16, D], BF16)
            pw = A.alloc([4, 2, 256], BF16)
            pm = A.alloc([12, 128], BF16)
            ic = A.alloc([16, 4])
            psrow = A.alloc([D])
            tq = [A.alloc([256]) for _ in range(2)]
            load_ln(l, 0)
            P.dma("pool", lambda e: e.dma_start(out=pw[:], in_=pool_w[j].rearrange("g (t p) n -> p g t n", p=128)),
                  writes=["pw"])
            P.dma("pool", lambda e: e.dma_start(out=pm[:], in_=pmat_d.rearrange("a k m -> k a m")), writes=["pm"])
            P.dma("sp", lambda e: e.dma_start(out=ic[:], in_=icnt_d), writes=["ic"])
            P.dma("sp", lambda e: e.dma_start(out=psrow[:], in_=pool_scale[j:j + 1, :].partition_broadcast(128)),
                  writes=["psrow"])
            P.op("dve", lambda e: e.tensor_tensor(out=psrow[:], in0=psrow[:], in1=grow[:, 0, :], op=ALU.mult),
                 reads=["psrow", "grow0"], writes=["psrow"])
            def pool_tr(s):
                ht = hT[s % 2]
                hk = "hT%d" % (s % 2)
                for ct in range(8):
                    pb = ct % 2
                    pt = PS(pb, [128])
                    P.op("pe", lambda e, pt=pt, s=s, ct=ct: e.transpose(pt[:], x[:, s, ct * 128:(ct + 1) * 128], ident[:]),
                         reads=["x%d" % s, "ident"], writes=["ps%d" % pb])
                    P.op("act", lambda e, pt=pt, ht=ht, ct=ct: e.activation(
                        out=ht[:, ct, :], in_=pt[:], func=AF.Identity, scale=mcol[:, 1, ct:ct + 1],
                        bias=mcol[:, 0, ct:ct + 1]), reads=["ps%d" % pb, "mcol"], writes=[hk])

            def pool_zm(s):
                ht = hT[s % 2]
                hk = "hT%d" % (s % 2)
                for g in range(4):
                    pb = 2 + g % 2
                    pz = PS(pb, [256])
                    for kt in range(2):
                        P.op("pe", lambda e, pz=pz, ht=ht, g=g, kt=kt: e.matmul(
                            pz[:], ht[:, 2 * g + kt, :], pw[:, g, kt, :], start=(kt == 0), stop=(kt == 1)),
                             reads=[hk, "pw"], writes=["ps%d" % pb])
                    P.op("act", lambda e, pz=pz, s=s, g=g: e.activation(out=zb[:, s, g * 256:(g + 1) * 256], in_=pz[:],
                                                                     func=AF.Identity), reads=["ps%d" % pb], writes=["zb"])

            pool_tr(0)
            for s in range(16):
                if s + 1 < 16:
                    pool_tr(s + 1)
                pool_zm(s)
            n = 0
            for sp_ in range(16):
                for g, k in enumerate(WINDOWS):
                    pb = 4 + n % 2
                    tqq = tq[n % 2]
                    tqk = "tq%d" % (n % 2)
                    n += 1
                    pp = PS(pb, [256])
                    terms = []
                    for dlt in range(-(k // 2), k - k // 2):
                        sidx = sp_ + dlt
                        a = 0
                        if sidx < 0:
                            sidx, a = sidx + 16, 2
                        elif sidx >= 16:
                            sidx, a = sidx - 16, 1
                        terms.append((sidx, a))
                    for i, (sidx, a) in enumerate(terms):
                        P.op("pe", lambda e, pp=pp, g=g, a=a, sidx=sidx, i=i, nt=len(terms): e.matmul(
                            pp[:], pm[:, g * 3 + a, :], zb[:, sidx, g * 256:(g + 1) * 256], start=(i == 0), stop=(i == nt - 1)),
                             reads=["zb", "pm"], writes=["ps%d" % pb], signal=(i in (0, len(terms) - 1)))
                    P.op("dve", lambda e, pp=pp, tqq=tqq, sp_=sp_, g=g: e.scalar_tensor_tensor(
                        out=tqq[:], in0=pp[:], scalar=ic[:, sp_, g:g + 1], in1=zb[:, sp_, g * 256:(g + 1) * 256],
                        op0=ALU.mult, op1=ALU.subtract), reads=["ps%d" % pb, "ic", "zb"], writes=[tqk])
                    P.op("dve", lambda e, tqq=tqq, g=g: e.tensor_tensor(out=tqq[:], in0=tqq[:], in1=psrow[:, g * 256:(g + 1) * 256],
                                                                    op=ALU.mult), reads=[tqk, "psrow"], writes=[tqk])
                    P.op("dve", lambda e, tqq=tqq, sp_=sp_, g=g: e.scalar_tensor_tensor(
                        out=x[:, sp_, g * 256:(g + 1) * 256], in0=x[:, sp_, g * 256:(g + 1) * 256], scalar=ALPHA, in1=tqq[:],
                        op0=ALU.mult, op1=ALU.add), reads=[tqk, "x%d" % sp_], writes=["x%d" % sp_])
            layer_norm_batch(0, 16, pool_assist=True)
            A.release()
            P.barrier()

        I32 = mybir.dt.int32
        a32i = a32.bitcast(I32)

        def V(eng, method, reads, writes, **kw):
            return P.op(eng, lambda e: getattr(e, method)(**kw), reads=reads, writes=writes)

        def dump(name, ap_, n, reads):
            if dbg_d is None:
                return
            c0 = dbg_pos[0]
            dbg_pos[0] += n
            print("DBG", name, c0, n)
            P.dma("sp", lambda e: e.dma_start(out=dbg_d[:, c0:c0 + n], in_=ap_), reads=reads, is_output=True)

        def stop(k):
            if stop_at == k:
                raise StopBuild()

        def sincos(ang, n, cosb, sinb, key):
            r = ang; m = A.alloc([n]); rf = m
            ri = Buf(a32i, AW, A.top, [n]); A.top += n
            k = key
            V("dve", "tensor_scalar", [k + "ang"], [k + "r"], out=r[:], in0=ang[:], scalar1=1.0 / (2 * np.pi), scalar2=None, op0=ALU.mult)
            V("dve", "tensor_copy", [k + "r"], [k + "ri"], out=ri[:], in_=r[:])
            V("dve", "tensor_copy", [k + "ri"], [k + "m"], out=rf[:], in_=ri[:])
            V("dve", "tensor_tensor", [k + "r", k + "m"], [k + "r"], out=r[:], in0=r[:], in1=rf[:], op=ALU.subtract)
            for (thr, op, sign) in ((0.5, ALU.is_gt, ALU.subtract), (-0.5, ALU.is_lt, ALU.add)):
                V("dve", "tensor_scalar", [k + "r"], [k + "m"], out=m[:], in0=r[:], scalar1=thr, scalar2=None, op0=op)
                V("dve", "tensor_tensor", [k + "r", k + "m"], [k + "r"], out=r[:], in0=r[:], in1=m[:], op=sign)
            V("act", "activation", [k + "r"], [k + "sin"], out=sinb[:], in_=r[:], func=AF.Sin, scale=6.283185)
            V("dve", "tensor_scalar", [k + "r"], [k + "r"], out=r[:], in0=r[:], scalar1=0.25, scalar2=None, op0=ALU.add)
            V("dve", "tensor_scalar", [k + "r"], [k + "m"], out=m[:], in0=r[:], scalar1=0.5, scalar2=None, op0=ALU.is_gt)
            V("dve", "tensor_tensor", [k + "r", k + "m"], [k + "r"], out=r[:], in0=r[:], in1=m[:], op=ALU.subtract)
            V("act", "activation", [k + "r"], [k + "cos"], out=cosb[:], in_=r[:], func=AF.Sin, scale=6.283185)

        def mixer_s5(l):
            jj = l // 2
            if l == 0:
                A.guard = A.W - ADA_HI
            GQ = 4
            A.mark()
            th = A.alloc([64]); adt = A.alloc([64]); are = A.alloc([64]); aim = A.alloc([64]); dtb = A.alloc([64])
            cre = A.alloc([64]); cim = A.alloc([64])
            expo = A.alloc([34])
            TRE = A.alloc([64 * 34]); TIM = A.alloc([64 * 34])
            BB = [A.alloc([64 * 16]) for _ in range(2)]
            CT = [A.alloc([64 * 16]) for _ in range(2)]
            Dcol = A.alloc([64])
            wmask = A.alloc([6, 256])
            mrow = A.alloc([128])
            Sfin = A.alloc([8 * 2 * 64])
            nat = A.alloc([3, 128])
            for i_, (dst, src, nm) in enumerate(((are, a_re_d, "are"), (aim, a_im_d, "aim"))):
                P.dma("sp", lambda e, src=src, i_=i_: e.dma_start(
                    out=nat.ap([[64, 2], [1, 64]], i_ * 128, 0, 64), in_=src[jj].rearrange("d g n -> g d n")), writes=["nat%d" % i_])
                pt = PS(i_, [128])
                V("pe", "transpose", ["nat%d" % i_, "ident"], ["ps%d" % i_], out=pt[:, 0:64], in_=nat.ap([[1, 128]], i_ * 128, 0, 64),
                  identity=ident[0:64, 0:64])
                V("act", "activation", ["ps%d" % i_], [nm], out=dst[:], in_=pt[:, 0:64], func=AF.Identity)
            for d in range(2):
                P.dma("sp", lambda e, d=d: e.dma_start(out=dtb[64 * d:64 * d + 64, :],
                                                       in_=ls_d[jj, d:d + 1, :].partition_broadcast(64)), writes=["dtb"])
            P.dma("sp", lambda e: e.dma_start(out=expo[:], in_=expo_d), writes=["expo"])
            P.dma("sp", lambda e: e.dma_start(out=wmask[:], in_=wmask_d), writes=["wmask"])
            P.dma("sp", lambda e: e.dma_start(out=mrow[:], in_=mrow_d), writes=["mrow"])
            P.dma("sp", lambda e: e.dma_start(out=nat.ap([[1, 16]], 256, 0, 64), in_=dsk_d[jj].rearrange("(g c) -> g c", c=16)),
                  writes=["nat2"])
            V("dve", "tensor_copy", ["nat2", "nat0"], ["nat0"], out=nat.ap([[16, 8], [1, 16]], 0, 0, 64),
              in_=nat.ap([[0, 8], [1, 16]], 256, 0, 64))
            ptd = PS(2, [128])
            V("pe", "transpose", ["nat0", "ident"], ["ps2"], out=ptd[:, 0:64], in_=nat.ap([[1, 128]], 0, 0, 64), identity=ident[0:64, 0:64])
            V("act", "activation", ["ps2"], ["Dcol"], out=Dcol[:], in_=ptd[:, 0:64], func=AF.Identity)
            A.mark()
            Braw = [A.alloc([64 * 16]) for _ in range(2)]
            CN = [A.alloc([8, 128]) for _ in range(2)]
            for ri, (bd, cd) in enumerate(((b_re_d, c_re_d), (b_im_d, c_im_d))):
                for d in range(2):
                    P.dma("act", lambda e, ri=ri, d=d, bd=bd: e.dma_start(
                        out=Braw[ri].ap([[16, 64], [1, 16]], 0, 64 * d, 64), in_=bd[jj, d].rearrange("g n c -> n g c")),
                          writes=["Braw%d" % ri])
                    P.dma("sp", lambda e, ri=ri, d=d, cd=cd: e.dma_start(
                        out=CN[ri].ap([[128, 8], [1, 64]], 64 * d), in_=cd[jj, d].rearrange("(gt g8) o n -> (g8 o) gt n", g8=8)),
                          writes=["CN%d" % ri])
            V("act", "activation", ["dtb"], ["dtb"], out=dtb[:], in_=dtb[:], func=AF.Exp)
            V("dve", "tensor_tensor", ["are", "dtb"], ["adt"], out=adt[:], in0=are[:], in1=dtb[:], op=ALU.mult)
            V("dve", "tensor_tensor", ["aim", "dtb"], ["th"], out=th[:], in0=aim[:], in1=dtb[:], op=ALU.mult)
            NT = 64 * 34
            ANG = A.alloc([NT]); MAG = A.alloc([NT]); COS = TRE; SIN = TIM
            full = [[34, 64], [1, 34]]
            V("dve", "tensor_tensor", ["th", "expo"], ["s5ang"], out=ANG.ap(full), in0=th.ap([[1, 64], [0, 34]]),
              in1=expo.ap([[0, 64], [1, 34]]), op=ALU.mult)
            V("dve", "tensor_tensor", ["adt", "expo"], ["MAG"], out=MAG.ap(full), in0=adt.ap([[1, 64], [0, 34]]),
              in1=expo.ap([[0, 64], [1, 34]]), op=ALU.mult)
            V("act", "activation", ["MAG"], ["MAG"], out=MAG[:], in_=MAG[:], func=AF.Exp)
            sincos(ANG, NT, COS, SIN, "s5")
            V("dve", "tensor_tensor", ["MAG", "s5cos"], ["TRE", "s5cos"], out=TRE[:], in0=MAG[:], in1=COS[:], op=ALU.mult)
            V("dve", "tensor_tensor", ["MAG", "s5sin"], ["TIM", "s5sin"], out=TIM[:], in0=MAG[:], in1=SIN[:], op=ALU.mult)
            stop(1)
            t1 = A.alloc([64]); t2 = A.alloc([64]); nr = A.alloc([64]); rden = A.alloc([64])
            lre, lim = TRE.ap([[34, 64]], 0), TIM.ap([[34, 64]], 0)
            V("dve", "tensor_scalar", ["TRE"], ["nr"], out=nr[:], in0=lre, scalar1=-1.0, scalar2=None, op0=ALU.add)
            V("dve", "tensor_tensor", ["are"], ["t1"], out=t1[:], in0=are[:], in1=are[:], op=ALU.mult)
            V("dve", "tensor_tensor", ["aim"], ["t2"], out=t2[:], in0=aim[:], in1=aim[:], op=ALU.mult)
            V("dve", "tensor_tensor", ["t1", "t2"], ["t1"], out=t1[:], in0=t1[:], in1=t2[:], op=ALU.add)
            V("dve", "reciprocal", ["t1"], ["rden"], out=rden[:], in_=t1[:])
            V("dve", "tensor_tensor", ["nr", "are"], ["t1"], out=t1[:], in0=nr[:], in1=are[:], op=ALU.mult)
            V("dve", "tensor_tensor", ["TIM", "aim"], ["t2"], out=t2[:], in0=lim, in1=aim[:], op=ALU.mult)
            V("dve", "tensor_tensor", ["t1", "t2"], ["t1"], out=t1[:], in0=t1[:], in1=t2[:], op=ALU.add)
            V("dve", "tensor_tensor", ["t1", "rden"], ["cre"], out=cre[:], in0=t1[:], in1=rden[:], op=ALU.mult)
            V("dve", "tensor_tensor", ["TIM", "are"], ["t1"], out=t1[:], in0=lim, in1=are[:], op=ALU.mult)
            V("dve", "tensor_tensor", ["nr", "aim"], ["t2"], out=t2[:], in0=nr[:], in1=aim[:], op=ALU.mult)
            V("dve", "tensor_tensor", ["t1", "t2"], ["t1"], out=t1[:], in0=t1[:], in1=t2[:], op=ALU.subtract)
            V("dve", "tensor_tensor", ["t1", "rden"], ["cim"], out=cim[:], in0=t1[:], in1=rden[:], op=ALU.mult)
            u1 = A.alloc([1024]); u2 = A.alloc([1024])
            f3 = [[16, 64], [1, 16]]
            cb = lambda c: c.ap([[1, 64], [0, 16]])
            for (o, (ca, xa), (cb_, xb), op) in ((0, (cre, 0), (cim, 1), ALU.subtract), (1, (cre, 1), (cim, 0), ALU.add)):
                V("dve", "tensor_tensor", ["cre", "cim", "Braw0", "Braw1"], ["u1"], out=u1.ap(f3), in0=cb(ca), in1=Braw[xa].ap(f3), op=ALU.mult)
                V("dve", "tensor_tensor", ["cre", "cim", "Braw0", "Braw1"], ["u2"], out=u2.ap(f3), in0=cb(cb_), in1=Braw[xb].ap(f3), op=ALU.mult)
                V("dve", "tensor_tensor", ["u1", "u2"], ["BB%d" % o], out=BB[o][:], in0=u1[:], in1=u2[:], op=op)
            stop(2)
            for ri in range(2):
                for gt in range(8):
                    pt = PS(gt % 2, [128])
                    pk = "ps%d" % (gt % 2)
                    V("pe", "transpose", ["CN%d" % ri, "ident"], [pk], out=pt[:], in_=CN[ri][:, gt, :], identity=ident[:])
                    V("act", "activation", [pk], ["CT%d" % ri], out=CT[ri][:, gt * 128:(gt + 1) * 128], in_=pt[:], func=AF.Identity,
                      scale=(-1.0 if ri == 1 else 1.0))
            A.release()
            P.barrier()
            A.guard = None
            yR = A.alloc([16, D], BF16)
            A.mark()
            GQ = 16
            SB = 1
            RQ = [A.alloc([GQ, 256], BF16) for _ in range(2)]
            Bbuf = A.alloc([128, 2, GQ], BF16)
            Pbuf = A.alloc([2, GQ, 128], BF16)
            G_, W2, W4 = GQ, 2 * GQ, 4 * GQ
            A.mark()
            LQ2 = [[A.alloc([SB, 256], BF16) for _ in range(2)] for _ in range(3)]
            g1 = A.alloc([SB * 256]); g2 = A.alloc([SB * 256])
            h1 = A.alloc([SB * 256]); h2 = A.alloc([SB * 256])
            XT = [A.alloc([2, 128], BF16) for _ in range(3)]
            Wg = [A.alloc([4, 256], BF16) for _ in range(2)]
            LTg = [A.alloc([4, 128], BF16) for _ in range(2)]
            xstg = A.alloc([2, 128])
            topA = A.top
            A.release()
            A.mark()
            UU = [A.alloc([8, W2]) for _ in range(3)]
            T2 = A.alloc([8, W2]); prod = A.alloc([8, W4]); C4 = A.alloc([W4]); C256 = A.alloc([W4])
            VV = [A.alloc([W2]) for _ in range(3)]
            Pin = A.alloc([8, W2]); prodb = A.alloc([W4]); tb = A.alloc([W2])
            Zc = [A.alloc([W2]) for _ in range(2)]; prodz = A.alloc([W4])
            ZC = A.alloc([16, W4])
            corr = [A.alloc([8, W2]) for _ in range(2)]; prodj = [A.alloc([8, W4]) for _ in range(2)]
            topS = A.top
            A.release()
            A.mark()
            ytl = [A.alloc([256]) for _ in range(2)]; yul = [A.alloc([256]) for _ in range(2)]; yvl = [A.alloc([256]) for _ in range(2)]
            topC = A.top
            A.release()
            A.top = max(topA, topS, topC)
            assert A.top <= A.W, ("arena overflow (s5 scratch)", A.top)
            o3 = [[256, SB], [16, 16], [1, 16]]

            def gen(eng, which, g0b, dst, t1_, t2_, kpre, doff=0, ksuf=""):
                col, X = (2, BB) if which == "L" else (18, CT)
                xk = ["BB0", "BB1"] if which == "L" else ["CT0", "CT1"]
                tap = lambda T: T.ap([[34, SB], [1, 16], [0, 16]], g0b * 34 + col)
                xap = lambda Xb: Xb.ap([[16, SB], [0, 16], [1, 16]], g0b * 16)
                k1, k2 = kpre + "1", kpre + "2"
                d0 = dst[0].ap([[1, SB * 256]], doff * 256)
                d1 = dst[1].ap([[1, SB * 256]], doff * 256)
                V(eng, "tensor_tensor", ["TRE"] + xk, [k1], out=t1_.ap(o3), in0=tap(TRE), in1=xap(X[0]), op=ALU.mult)
                V(eng, "tensor_tensor", ["TIM"] + xk, [k2], out=t2_.ap(o3), in0=tap(TIM), in1=xap(X[1]), op=ALU.mult)
                V(eng, "tensor_tensor", [k1, k2], [which + "0" + ksuf], out=d0, in0=t1_[:], in1=t2_[:],
                  op=(ALU.subtract if which == "L" else ALU.add))
                V(eng, "tensor_tensor", ["TRE"] + xk, [k1], out=t1_.ap(o3), in0=tap(TRE), in1=xap(X[1]), op=ALU.mult)
                V(eng, "tensor_tensor", ["TIM"] + xk, [k2], out=t2_.ap(o3), in0=tap(TIM), in1=xap(X[0]), op=ALU.mult)
                V(eng, "tensor_tensor", [k1, k2], [which + "1" + ksuf], out=d1, in0=t1_[:], in1=t2_[:],
                  op=(ALU.add if which == "L" else ALU.subtract))

            for Q in range(64 // GQ):
                g0 = Q * GQ
                stop(3)
                P.barrier()
                def stage0(gl):
                    g = g0 + gl
                    LQ = LQ2[gl % 3]
                    lsuf = "_%d" % (gl % 3)
                    gen("dve", "L", g, LQ, g1, g2, "g", ksuf=lsuf)
                    if False:
                        gen("dve", "R", g, RQ, g1, g2, "g", doff=gl, ksuf="_%d" % gl)
                    else:
                        gen("pool", "R", g, RQ, h1, h2, "h", doff=gl, ksuf="_%d" % gl)

                def stage1(gl):
                    g = g0 + gl
                    LQ = LQ2[gl % 3]
                    lsuf = "_%d" % (gl % 3)
                    lt = LTg[gl % 2]; ltk = "LT%d" % (gl % 2)
                    xt = XT[gl % 3]; xtk = "XT%d" % (gl % 3)
                    p2 = PS(2, [4, 128], BF16)
                    for ri in range(2):
                        for jt in range(2):
                            V("pe", "transpose", ["L%d" % ri + lsuf, "identb"], ["ps2"], out=p2[:, ri * 2 + jt, :],
                              in_=LQ[ri][:, 0, jt * 128:(jt + 1) * 128], identity=identb[:])
                    V("act", "activation", ["ps2"], [ltk], out=lt[:], in_=p2[:], func=AF.Identity)
                    p3 = PS(3, [2, 128])
                    for jt in range(2):
                        V("pool", "tensor_copy", ["x"] + ["x%d" % (8 * jt + i) for i in range(8)], ["xstg%d" % jt],
                          out=xstg.ap([[16, 8], [1, 16]], jt * 128), in_=x.ap([[1024, 8], [1, 16]], 8 * jt * 1024 + g * 16))
                        V("pe", "transpose", ["xstg%d" % jt, "ident"], ["ps3"], out=p3[:, jt, :], in_=xstg[:, jt, :], identity=ident[:])
                    V("act", "activation", ["ps3", "mcolS"], [xtk], out=xt[:], in_=p3[:], func=AF.Identity,
                      scale=mcolS[:, 1, g:g + 1], bias=mcolS[:, 0, g:g + 1])

                def stage2(gl):
                    g = g0 + gl
                    LQ = LQ2[gl % 3]
                    lsuf = "_%d" % (gl % 3)
                    lt = LTg[gl % 2]; ltk = "LT%d" % (gl % 2)
                    xt = XT[gl % 3]; xtk = "XT%d" % (gl % 3)
                    p4 = PS(4, [2, 128])
                    for ri in range(2):
                        for jt in range(2):
                            P.op("pe", lambda e, p4=p4, ri=ri, jt=jt, lt=lt, xt=xt: e.matmul(
                                p4[:, ri, :], lt[:, ri * 2 + jt, :], xt[:, jt, :], start=(jt == 0), stop=(jt == 1)),
                                 reads=[ltk, xtk], writes=["ps4"])
                    V("act", "activation", ["ps4"], ["Bbuf"], out=Bbuf.ap([[GQ, 2], [2 * GQ, 128]], gl, 0, 64),
                      in_=p4.ap([[128, 2], [1, 128]], 0, 0, 64), func=AF.Identity)
                    V("act", "activation", ["ps4"], ["Bbuf"], out=Bbuf.ap([[GQ, 2], [-2 * GQ, 128]], 127 * 2 * GQ + gl, 64, 64),
                      in_=p4.ap([[128, 2], [1, 128]], 0, 64, 64), func=AF.Identity)
                    wg = Wg[gl % 2]; wk = "Wg%d" % (gl % 2)
                    for d in range(2):
                        pw_ = PS(d, [2, 256])
                        for jt in range(2):
                            for ri in range(2):
                                P.op("pe", lambda e, pw_=pw_, d=d, jt=jt, ri=ri, gl=gl, LQ=LQ: e.matmul(
                                    pw_[:, jt, :], LQ[ri].ap([[1, 128]], jt * 128, 64 * d, 64),
                                    RQ[ri].ap([[1, 256]], gl * 256, 64 * d, 64), start=(ri == 0), stop=(ri == 1)),
                                     reads=["L0" + lsuf, "L1" + lsuf, "R0_%d" % gl, "R1_%d" % gl], writes=["ps%d" % d])
                            V("dve", "tensor_tensor", ["ps%d" % d, "wmask"], [wk + "_%d" % (d * 2 + jt)], out=wg[:, d * 2 + jt, :],
                              in0=pw_[:, jt, :], in1=wmask[:, d * 2 + jt, :], op=ALU.mult)
                    for jt in range(2):
                        V("dve", "scalar_tensor_tensor", [wk + "_%d" % jt, "wmask", "Dcol"], [wk + "_%d" % jt], out=wg[:, jt, :],
                          in0=wmask[:, 4 + jt, :], scalar=Dcol[:, g:g + 1], in1=wg[:, jt, :], op0=ALU.mult, op1=ALU.add)

                def stage3(gl):
                    g = g0 + gl
                    xt = XT[gl % 3]; xtk = "XT%d" % (gl % 3)
                    wg = Wg[gl % 2]; wk = "Wg%d" % (gl % 2)
                    pb = 5 + gl % 2
                    py = PS(pb, [256])
                    n_mm = 0
                    for d in range(2):
                        for jt in range(2):
                            P.op("pe", lambda e, py=py, d=d, jt=jt, xt=xt, wg=wg, n_mm=n_mm: e.matmul(
                                py[:], xt[:, jt, :], wg[:, d * 2 + jt, :], start=(n_mm == 0), stop=(n_mm == 3)),
                                 reads=[xtk] + [wk + "_%d" % q for q in range(4)], writes=["ps%d" % pb])
                            n_mm += 1
                    V("act", "activation", ["ps%d" % pb], ["yR"], out=yR.ap([[1024, 16], [1, 16]], g * 16),
                      in_=py.ap([[16, 16], [1, 16]]), func=AF.Identity)

                for t in range(GQ + 3):
                    if 0 <= t - 2 < GQ:
                        stage2(t - 2)
                    if t < GQ:
                        stage0(t)
                    if 0 <= t - 1 < GQ:
                        stage1(t - 1)
                    if 0 <= t - 3 < GQ:
                        stage3(t - 3)
                stop(5)
                P.barrier()
                P.dma("sp", lambda e, Q=Q: e.dma_start(out=VV[0].ap([[GQ, 2], [1, GQ]]),
                                                       in_=h0_d[jj, :, :, Q * GQ:(Q + 1) * GQ]), writes=["VV0"])
                lam = lambda T: T.ap([[34, GQ]], g0 * 34 + 1)
                V("dve", "tensor_copy", ["TRE"], ["C4"], out=C4.ap([[G_, 2], [1, G_]]), in_=TRE.ap([[0, 2], [34, GQ]], g0 * 34 + 1))
                V("dve", "tensor_copy", ["TIM"], ["C4"], out=C4[:, 2 * G_:3 * G_], in_=lam(TIM))
                V("dve", "tensor_scalar", ["TIM"], ["C4"], out=C4[:, 3 * G_:4 * G_], in0=lam(TIM), scalar1=-1.0, scalar2=None, op0=ALU.mult)
                P.op("pool", lambda e: e.memset(ZC[:, 0, 0:W2], 1.0), writes=["ZC0"])
                P.op("pool", lambda e: e.memset(ZC[:, 0, W2:W4], 0.0), writes=["ZC0"])
                P.op("pool", lambda e: e.memset(Zc[0][:, 0:G_], 1.0), writes=["Zc0"])
                P.op("pool", lambda e: e.memset(Zc[0][:, G_:W2], 0.0), writes=["Zc0"])
                for j in range(1, 17):
                    zp, zn = Zc[(j - 1) % 2], Zc[j % 2]
                    kzp, kzn = "Zc%d" % ((j - 1) % 2), "Zc%d" % (j % 2)
                    V("pool", "tensor_tensor", [kzp, "C4"], ["prodz"], out=prodz.ap([[W2, 2], [1, W2]]),
                      in0=zp.ap([[0, 2], [1, W2]]), in1=C4.ap([[W2, 2], [1, W2]]), op=ALU.mult)
                    V("pool", "tensor_tensor", ["prodz"], [kzn], out=zn.ap([[G_, 2], [1, G_]]), in0=prodz.ap([[G_, 2], [1, G_]]),
                      in1=prodz.ap([[-G_, 2], [1, G_]], 3 * G_), op=ALU.add)
                    dst = C256 if j == 16 else None
                    dk = "C256" if j == 16 else "ZC%d" % j
                    oa = (lambda a, n: C256.ap([[1, n]], a)) if j == 16 else (lambda a, n, j=j: ZC.ap([[1, n]], j * W4 + a))
                    V("pool", "tensor_copy", [kzn], [dk], out=(C256.ap([[G_, 2], [1, G_]]) if j == 16 else ZC.ap([[G_, 2], [1, G_]], j * W4)),
                      in_=zn.ap([[0, 2], [1, G_]]))
                    V("pool", "tensor_copy", [kzn], [dk], out=oa(2 * G_, G_), in_=zn[:, G_:W2])
                    V("pool", "tensor_scalar", [kzn], [dk], out=oa(3 * G_, G_), in0=zn[:, G_:W2], scalar1=-1.0, scalar2=0.0,
                      op0=ALU.mult, op1=ALU.add)
                P.op("dve", lambda e: e.memset(UU[0][:], 0.0), writes=["UU0"])
                u3 = [[W2, 8], [G_, 2], [1, G_]]
                for j in range(16):
                    up, un = UU[j % 3], UU[(j + 1) % 3]
                    kp, kn = "UU%d" % (j % 3), "UU%d" % ((j + 1) % 3)
                    V("dve", "tensor_tensor", [kp, "Bbuf"], ["T2"], out=T2.ap([[W2, 8], [1, W2]]), in0=up.ap([[W2, 8], [1, W2]]),
                      in1=Bbuf.ap([[16 * W2, 8], [1, W2]], j * W2), op=ALU.add)
                    V("dve", "tensor_tensor", ["T2", "C4"], ["prod"], out=prod.ap([[W4, 8], [W2, 2], [1, W2]]),
                      in0=T2.ap([[W2, 8], [0, 2], [1, W2]]), in1=C4.ap([[0, 8], [W2, 2], [1, W2]]), op=ALU.mult)
                    V("dve", "tensor_tensor", ["prod"], [kn], out=un.ap(u3), in0=prod.ap([[W4, 8], [G_, 2], [1, G_]]),
                      in1=prod.ap([[W4, 8], [-G_, 2], [1, G_]], 3 * G_), op=ALU.add)
                    if j < 15:
                        V("act", "activation", [kn], ["Pbuf"], out=Pbuf.ap([[16, 8], [GQ * 128, 2], [128, GQ]], j + 1, 0, 64),
                          in_=un.ap(u3, 0, 0, 64), func=AF.Identity)
                        V("act", "activation", [kn], ["Pbuf"], out=Pbuf.ap([[-16, 8], [GQ * 128, 2], [128, GQ]], 127 - (j + 1), 64, 64),
                          in_=un.ap(u3, 0, 64, 64), func=AF.Identity)
                U15, kU15 = UU[16 % 3], "UU%d" % (16 % 3)
                for b in range(8):
                    vp, vn = VV[b % 3], VV[(b + 1) % 3]
                    kvp, kvn = "VV%d" % (b % 3), "VV%d" % ((b + 1) % 3)
                    V("dve", "tensor_scalar", [kvp, "mrow"], ["Pin"], out=Pin[:, b, :], in0=vp[:], scalar1=mrow[:, 16 * b:16 * b + 1],
                      scalar2=None, op0=ALU.mult)
                    V("dve", "tensor_tensor", ["Pin", "C256"], ["prodb"], out=prodb.ap([[W2, 2], [1, W2]]),
                      in0=Pin.ap([[0, 2], [1, W2]], b * W2), in1=C256.ap([[W2, 2], [1, W2]]), op=ALU.mult)
                    V("dve", "tensor_tensor", ["prodb"], ["tb"], out=tb.ap([[G_, 2], [1, G_]]), in0=prodb.ap([[G_, 2], [1, G_]]),
                      in1=prodb.ap([[-G_, 2], [1, G_]], 3 * G_), op=ALU.add)
                    V("dve", "tensor_tensor", ["tb", kU15], [kvn], out=vn[:], in0=tb[:], in1=U15[:, b, :], op=ALU.add)
                    for hh in range(2):
                        p0 = 64 * hh
                        bb = b if hh == 0 else 7 - b
                        V("act", "activation", [kvn], ["Sfin"], out=Sfin.ap([[64, 2], [1, GQ]], bb * 128 + g0, p0, 64),
                          in_=vn.ap([[GQ, 2], [1, GQ]], 0, p0, 64), func=AF.Identity)
                for j in range(16):
                    pj, cj = prodj[j % 2], corr[j % 2]
                    kpj, kcj = "prodj%d" % (j % 2), "corr%d" % (j % 2)
                    V("dve", "tensor_tensor", ["Pin", "ZC%d" % j], [kpj], out=pj.ap([[W4, 8], [W2, 2], [1, W2]]),
                      in0=Pin.ap([[W2, 8], [0, 2], [1, W2]]), in1=ZC.ap([[0, 8], [W2, 2], [1, W2]], j * W4), op=ALU.mult)
                    V("dve", "tensor_tensor", [kpj], [kcj], out=cj.ap(u3), in0=pj.ap([[W4, 8], [G_, 2], [1, G_]]),
                      in1=pj.ap([[W4, 8], [-G_, 2], [1, G_]], 3 * G_), op=ALU.add)
                    for hh in range(2):
                        p0 = 64 * hh
                        if hh == 0:
                            pap = Pbuf.ap([[16, 8], [GQ * 128, 2], [128, GQ]], j, 0, 64)
                        else:
                            pap = Pbuf.ap([[-16, 8], [GQ * 128, 2], [128, GQ]], 127 - j, 64, 64)
                        if j == 0:
                            V("act", "activation", [kcj], ["Pbuf"], out=pap, in_=cj.ap(u3, 0, p0, 64), func=AF.Identity)
                        else:
                            V("pool", "tensor_tensor", [kcj, "Pbuf"], ["Pbuf"], out=pap, in0=pap, in1=cj.ap(u3, 0, p0, 64), op=ALU.add)
                P.barrier()
                stop(6)
                for gl in range(GQ):
                    g = g0 + gl
                    pb = 5 + gl % 2
                    py = PS(pb, [256])
                    yt, yu, yv = ytl[gl % 2], yul[gl % 2], yvl[gl % 2]
                    ks = "_%d" % (gl % 2)
                    for ri in range(2):
                        P.op("pe", lambda e, py=py, ri=ri, gl=gl: e.matmul(
                            py[:], Pbuf[:, ri, gl, :], RQ[ri][:, gl, :], start=(ri == 0), stop=(ri == 1)),
                             reads=["Pbuf", "R0_%d" % gl, "R1_%d" % gl], writes=["ps%d" % pb])
                    V("dve", "tensor_tensor", ["ps%d" % pb, "yR"], ["yu" + ks], out=yu.ap([[16, 16], [1, 16]]),
                      in0=py.ap([[16, 16], [1, 16]]), in1=yR.ap([[1024, 16], [1, 16]], g * 16), op=ALU.add)
                    V("act", "activation", ["yu" + ks], ["yt" + ks], out=yt[:], in_=yu[:], func=AF.Square)
                    V("dve", "tensor_scalar", ["yt" + ks], ["yt" + ks], out=yt[:], in0=yt[:], scalar1=0.044715, scalar2=1.0, op0=ALU.mult, op1=ALU.add)
                    V("dve", "tensor_tensor", ["yt" + ks, "yu" + ks], ["yt" + ks], out=yt[:], in0=yt[:], in1=yu[:], op=ALU.mult)
                    V("act", "activation", ["yt" + ks], ["yv" + ks], out=yv[:], in_=yt[:], func=AF.Sigmoid, scale=1.5957691216057308)
                    V("dve", "tensor_tensor", ["yv" + ks, "yu" + ks], ["yR"], out=yR.ap([[1024, 16], [1, 16]], g * 16),
                      in0=yv.ap([[16, 16], [1, 16]]), in1=yu.ap([[16, 16], [1, 16]]), op=ALU.mult)
            A.release()
            P.barrier()
            stop(7)
            A.mark()
            SfT = A.alloc([16, 128])
            for bq in range(8):
                for ri in range(2):
                    pt = PS(ri, [128])
                    V("pe", "transpose", ["Sfin", "ident"], ["ps%d" % ri], out=pt[0:64, :],
                      in_=Sfin.ap([[1, 64]], bq * 128 + ri * 64), identity=ident[:])
                    V("act", "activation", ["ps%d" % ri], ["SfT"], out=SfT[0:64, bq * 2 + ri, :], in_=pt[0:64, :], func=AF.Identity)
            for ri in range(2):
                for d in range(2):
                    P.dma("sp", lambda e, ri=ri, d=d: e.dma_start(
                        out=st_d[ri][:, jj, d, :, :].rearrange("b g n -> g b n"),
                        in_=SfT.ap([[256, 8], [1, 64]], ri * 128 + d * 64, 0, 64)), reads=["SfT"], is_output=True)
            stop(8)
            gw = A.alloc([2, 8, 512], BF16)
            gb = A.alloc([2, 512], BF16)
            yT = [A.alloc([8, 128], BF16) for _ in range(2)]
            sgm = [A.alloc([512]) for _ in range(2)]
            load_ln(l, 0)
            for cbk in range(2):
                for vg in range(2):
                    c0 = vg * D + cbk * 512
                    P.dma("pool", lambda e, vg=vg, c0=c0: e.dma_start(
                        out=gw[:, vg, :, :], in_=gluw_d[jj, :, c0:c0 + 512].rearrange("(t p) n -> p t n", p=128)), writes=["gw"])
                    P.dma("pool", lambda e, vg=vg, c0=c0: e.dma_start(out=gb[0:1, vg, :], in_=glub_d[jj:jj + 1, c0:c0 + 512]),
                          writes=["gb"])
                def glu_tr(s):
                    yt_ = yT[s % 2]; ytk = "yT%d" % (s % 2)
                    ptb = PS(s % 2, [8, 128], BF16)
                    for ct in range(8):
                        P.op("pe", lambda e, ptb=ptb, ct=ct, s=s: e.transpose(ptb[:, ct, :], yR[:, s, ct * 128:(ct + 1) * 128], identb[:]),
                             reads=["yR", "identb"], writes=["ps%d" % (s % 2)], signal=(ct in (0, 7)))
                    V("act", "activation", ["ps%d" % (s % 2)], [ytk], out=yt_[:], in_=ptb[:], func=AF.Identity)

                def glu_mm(s, cbk=cbk):
                    yt_ = yT[s % 2]; ytk = "yT%d" % (s % 2)
                    pv, pg_ = PS(2 + 2 * (s % 2), [512]), PS(3 + 2 * (s % 2), [512])
                    kv, kg = "ps%d" % (2 + 2 * (s % 2)), "ps%d" % (3 + 2 * (s % 2))
                    for (pp, kq, vg) in ((pv, kv, 0), (pg_, kg, 1)):
                        for ct in range(8):
                            P.op("pe", lambda e, pp=pp, ct=ct, vg=vg, yt_=yt_: e.matmul(
                                pp[:], yt_[:, ct, :], gw[:, vg, ct, :], start=(ct == 0), stop=False),
                                 reads=[ytk, "gw"], writes=[kq], signal=(ct == 0))
                        P.op("pe", lambda e, pp=pp, vg=vg: e.matmul(pp[:], onesb[0:1, :], gb[0:1, vg, :], start=False, stop=True),
                             reads=[ytk, "gw", "gb", "onesb"], writes=[kq])
                    sg_ = sgm[s % 2]; sgk = "sgm%d" % (s % 2)
                    V("act", "activation", [kg], [sgk], out=sg_[:], in_=pg_[:], func=AF.Sigmoid)
                    V("dve", "tensor_tensor", [kv, sgk], [sgk], out=sg_[:], in0=pv[:], in1=sg_[:], op=ALU.mult)
                    V("dve", "tensor_tensor", [sgk, "grow0"], [sgk], out=sg_[:], in0=sg_[:], in1=grow[:, 0, cbk * 512:(cbk + 1) * 512], op=ALU.mult)
                    V("dve", "scalar_tensor_tensor", [sgk, "x%d" % s], ["x%d" % s], out=x[:, s, cbk * 512:(cbk + 1) * 512],
                      in0=x[:, s, cbk * 512:(cbk + 1) * 512], scalar=ALPHA, in1=sg_[:], op0=ALU.mult, op1=ALU.add)

                glu_tr(0)
                for s in range(16):
                    if s + 1 < 16:
                        glu_tr(s + 1)
                    glu_mm(s)
            layer_norm_batch(0, 16, pool_assist=True)
            A.release()
            A.release()
            P.barrier()

        ada_mod(0)
        for l in range(depth):
          try:
            mix = cfg.get("mix", "all")
            if l % 2 == 1 and mix in ("all", "pool"):
                mixer_pool(l)
            elif l % 2 == 0 and mix in ("all", "s5"):
                mixer_s5(l)
            else:
                mixer_none(l)
            ffn(l, l + 1 if l + 1 < depth else None)
          except StopBuild:
            P.barrier()
            break

        P.dma("sp", lambda e: e.dma_start(out=y_R, in_=x[:]), reads=["x"] + ["x%d" % s for s in range(16)],
              is_output=True)
        P.finish()
        P.replay()
    return nc


def _pool_consts(sample):
    pm = np.zeros((4, 3, 128, 128), np.float32)
    ic = np.zeros((128, 16, 4), np.float32)
    for g, k in enumerate(WINDOWS):
        for kq in range(128):
            if sample:
                row, cb = divmod(kq, 4)
                rows = range(max(0, row - k // 2), min(32, row - k // 2 + k))
                for a, off in ((0, 0), (1, 1), (2, -1)):
                    cbm = cb + off
                    if 0 <= cbm < 4:
                        for r in rows:
                            pm[g, a, r * 4 + cbm, kq] = 1.0
                for sq in range(16):
                    col = cb * 16 + sq
                    ccnt = min(64, col - k // 2 + k) - max(0, col - k // 2)
                    ic[kq, sq, g] = 1.0 / (len(rows) * ccnt)
            else:
                seq, cpos = divmod(kq, 16)
                for a, off in ((0, 0), (1, 1), (2, -1)):
                    m = cpos + off
                    if 0 <= m < 16:
                        pm[g, a, seq * 16 + m, kq] = 1.0
                for sq in range(16):
                    t = cpos * 16 + sq
                    cnt = min(256, t - k // 2 + k) - max(0, t - k // 2)
                    ic[kq, sq, g] = 1.0 / cnt
    return pm.reshape(12, 128, 128), ic


def _s5_consts():
    expo = np.zeros((128, 34), np.float32)
    expo[:, 0], expo[:, 1] = 1.0, 16.0
    sidx = np.arange(16, dtype=np.float32)
    expo[:64, 2:18] = -(sidx + 1)
    expo[64:, 2:18] = -(16 - sidx)
    expo[:64, 18:34] = sidx + 1
    expo[64:, 18:34] = 16 - sidx
    wm = np.zeros((128, 6, 256), np.float32)
    for p in range(128):
        s8, c = divmod(p, 16)
        for jt in range(2):
            sv = 8 * jt + s8
            for sp in range(16):
                wm[p, 0 * 2 + jt, sp * 16:(sp + 1) * 16] = 1.0 if sp >= sv else 0.0
                wm[p, 1 * 2 + jt, sp * 16:(sp + 1) * 16] = 1.0 if sp <= sv else 0.0
            wm[p, 4 + jt, sv * 16 + c] = 1.0
    return expo, wm


def kernel(**inputs):
    cfg = inputs.pop("_cfg", {})
    f = lambda a: np.ascontiguousarray(np.asarray(a, dtype=np.float32))
    x_prompt, x_sample = f(inputs["x_prompt"]), f(inputs["x_sample"])
    c, c_ctx = f(inputs["c"]), f(inputs["c_ctx"])
    shared = {k: f(inputs[k]) for k in ("ada_w", "ada_b", "ln_g", "ln_b", "ffn_w_gate", "ffn_w_up", "ffn_w_down",
                                         "pool_w", "pool_scale")}
    shared["ident"] = np.eye(128, dtype=np.float32)
    pidx = np.arange(128)
    shared["selT"] = (pidx[:, None] % 16 == pidx[None, :] % 16).astype(np.float32)
    shared["maskG"] = (pidx[:, None] // 16 == np.arange(8)[None, :]).astype(np.float32)
    in_maps = []
    for core in range(NCORES):
        m = dict(shared)
        if core < 4:
            m["x"] = x_sample[core]
            cond = c[core]
        else:
            m["x"] = x_prompt[(core - 4) * 8:(core - 3) * 8].reshape(2048, D)
            cond = c_ctx
        m["cond"] = np.ascontiguousarray(cond.reshape(8, 128).T)
        m["pmat"], m["icnt"] = _pool_consts(core < 4)
        for k in ("ssm_a_re", "ssm_a_im", "ssm_log_step", "ssm_b_re", "ssm_b_im", "ssm_c_re", "ssm_c_im", "ssm_d",
                  "ssm_glu_w", "ssm_glu_b"):
            m[k] = f(inputs[k])
        h0 = np.zeros((2, 128, 2, 64), np.float32)
        mrow = np.ones((128, 128), np.float32)
        if core < 4:
            for ri, key in enumerate(("state_ssm_re", "state_ssm_im")):
                stt = f(inputs[key])[core]
                h0[:, :, ri, :] = stt.transpose(0, 1, 3, 2).reshape(2, 128, 64)
        else:
            mrow[:, ::16] = 0.0
        m["h0"], m["mrow"] = h0, mrow
        m["expo"], m["wmask"] = _s5_consts()
        in_maps.append(m)
    nc = build(cfg)
    if cfg.get("trace"):
        res = run_bass_kernel_spmd(nc, in_maps, core_ids=list(range(NCORES)), trace=True)
        return res
    if cfg.get("cores"):
        in_maps = [in_maps[i] for i in cfg["cores"]]
        res = run_bass_kernel_spmd(nc, in_maps, core_ids=list(range(len(in_maps))))
        return res
    res = run_bass_kernel_spmd(nc, in_maps, core_ids=list(range(NCORES)))
    ys = [r["y"] for r in res.results]
    y_sample = np.stack(ys[:4], 0)
    y_prompt = np.concatenate(ys[4:], 0).reshape(32, 256, D)
    st_re = np.concatenate([r["st_re"] for r in res.results[4:]], 0)
    st_im = np.concatenate([r["st_im"] for r in res.results[4:]], 0)
    return (y_prompt, y_sample, st_re, st_im)
```

```python
import numpy as np
import concourse.bass as bass
import concourse.mybir as mybir
from concourse.bass_utils import run_bass_kernel_spmd

F32 = mybir.dt.float32
BF16 = mybir.dt.bfloat16
AF = mybir.ActivationFunctionType
ALU = mybir.AluOpType

D = 1024
DFF = 2816
NFT = 22
DEPTH = 4
ALPHA = (2 * DEPTH) ** 0.25
EPS = 1e-5
NCORES = 8
WINDOWS = (2, 4, 8, 16)


class Prog:
    ENG = ("pe", "act", "dve", "pool", "sp")

    def __init__(self, nc, n_dma_sems=6):
        self.nc = nc
        self.ops = {e: [] for e in self.ENG}
        self.count = {}
        self.waited = {e: {} for e in self.ENG}
        self.bufs = {}
        self.n_dma = n_dma_sems
        self.dma_rr = {e: 0 for e in self.ENG}
        self.out_tickets = []

    def _deps(self, reads, writes):
        w = {}

        def add(t):
            if t is None:
                return
            k, v = t
            if w.get(k, 0) < v:
                w[k] = v

        for key in reads:
            b = self.bufs.get(key)
            if b:
                add(b["w"])
        for key in writes:
            b = self.bufs.get(key)
            if b:
                add(b["w"])
                for k, v in b["r"].items():
                    add((k, v))
        return w

    def _filter(self, eng, w, skip_self=False):
        out = []
        for k, v in w.items():
            if skip_self and k == eng:
                continue
            if self.waited[eng].get(k, 0) >= v:
                continue
            self.waited[eng][k] = v
            out.append((k, v))
        return out

    def _commit(self, ticket, reads, writes):
        k, v = ticket
        for key in reads:
            b = self.bufs.setdefault(key, {"w": None, "r": {}})
            if b["r"].get(k, 0) < v:
                b["r"][k] = v
        for key in writes:
            self.bufs[key] = {"w": ticket, "r": {}}

    def op(self, eng, fn, reads=(), writes=(), signal=True, extra_waits=()):
        w = self._deps(reads, writes)
        for t in extra_waits:
            if t is not None and w.get(t[0], 0) < t[1]:
                w[t[0]] = t[1]
        waits = self._filter(eng, w, skip_self=(eng == "pe"))
        inc = None
        ticket = None
        if signal:
            self.count[eng] = self.count.get(eng, 0) + 1
            ticket = (eng, self.count[eng])
            inc = (eng, 1)
            self._commit(ticket, reads, writes)
        self.ops[eng].append((waits, fn, inc))
        return ticket

    def dma(self, eng, fn, reads=(), writes=(), is_output=False):
        i = self.dma_rr[eng]
        self.dma_rr[eng] = (i + 1) % self.n_dma
        sk = "dma_%s_%d" % (eng, i)
        prev = self.count.get(sk, 0)
        w = self._deps(reads, writes)
        if prev > 0 and w.get(sk, 0) < prev:
            w[sk] = prev
        waits = self._filter(eng, w)
        self.count[sk] = prev + 16
        ticket = (sk, prev + 16)
        self._commit(ticket, reads, writes)
        self.ops[eng].append((waits, fn, (sk, 16)))
        if is_output:
            self.out_tickets.append(ticket)
        return ticket

    def barrier(self):
        for e in self.ENG:
            w = {k: v for k, v in self.count.items()}
            waits = self._filter(e, w, skip_self=(e == "pe"))
            if waits:
                self.ops[e].append((waits, None, None))

    def finish(self):
        w = {}
        for k, v in self.out_tickets:
            w[k] = max(w.get(k, 0), v)
        waits = self._filter("sp", w)
        if waits:
            self.ops["sp"].append((waits, None, None))

    def replay(self):
        nc = self.nc
        sem_names = sorted(self.count.keys())
        import contextlib

        with contextlib.ExitStack() as st:
            sems = {k: st.enter_context(nc.semaphore("s_" + k)) for k in sem_names}
            block = st.enter_context(nc.Block())
            engs = {"pe": block.tensor, "act": block.scalar, "dve": block.vector,
                    "pool": block.gpsimd, "sp": block.sync}

            def mk(ename):
                ops = self.ops[ename]

                def body(e):
                    for waits, fn, inc in ops:
                        for k, v in waits:
                            e.wait_ge(sems[k], v)
                        if fn is not None:
                            ins = fn(e)
                            if inc is not None:
                                ins.then_inc(sems[inc[0]], inc[1])
                return body

            for ename in self.ENG:
                engs[ename](mk(ename))


class StopBuild(Exception):
    pass


class Buf:
    def __init__(self, tens, width, off, shape, pbase=0):
        self.t = tens
        self.W = width
        self.off = off
        self.shape = tuple(shape)
        st = []
        acc = 1
        for n in reversed(self.shape):
            st.append(acc)
            acc *= n
        self.strides = tuple(reversed(st))
        self.size = acc
        self.pbase = pbase

    def ap(self, dims, off=0, p0=0, np_=128):
        return bass.AP(self.t, (self.pbase + p0) * self.W + self.off + off,
                       [[self.W, np_]] + [list(d) for d in dims])

    def __getitem__(self, idx):
        if not isinstance(idx, tuple):
            idx = (idx,)
        ps = idx[0]
        p0, p1 = (ps.start or 0), (128 if ps.stop is None else ps.stop)
        off = 0
        dims = []
        rest = idx[1:]
        for i, n in enumerate(self.shape):
            s = self.strides[i]
            if i < len(rest):
                r = rest[i]
                if isinstance(r, int):
                    off += r * s
                    continue
                a = r.start or 0
                b = n if r.stop is None else r.stop
                stp = r.step or 1
                off += a * s
                dims.append([s * stp, (b - a + stp - 1) // stp])
            else:
                dims.append([s, n])
        merged = []
        for d in dims:
            if merged and merged[-1][0] == d[0] * d[1]:
                merged[-1] = [d[0], merged[-1][1] * d[1]]
            else:
                merged.append(d)
        if not merged:
            merged = [[1, 1]]
        return self.ap(merged, off, p0, p1 - p0)


class Arena:
    def __init__(self, t32, t16, width32):
        self.t32, self.t16, self.W = t32, t16, width32
        self.top = 0
        self.marks = []
        self.guard = None

    def alloc(self, shape, dt=F32):
        n = int(np.prod(shape))
        if dt == F32:
            b = Buf(self.t32, self.W, self.top, shape)
            self.top += n
        else:
            b = Buf(self.t16, 2 * self.W, 2 * self.top, shape)
            self.top += (n + 1) // 2
        assert self.top <= self.W, ("arena overflow", self.top, self.W)
        assert self.guard is None or self.top <= self.guard, ("arena guard hit", self.top, self.guard)
        return b

    def mark(self):
        self.marks.append(self.top)

    def release(self):
        self.top = self.marks.pop()


def build(cfg):
    nc = bass.Bass("TRN2", target_bir_lowering=False)
    P = Prog(nc)
    depth = cfg.get("depth", DEPTH)
    mixers = cfg.get("mixers", True)

    def din(name, shape):
        return nc.dram_tensor(name, list(shape), F32, kind="ExternalInput").ap()

    x_d = din("x", [2048, D])
    cond_d = din("cond", [128, 8])
    ada_w = din("ada_w", [DEPTH, D, 6 * D])
    ada_b = din("ada_b", [DEPTH, 6 * D])
    ln_g = din("ln_g", [DEPTH, 2, D])
    ln_b = din("ln_b", [DEPTH, 2, D])
    wg_d = din("ffn_w_gate", [DEPTH, D, DFF])
    wu_d = din("ffn_w_up", [DEPTH, D, DFF])
    wd_d = din("ffn_w_down", [DEPTH, DFF, D])
    ident_d = din("ident", [128, 128])
    selT_d = din("selT", [128, 128])
    maskG_d = din("maskG", [128, 8])
    pool_w = din("pool_w", [2, 4, 256, 256])
    pool_scale = din("pool_scale", [2, D])
    pmat_d = din("pmat", [12, 128, 128])
    icnt_d = din("icnt", [128, 16, 4])
    a_re_d = din("ssm_a_re", [2, 2, 64, 64])
    a_im_d = din("ssm_a_im", [2, 2, 64, 64])
    ls_d = din("ssm_log_step", [2, 2, 64])
    b_re_d = din("ssm_b_re", [2, 2, 64, 64, 16])
    b_im_d = din("ssm_b_im", [2, 2, 64, 64, 16])
    c_re_d = din("ssm_c_re", [2, 2, 64, 16, 64])
    c_im_d = din("ssm_c_im", [2, 2, 64, 16, 64])
    dsk_d = din("ssm_d", [2, D])
    gluw_d = din("ssm_glu_w", [2, D, 2 * D])
    glub_d = din("ssm_glu_b", [2, 2 * D])
    h0_d = din("h0", [2, 128, 2, 64])
    mrow_d = din("mrow", [128, 128])
    expo_d = din("expo", [128, 34])
    wmask_d = din("wmask", [128, 6, 256])
    y_d = nc.dram_tensor("y", [2048, D], F32, kind="ExternalOutput").ap()
    st_d = [nc.dram_tensor(nm, [8, 2, 2, 64, 64], F32, kind="ExternalOutput").ap() for nm in ("st_re", "st_im")]
    dbg_d = nc.dram_tensor("dbg", [128, 8192], F32, kind="ExternalOutput").ap() if cfg.get("dbg") else None
    dbg_pos = [0]
    stop_at = cfg.get("s5_stop", None)

    import contextlib
    with contextlib.ExitStack() as es:
        AW = 52900
        a32 = es.enter_context(nc.sbuf_tensor("arena", [128, AW], F32))
        a16 = a32.bitcast(BF16)
        ps32 = es.enter_context(nc.psum_tensor("psum", [128, 4096], F32))
        ps16 = ps32.bitcast(BF16)
        A = Arena(a32, a16, AW)

        def PS(bank, shape, dt=F32, off=0):
            if dt == F32:
                return Buf(ps32, 4096, bank * 512 + off, shape)
            return Buf(ps16, 8192, bank * 1024 + off, shape)

        x = A.alloc([16, D])
        ident = A.alloc([128])
        identb = A.alloc([128], BF16)
        ones = A.alloc([128])
        condc = A.alloc([8])
        scc = A.alloc([8])
        grow = A.alloc([2, D])
        mcol = A.alloc([4, 8])
        LN = {}
        mcolS = A.alloc([2, 64])
        small = A.alloc([64])

        x_R = x_d.rearrange("(k s) c -> k s c", s=16)
        y_R = y_d.rearrange("(k s) c -> k s c", s=16)

        P.dma("sp", lambda e: e.dma_start(out=x[:], in_=x_R), writes=["x"])
        P.dma("sp", lambda e: e.dma_start(out=ident[:], in_=ident_d), writes=["ident"])
        P.dma("sp", lambda e: e.dma_start(out=condc[:], in_=cond_d), writes=["condc"])
        P.op("pool", lambda e: e.memset(ones[:], 1.0), writes=["ones"])
        P.op("dve", lambda e: e.tensor_copy(out=identb[:], in_=ident[:]), reads=["ident"], writes=["identb"])
        P.op("act", lambda e: e.activation(out=scc[:], in_=condc[:], func=AF.Silu), reads=["condc"], writes=["scc"])

        selT = A.alloc([128])
        maskG = A.alloc([8])
        P.dma("sp", lambda e: e.dma_start(out=selT[:], in_=selT_d), writes=["selT"])
        P.dma("sp", lambda e: e.dma_start(out=maskG[:], in_=maskG_d), writes=["maskG"])
        onesb = A.alloc([128], BF16)
        P.op("dve", lambda e: e.tensor_copy(out=onesb[:], in_=ones[:]), reads=["ones"], writes=["onesb"])

        def ada_gen(l, pb0):
            wa = [A.alloc([8, 128], BF16) for _ in range(3)]
            ba = [A.alloc([128], BF16) for _ in range(3)]
            bc = [A.alloc([1]) for _ in range(3)]
            sccb = A.alloc([8, 2], BF16)
            screpb = A.alloc([8, 128], BF16)
            rhsS = [A.alloc([8]) for _ in range(2)]
            grow_n = A.alloc([2, D]); mcol_n = A.alloc([4, 8]); mcolS_n = A.alloc([2, 64])
            AD["bufs"] = (grow_n, mcol_n, mcolS_n)
            P.op("dve", lambda e: e.tensor_copy(out=screpb.ap([[128, 8], [1, 128]]), in_=scc.ap([[1, 8], [0, 128]])),
                 reads=["scc"], writes=["screpb"])
            P.op("dve", lambda e: e.tensor_copy(out=sccb.ap([[2, 8], [1, 2]]), in_=scc.ap([[1, 8], [0, 2]])),
                 reads=["scc"], writes=["sccb"])

            def load(b):
                i3 = b % 3
                v = b // 8
                src = ada_w[l, :, b * 128:(b + 1) * 128].rearrange("(t p) n -> p t n", p=128)
                P.dma("pool", lambda e: e.dma_start(out=wa[i3][:], in_=src), writes=["adaw%d" % i3])
                if v in (2, 5):
                    P.dma("pool", lambda e: e.dma_start(out=ba[i3][0:1, :], in_=ada_b[l:l + 1, b * 128:(b + 1) * 128]),
                          writes=["adab%d" % i3])
                else:
                    P.dma("sp", lambda e: e.dma_start(out=bc[i3][:], in_=ada_b[l, b * 128:(b + 1) * 128].rearrange("(p o) -> p o", o=1)),
                          writes=["adac%d" % i3])

            pending = []
            load(0)
            load(1)
            for b in range(48):
                v, ct = divmod(b, 8)
                i2, i3 = b % 2, b % 3
                if b + 2 < 48:
                    load(b + 2)
                w_ = wa[i3]
                wk = "adaw%d" % i3
                for fn in pending:
                    fn()
                pending = []
                if v in (2, 5):
                    gi = 0 if v == 2 else 1
                    pa = PS(pb0 + i2, [128])
                    pka = "adabk%d" % i2
                    for kt in range(8):
                        P.op("pe", lambda e, kt=kt, w_=w_, pa=pa: e.matmul(pa[:], screpb[:, kt, :], w_[:, kt, :],
                                                                           start=(kt == 0), stop=False),
                             reads=["screpb", wk], writes=[pka], signal=(kt == 0))
                    P.op("pe", lambda e, pa=pa, i3=i3: e.matmul(pa[:], onesb[0:1, :], ba[i3][0:1, :], start=False, stop=True),
                         reads=["onesb", "adab%d" % i3, "screpb", wk], writes=[pka])
                    P.op("act", lambda e, pa=pa, gi=gi, ct=ct: e.activation(
                        out=grow_n[:, gi, ct * 128:(ct + 1) * 128], in_=pa[:], func=AF.Identity),
                         reads=[pka], writes=["grow_n"])
                else:
                    vi = {0: 0, 1: 1, 3: 2, 4: 3}[v]
                    pc = PS(pb0 + i2, [2], off=128)
                    pkc = "adabk%d" % i2
                    for kt in range(8):
                        P.op("pe", lambda e, kt=kt, w_=w_, pc=pc: e.matmul(pc[:], w_[:, kt, :], sccb[:, kt, :],
                                                                           start=(kt == 0), stop=(kt == 7)),
                             reads=["sccb", wk], writes=[pkc], signal=(kt in (0, 7)))
                    addc = 1.0 if vi in (1, 3) else 0.0
                    P.op("dve", lambda e, pc=pc, vi=vi, ct=ct, addc=addc, i3=i3: e.scalar_tensor_tensor(
                        out=mcol_n[:, vi, ct:ct + 1], in0=pc[:, 0:1], scalar=addc, in1=bc[i3][:, 0:1], op0=ALU.add, op1=ALU.add),
                         reads=[pkc, "adac%d" % i3], writes=["mcol_n"])
                    if v in (0, 1):
                        rs = rhsS[ct % 2]
                        rk = "adars%d" % (ct % 2)
                        P.op("dve", lambda e, rs=rs, vi=vi, ct=ct: e.tensor_scalar(
                            out=rs[:], in0=maskG[:], scalar1=mcol_n[:, vi, ct:ct + 1], scalar2=None, op0=ALU.mult),
                             reads=["mcol_n", "maskG"], writes=[rk])

                        def later(rs=rs, rk=rk, v=v, ct=ct, i2=i2):
                            p3 = PS(pb0 + i2, [8], off=256)
                            pk3 = "adabk%d" % i2
                            P.op("pe", lambda e: e.matmul(p3[:], selT[:], rs[:], start=True, stop=True),
                                 reads=[rk, "selT"], writes=[pk3])
                            P.op("dve", lambda e: e.tensor_copy(out=mcolS_n[:, v, ct * 8:(ct + 1) * 8], in_=p3[:]),
                                 reads=[pk3], writes=["mcolS_n"])
                        pending.append(later)
                yield
            for fn in pending:
                fn()

        AD = {}

        def ada_commit():
            grow_n, mcol_n, mcolS_n = AD["bufs"]
            P.op("act", lambda e: e.activation(out=grow[:], in_=grow_n[:], func=AF.Identity),
                 reads=["grow_n"], writes=["grow0", "grow1"])
            P.op("dve", lambda e: e.tensor_copy(out=mcol[:], in_=mcol_n[:]), reads=["mcol_n"], writes=["mcol"])
            P.op("dve", lambda e: e.tensor_copy(out=mcolS[:], in_=mcolS_n[:]), reads=["mcolS_n"], writes=["mcolS"])

        ADA_HI = 4800

        def ada_mod(l):
            saved = A.top
            A.top = A.W - ADA_HI
            assert A.top >= saved
            A.mark()
            for _ in ada_gen(l, 6):
                pass
            ada_commit()
            A.release()
            A.top = saved

        def load_ln(l, which):
            LN["row"] = A.alloc([2, D])
            lnrow = LN["row"]
            for i, src in enumerate((ln_g, ln_b)):
                P.dma("sp", lambda e, i=i, src=src: e.dma_start(
                    out=lnrow[:, i, :], in_=src[l, which:which + 1, :].partition_broadcast(128)), writes=["lnrow%d" % i])

        lnst = A.alloc([16, 2, 6])
        lnmv = A.alloc([16, 2])
        lnr = A.alloc([3, 16])

        def layer_norm_batch(s0, n, pool_assist=False):
            lnrow = LN["row"]
            xk = ["x%d" % s for s in range(s0, s0 + n)]
            for s in range(s0, s0 + n):
                for h in range(2):
                    P.op("dve", lambda e, s=s, h=h: e.bn_stats(out=lnst[:, s, h, :], in_=x[:, s, h * 512:(h + 1) * 512]),
                         reads=["x%d" % s], writes=["lnst%d_%d" % (s, h)])
                P.op("dve", lambda e, s=s: e.bn_aggr(out=lnmv[:, s, :], in_=lnst[:, s, :, :]),
                     reads=["lnst%d_0" % s, "lnst%d_1" % s], writes=["lnmv"])
            var = lnmv.ap([[2, n]], 2 * s0 + 1)
            mean = lnmv.ap([[2, n]], 2 * s0)
            P.op("act", lambda e: e.activation(out=lnr[:, 0, s0:s0 + n], in_=var, func=AF.Sqrt, bias=epsc[:, 0:1], scale=1.0),
                 reads=["lnmv", "epsc"], writes=["lnr0"])
            P.op("dve", lambda e: e.reciprocal(out=lnr[:, 1, s0:s0 + n], in_=lnr[:, 0, s0:s0 + n]), reads=["lnr0"], writes=["lnr1"])
            P.op("dve", lambda e: e.scalar_tensor_tensor(out=lnr[:, 2, s0:s0 + n], in0=mean, scalar=-1.0, in1=lnr[:, 1, s0:s0 + n],
                                                          op0=ALU.mult, op1=ALU.mult), reads=["lnmv", "lnr1"], writes=["lnr2"])
            for s in range(s0, s0 + n):
                P.op("act", lambda e, s=s: e.activation(out=x[:, s, :], in_=x[:, s, :], func=AF.Identity,
                                                        scale=lnr[:, 1, s:s + 1], bias=lnr[:, 2, s:s + 1]),
                     reads=["x%d" % s, "lnr1", "lnr2"], writes=["x%d" % s])
            if pool_assist:
                nd = (2 * n + 2) // 3
                parts = (("dve", s0, nd), ("pool", s0 + nd, n - nd))
            else:
                h2 = n // 2
                parts = (("dve", s0, h2), ("dve", s0 + h2, n - h2))
            for (eng, a, m) in parts:
                ks = ["x%d" % s for s in range(a, a + m)]
                P.op(eng, lambda e, a=a, m=m: e.tensor_tensor(out=x[:, a:a + m, :], in0=x[:, a:a + m, :],
                                                              in1=lnrow.ap([[0, m], [1, D]], 0), op=ALU.mult),
                     reads=ks + ["lnrow0"], writes=ks)
                P.op(eng, lambda e, a=a, m=m: e.tensor_tensor(out=x[:, a:a + m, :], in0=x[:, a:a + m, :],
                                                              in1=lnrow.ap([[0, m], [1, D]], D), op=ALU.add),
                     reads=ks + ["lnrow1"], writes=ks + ["x"])

        epsc = A.alloc([1])
        P.op("pool", lambda e: e.memset(epsc[:], EPS), writes=["epsc"])

        def ffn(l, nxt=None):
            A.mark()
            agen = ada_gen(nxt, 0) if nxt is not None else None

            def ada_step(n=1):
                if agen is None:
                    return
                for _ in range(n):
                    try:
                        next(agen)
                    except StopIteration:
                        return
            aT = A.alloc([8, 1024], BF16)
            hid = A.alloc([NFT, 1024], BF16)
            wgb = [A.alloc([8, 128], BF16) for _ in range(2)]
            wub = [A.alloc([8, 128], BF16) for _ in range(2)]
            wdb = [A.alloc([NFT, 256], BF16) for _ in range(2)]
            sg = [A.alloc([512]) for _ in range(2)]
            load_ln(l, 1)
            nwd = 0
            for half in range(2):
                for ct in range(8):
                    for sq in range(2):
                        pb = 2 + (ct * 2 + sq) % 2
                        pt = PS(pb, [4, 128])
                        for q in range(4):
                            s = half * 8 + sq * 4 + q
                            P.op("pe", lambda e, pt=pt, q=q, s=s, ct=ct: e.transpose(
                                pt[:, q, :], x[:, s, ct * 128:(ct + 1) * 128], ident[:]),
                                 reads=["x%d" % s, "ident"] if q < 3 else ["x%d" % (s - i) for i in range(4)] + ["ident"],
                                 writes=["ps%d" % pb] if q in (0, 3) else [], signal=(q in (0, 3)))
                        P.op("act", lambda e, pt=pt, ct=ct, sq=sq: e.activation(
                            out=aT[:, ct, sq * 512:(sq + 1) * 512], in_=pt[:], func=AF.Identity,
                            scale=mcol[:, 3, ct:ct + 1], bias=mcol[:, 2, ct:ct + 1]),
                             reads=["ps%d" % pb, "mcol"], writes=["aT%d" % sq])
                for ft in range(NFT):
                    ada_step(1)
                    wb_i = ft % 2
                    for (dst, srcw, nm) in ((wgb[wb_i], wg_d, "wg"), (wub[wb_i], wu_d, "wu")):
                        src = srcw[l, :, ft * 128:(ft + 1) * 128].rearrange("(t p) n -> p t n", p=128)
                        P.dma("pool", lambda e, dst=dst, src=src: e.dma_start(out=dst[:], in_=src),
                              writes=["%s%d" % (nm, wb_i)])
                    for blk in range(2):
                        pg, pu = PS(2 + blk * 2, [512]), PS(3 + blk * 2, [512])
                        kg, ku = "ps%d" % (2 + blk * 2), "ps%d" % (3 + blk * 2)
                        for (pp, kk_, wbuf, nm) in ((pg, kg, wgb[wb_i], "wg"), (pu, ku, wub[wb_i], "wu")):
                            for ct in range(8):
                                P.op("pe", lambda e, pp=pp, wbuf=wbuf, ct=ct, blk=blk: e.matmul(
                                    pp[:], wbuf[:, ct, :], aT[:, ct, blk * 512:(blk + 1) * 512],
                                    start=(ct == 0), stop=(ct == 7)),
                                     reads=["%s%d" % (nm, wb_i), "aT%d" % blk], writes=[kk_],
                                     signal=(ct in (0, 7)))
                        sgt = sg[blk]
                        P.op("act", lambda e, sgt=sgt, pg=pg: e.activation(out=sgt[:], in_=pg[:], func=AF.Silu),
                             reads=[kg], writes=["sg%d" % blk])
                        P.op("dve", lambda e, sgt=sgt, pu=pu, ft=ft, blk=blk: e.tensor_tensor(
                            out=hid[:, ft, blk * 512:(blk + 1) * 512], in0=pu[:], in1=sgt[:], op=ALU.mult),
                             reads=[ku, "sg%d" % blk], writes=["hid"])
                for cq in range(4):
                    ada_step(1)
                    wb_i = nwd % 2
                    nwd += 1
                    wdt = wdb[wb_i]
                    src = wd_d[l, :, cq * 256:(cq + 1) * 256].rearrange("(t p) n -> p t n", p=128)
                    P.dma("pool", lambda e, wdt=wdt, src=src: e.dma_start(out=wdt[:], in_=src), writes=["wd%d" % wb_i])
                    for s8 in range(8):
                        s = half * 8 + s8
                        pb = 6 + (s8 % 2)
                        po = PS(pb, [256])
                        for ft in range(NFT):
                            P.op("pe", lambda e, po=po, ft=ft, s8=s8, wdt=wdt: e.matmul(
                                po[:], hid[:, ft, s8 * 128:(s8 + 1) * 128], wdt[:, ft, :],
                                start=(ft == 0), stop=(ft == NFT - 1)),
                                 reads=["hid", "wd%d" % wb_i], writes=["ps%d" % pb], signal=(ft in (0, NFT - 1)))
                        tq = sg[s8 % 2]
                        P.op("dve", lambda e, tq=tq, po=po, cq=cq: e.tensor_tensor(
                            out=tq[:, 0:256], in0=po[:], in1=grow[:, 1, cq * 256:(cq + 1) * 256], op=ALU.mult),
                             reads=["ps%d" % pb, "grow1"], writes=["sg%d" % (s8 % 2)])
                        P.op("dve", lambda e, tq=tq, s=s, cq=cq: e.scalar_tensor_tensor(
                            out=x[:, s, cq * 256:(cq + 1) * 256], in0=x[:, s, cq * 256:(cq + 1) * 256], scalar=ALPHA,
                            in1=tq[:, 0:256], op0=ALU.mult, op1=ALU.add),
                             reads=["sg%d" % (s8 % 2), "x%d" % s], writes=["x%d" % s])
                if half == 1 and agen is not None:
                    ada_step(100)
                    ada_commit()
                layer_norm_batch(half * 8, 8)
            A.release()
            P.barrier()

        def mixer_none(l):
            A.mark()
            load_ln(l, 0)
            for s in range(16):
                P.op("dve", lambda e, s=s: e.tensor_scalar(out=x[:, s, :], in0=x[:, s, :], scalar1=ALPHA, scalar2=None,
                                                           op0=ALU.mult), reads=["x%d" % s], writes=["x%d" % s])
            layer_norm_batch(0, 16)
            A.release()
            P.barrier()

        def mixer_pool(l):
            j = l // 2
            A.mark()
            hT = [A.alloc([8, 128], BF16) for _ in range(2)]
            zb = A.alloc([16, D], BF16)
            pw = A.alloc([4, 2, 256], BF16)
            pm = A.alloc([12, 128], BF16)
            ic = A.alloc([16, 4])
            psrow = A.alloc([D])
            tq = [A.alloc([256]) for _ in range(2)]
            load_ln(l, 0)
            P.dma("pool", lambda e: e.dma_start(out=pw[:], in_=pool_w[j].rearrange("g (t p) n -> p g t n", p=128)),
                  writes=["pw"])
            P.dma("pool", lambda e: e.dma_start(out=pm[:], in_=pmat_d.rearrange("a k m -> k a m")), writes=["pm"])
            P.dma("sp", lambda e: e.dma_start(out=ic[:], in_=icnt_d), writes=["ic"])
            P.dma("sp", lambda e: e.dma_start(out=psrow[:], in_=pool_scale[j:j + 1, :].partition_broadcast(128)),
                  writes=["psrow"])
            P.op("dve", lambda e: e.tensor_tensor(out=psrow[:], in0=psrow[:], in1=grow[:, 0, :], op=ALU.mult),
                 reads=["psrow", "grow0"], writes=["psrow"])
            def pool_tr(s):
                ht = hT[s % 2]
                hk = "hT%d" % (s % 2)
                for ct in range(8):
                    pb = ct % 2
                    pt = PS(pb, [128])
                    P.op("pe", lambda e, pt=pt, s=s, ct=ct: e.transpose(pt[:], x[:, s, ct * 128:(ct + 1) * 128], ident[:]),
                         reads=["x%d" % s, "ident"], writes=["ps%d" % pb])
                    P.op("act", lambda e, pt=pt, ht=ht, ct=ct: e.activation(
                        out=ht[:, ct, :], in_=pt[:], func=AF.Identity, scale=mcol[:, 1, ct:ct + 1],
                        bias=mcol[:, 0, ct:ct + 1]), reads=["ps%d" % pb, "mcol"], writes=[hk])

            def pool_zm(s):
                ht = hT[s % 2]
                hk = "hT%d" % (s % 2)
                for g in range(4):
                    pb = 2 + g % 2
                    pz = PS(pb, [256])
                    for kt in range(2):
                        P.op("pe", lambda e, pz=pz, ht=ht, g=g, kt=kt: e.matmul(
                            pz[:], ht[:, 2 * g + kt, :], pw[:, g, kt, :], start=(kt == 0), stop=(kt == 1)),
                             reads=[hk, "pw"], writes=["ps%d" % pb])
                    P.op("act", lambda e, pz=pz, s=s, g=g: e.activation(out=zb[:, s, g * 256:(g + 1) * 256], in_=pz[:],
                                                                     func=AF.Identity), reads=["ps%d" % pb], writes=["zb"])

            pool_tr(0)
            for s in range(16):
                if s + 1 < 16:
                    pool_tr(s + 1)
                pool_zm(s)
            n = 0
            for sp_ in range(16):
                for g, k in enumerate(WINDOWS):
                    pb = 4 + n % 2
                    tqq = tq[n % 2]
                    tqk = "tq%d" % (n % 2)
                    n += 1
                    pp = PS(pb, [256])
                    terms = []
                    for dlt in range(-(k // 2), k - k // 2):
                        sidx = sp_ + dlt
                        a = 0
                        if sidx < 0:
                            sidx, a = sidx + 16, 2
                        elif sidx >= 16:
                            sidx, a = sidx - 16, 1
                        terms.append((sidx, a))
                    for i, (sidx, a) in enumerate(terms):
                        P.op("pe", lambda e, pp=pp, g=g, a=a, sidx=sidx, i=i, nt=len(terms): e.matmul(
                            pp[:], pm[:, g * 3 + a, :], zb[:, sidx, g * 256:(g + 1) * 256], start=(i == 0), stop=(i == nt - 1)),
                             reads=["zb", "pm"], writes=["ps%d" % pb], signal=(i in (0, len(terms) - 1)))
                    P.op("dve", lambda e, pp=pp, tqq=tqq, sp_=sp_, g=g: e.scalar_tensor_tensor(
                        out=tqq[:], in0=pp[:], scalar=ic[:, sp_, g:g + 1], in1=zb[:, sp_, g * 256:(g + 1) * 256],
                        op0=ALU.mult, op1=ALU.subtract), reads=["ps%d" % pb, "ic", "zb"], writes=[tqk])
                    P.op("dve", lambda e, tqq=tqq, g=g: e.tensor_tensor(out=tqq[:], in0=tqq[:], in1=psrow[:, g * 256:(g + 1) * 256],
                                                                    op=ALU.mult), reads=[tqk, "psrow"], writes=[tqk])
                    P.op("dve", lambda e, tqq=tqq, sp_=sp_, g=g: e.scalar_tensor_tensor(
                        out=x[:, sp_, g * 256:(g + 1) * 256], in0=x[:, sp_, g * 256:(g + 1) * 256], scalar=ALPHA, in1=tqq[:],
                        op0=ALU.mult, op1=ALU.add), reads=[tqk, "x%d" % sp_], writes=["x%d" % sp_])
            layer_norm_batch(0, 16, pool_assist=True)
            A.release()
            P.barrier()

        I32 = mybir.dt.int32
        a32i = a32.bitcast(I32)

        def V(eng, method, reads, writes, **kw):
            return P.op(eng, lambda e: getattr(e, method)(**kw), reads=reads, writes=writes)

        def dump(name, ap_, n, reads):
            if dbg_d is None:
                return
            c0 = dbg_pos[0]
            dbg_pos[0] += n
            print("DBG", name, c0, n)
            P.dma("sp", lambda e: e.dma_start(out=dbg_d[:, c0:c0 + n], in_=ap_), reads=reads, is_output=True)

        def stop(k):
            if stop_at == k:
                raise StopBuild()

        def sincos(ang, n, cosb, sinb, key):
            r = ang; m = A.alloc([n]); rf = m
            ri = Buf(a32i, AW, A.top, [n]); A.top += n
            k = key
            V("dve", "tensor_scalar", [k + "ang"], [k + "r"], out=r[:], in0=ang[:], scalar1=1.0 / (2 * np.pi), scalar2=None, op0=ALU.mult)
            V("dve", "tensor_copy", [k + "r"], [k + "ri"], out=ri[:], in_=r[:])
            V("dve", "tensor_copy", [k + "ri"], [k + "m"], out=rf[:], in_=ri[:])
            V("dve", "tensor_tensor", [k + "r", k + "m"], [k + "r"], out=r[:], in0=r[:], in1=rf[:], op=ALU.subtract)
            for (thr, op, sign) in ((0.5, ALU.is_gt, ALU.subtract), (-0.5, ALU.is_lt, ALU.add)):
                V("dve", "tensor_scalar", [k + "r"], [k + "m"], out=m[:], in0=r[:], scalar1=thr, scalar2=None, op0=op)
                V("dve", "tensor_tensor", [k + "r", k + "m"], [k + "r"], out=r[:], in0=r[:], in1=m[:], op=sign)
            V("act", "activation", [k + "r"], [k + "sin"], out=sinb[:], in_=r[:], func=AF.Sin, scale=6.283185)
            V("dve", "tensor_scalar", [k + "r"], [k + "r"], out=r[:], in0=r[:], scalar1=0.25, scalar2=None, op0=ALU.add)
            V("dve", "tensor_scalar", [k + "r"], [k + "m"], out=m[:], in0=r[:], scalar1=0.5, scalar2=None, op0=ALU.is_gt)
            V("dve", "tensor_tensor", [k + "r", k + "m"], [k + "r"], out=r[:], in0=r[:], in1=m[:], op=ALU.subtract)
            V("act", "activation", [k + "r"], [k + "cos"], out=cosb[:], in_=r[:], func=AF.Sin, scale=6.283185)

        def mixer_s5(l):
            jj = l // 2
            if l == 0:
                A.guard = A.W - ADA_HI
            GQ = 4
            A.mark()
            th = A.alloc([64]); adt = A.alloc([64]); are = A.alloc([64]); aim = A.alloc([64]); dtb = A.alloc([64])
            cre = A.alloc([64]); cim = A.alloc([64])
            expo = A.alloc([34])
            TRE = A.alloc([64 * 34]); TIM = A.alloc([64 * 34])
            BB = [A.alloc([64 * 16]) for _ in range(2)]
            CT = [A.alloc([64 * 16]) for _ in range(2)]
            Dcol = A.alloc([64])
            wmask = A.alloc([6, 256])
            mrow = A.alloc([128])
            Sfin = A.alloc([8 * 2 * 64])
            nat = A.alloc([3, 128])
            for i_, (dst, src, nm) in enumerate(((are, a_re_d, "are"), (aim, a_im_d, "aim"))):
                P.dma("sp", lambda e, src=src, i_=i_: e.dma_start(
                    out=nat.ap([[64, 2], [1, 64]], i_ * 128, 0, 64), in_=src[jj].rearrange("d g n -> g d n")), writes=["nat%d" % i_])
                pt = PS(i_, [128])
                V("pe", "transpose", ["nat%d" % i_, "ident"], ["ps%d" % i_], out=pt[:, 0:64], in_=nat.ap([[1, 128]], i_ * 128, 0, 64),
                  identity=ident[0:64, 0:64])
                V("act", "activation", ["ps%d" % i_], [nm], out=dst[:], in_=pt[:, 0:64], func=AF.Identity)
            for d in range(2):
                P.dma("sp", lambda e, d=d: e.dma_start(out=dtb[64 * d:64 * d + 64, :],
                                                       in_=ls_d[jj, d:d + 1, :].partition_broadcast(64)), writes=["dtb"])
            P.dma("sp", lambda e: e.dma_start(out=expo[:], in_=expo_d), writes=["expo"])
            P.dma("sp", lambda e: e.dma_start(out=wmask[:], in_=wmask_d), writes=["wmask"])
            P.dma("sp", lambda e: e.dma_start(out=mrow[:], in_=mrow_d), writes=["mrow"])
            P.dma("sp", lambda e: e.dma_start(out=nat.ap([[1, 16]], 256, 0, 64), in_=dsk_d[jj].rearrange("(g c) -> g c", c=16)),
                  writes=["nat2"])
            V("dve", "tensor_copy", ["nat2", "nat0"], ["nat0"], out=nat.ap([[16, 8], [1, 16]], 0, 0, 64),
              in_=nat.ap([[0, 8], [1, 16]], 256, 0, 64))
            ptd = PS(2, [128])
            V("pe", "transpose", ["nat0", "ident"], ["ps2"], out=ptd[:, 0:64], in_=nat.ap([[1, 128]], 0, 0, 64), identity=ident[0:64, 0:64])
            V("act", "activation", ["ps2"], ["Dcol"], out=Dcol[:], in_=ptd[:, 0:64], func=AF.Identity)
            A.mark()
            Braw = [A.alloc([64 * 16]) for _ in range(2)]
            CN = [A.alloc([8, 128]) for _ in range(2)]
            for ri, (bd, cd) in enumerate(((b_re_d, c_re_d), (b_im_d, c_im_d))):
                for d in range(2):
                    P.dma("act", lambda e, ri=ri, d=d, bd=bd: e.dma_start(
                        out=Braw[ri].ap([[16, 64], [1, 16]], 0, 64 * d, 64), in_=bd[jj, d].rearrange("g n c -> n g c")),
                          writes=["Braw%d" % ri])
                    P.dma("sp", lambda e, ri=ri, d=d, cd=cd: e.dma_start(
                        out=CN[ri].ap([[128, 8], [1, 64]], 64 * d), in_=cd[jj, d].rearrange("(gt g8) o n -> (g8 o) gt n", g8=8)),
                          writes=["CN%d" % ri])
            V("act", "activation", ["dtb"], ["dtb"], out=dtb[:], in_=dtb[:], func=AF.Exp)
            V("dve", "tensor_tensor", ["are", "dtb"], ["adt"], out=adt[:], in0=are[:], in1=dtb[:], op=ALU.mult)
            V("dve", "tensor_tensor", ["aim", "dtb"], ["th"], out=th[:], in0=aim[:], in1=dtb[:], op=ALU.mult)
            NT = 64 * 34
            ANG = A.alloc([NT]); MAG = A.alloc([NT]); COS = TRE; SIN = TIM
            full = [[34, 64], [1, 34]]
            V("dve", "tensor_tensor", ["th", "expo"], ["s5ang"], out=ANG.ap(full), in0=th.ap([[1, 64], [0, 34]]),
              in1=expo.ap([[0, 64], [1, 34]]), op=ALU.mult)
            V("dve", "tensor_tensor", ["adt", "expo"], ["MAG"], out=MAG.ap(full), in0=adt.ap([[1, 64], [0, 34]]),
              in1=expo.ap([[0, 64], [1, 34]]), op=ALU.mult)
            V("act", "activation", ["MAG"], ["MAG"], out=MAG[:], in_=MAG[:], func=AF.Exp)
            sincos(ANG, NT, COS, SIN, "s5")
            V("dve", "tensor_tensor", ["MAG", "s5cos"], ["TRE", "s5cos"], out=TRE[:], in0=MAG[:], in1=COS[:], op=ALU.mult)
            V("dve", "tensor_tensor", ["MAG", "s5sin"], ["TIM", "s5sin"], out=TIM[:], in0=MAG[:], in1=SIN[:], op=ALU.mult)
            stop(1)
            t1 = A.alloc([64]); t2 = A.alloc([64]); nr = A.alloc([64]); rden = A.alloc([64])
            lre, lim = TRE.ap([[34, 64]], 0), TIM.ap([[34, 64]], 0)
            V("dve", "tensor_scalar", ["TRE"], ["nr"], out=nr[:], in0=lre, scalar1=-1.0, scalar2=None, op0=ALU.add)
            V("dve", "tensor_tensor", ["are"], ["t1"], out=t1[:], in0=are[:], in1=are[:], op=ALU.mult)
            V("dve", "tensor_tensor", ["aim"], ["t2"], out=t2[:], in0=aim[:], in1=aim[:], op=ALU.mult)
            V("dve", "tensor_tensor", ["t1", "t2"], ["t1"], out=t1[:], in0=t1[:], in1=t2[:], op=ALU.add)
            V("dve", "reciprocal", ["t1"], ["rden"], out=rden[:], in_=t1[:])
            V("dve", "tensor_tensor", ["nr", "are"], ["t1"], out=t1[:], in0=nr[:], in1=are[:], op=ALU.mult)
            V("dve", "tensor_tensor", ["TIM", "aim"], ["t2"], out=t2[:], in0=lim, in1=aim[:], op=ALU.mult)
            V("dve", "tensor_tensor", ["t1", "t2"], ["t1"], out=t1[:], in0=t1[:], in1=t2[:], op=ALU.add)
            V("dve", "tensor_tensor", ["t1", "rden"], ["cre"], out=cre[:], in0=t1[:], in1=rden[:], op=ALU.mult)
            V("dve", "tensor_tensor", ["TIM", "are"], ["t1"], out=t1[:], in0=lim, in1=are[:], op=ALU.mult)
            V("dve", "tensor_tensor", ["nr", "aim"], ["t2"], out=t2[:], in0=nr[:], in1=aim[:], op=ALU.mult)
            V("dve", "tensor_tensor", ["t1", "t2"], ["t1"], out=t1[:], in0=t1[:], in1=t2[:], op=ALU.subtract)
            V("dve", "tensor_tensor", ["t1", "rden"], ["cim"], out=cim[:], in0=t1[:], in1=rden[:], op=ALU.mult)
            u1 = A.alloc([1024]); u2 = A.alloc([1024])
            f3 = [[16, 64], [1, 16]]
            cb = lambda c: c.ap([[1, 64], [0, 16]])
            for (o, (ca, xa), (cb_, xb), op) in ((0, (cre, 0), (cim, 1), ALU.subtract), (1, (cre, 1), (cim, 0), ALU.add)):
                V("dve", "tensor_tensor", ["cre", "cim", "Braw0", "Braw1"], ["u1"], out=u1.ap(f3), in0=cb(ca), in1=Braw[xa].ap(f3), op=ALU.mult)
                V("dve", "tensor_tensor", ["cre", "cim", "Braw0", "Braw1"], ["u2"], out=u2.ap(f3), in0=cb(cb_), in1=Braw[xb].ap(f3), op=ALU.mult)
                V("dve", "tensor_tensor", ["u1", "u2"], ["BB%d" % o], out=BB[o][:], in0=u1[:], in1=u2[:], op=op)
            stop(2)
            for ri in range(2):
                for gt in range(8):
                    pt = PS(gt % 2, [128])
                    pk = "ps%d" % (gt % 2)
                    V("pe", "transpose", ["CN%d" % ri, "ident"], [pk], out=pt[:], in_=CN[ri][:, gt, :], identity=ident[:])
                    V("act", "activation", [pk], ["CT%d" % ri], out=CT[ri][:, gt * 128:(gt + 1) * 128], in_=pt[:], func=AF.Identity,
                      scale=(-1.0 if ri == 1 else 1.0))
            A.release()
            P.barrier()
            A.guard = None
            yR = A.alloc([16, D], BF16)
            A.mark()
            GQ = 16
            SB = 1
            RQ = [A.alloc([GQ, 256], BF16) for _ in range(2)]
            Bbuf = A.alloc([128, 2, GQ], BF16)
            Pbuf = A.alloc([2, GQ, 128], BF16)
            G_, W2, W4 = GQ, 2 * GQ, 4 * GQ
            A.mark()
            LQ2 = [[A.alloc([SB, 256], BF16) for _ in range(2)] for _ in range(3)]
            g1 = A.alloc([SB * 256]); g2 = A.alloc([SB * 256])
            h1 = A.alloc([SB * 256]); h2 = A.alloc([SB * 256])
            XT = [A.alloc([2, 128], BF16) for _ in range(3)]
            Wg = [A.alloc([4, 256], BF16) for _ in range(2)]
            LTg = [A.alloc([4, 128], BF16) for _ in range(2)]
            xstg = A.alloc([2, 128])
            topA = A.top
            A.release()
            A.mark()
            UU = [A.alloc([8, W2]) for _ in range(3)]
            T2 = A.alloc([8, W2]); prod = A.alloc([8, W4]); C4 = A.alloc([W4]); C256 = A.alloc([W4])
            VV = [A.alloc([W2]) for _ in range(3)]
            Pin = A.alloc([8, W2]); prodb = A.alloc([W4]); tb = A.alloc([W2])
            Zc = [A.alloc([W2]) for _ in range(2)]; prodz = A.alloc([W4])
            ZC = A.alloc([16, W4])
            corr = [A.alloc([8, W2]) for _ in range(2)]; prodj = [A.alloc([8, W4]) for _ in range(2)]
            topS = A.top
            A.release()
            A.mark()
            ytl = [A.alloc([256]) for _ in range(2)]; yul = [A.alloc([256]) for _ in range(2)]; yvl = [A.alloc([256]) for _ in range(2)]
            topC = A.top
            A.release()
            A.top = max(topA, topS, topC)
            assert A.top <= A.W, ("arena overflow (s5 scratch)", A.top)
            o3 = [[256, SB], [16, 16], [1, 16]]

            def gen(eng, which, g0b, dst, t1_, t2_, kpre, doff=0, ksuf=""):
                col, X = (2, BB) if which == "L" else (18, CT)
                xk = ["BB0", "BB1"] if which == "L" else ["CT0", "CT1"]
                tap = lambda T: T.ap([[34, SB], [1, 16], [0, 16]], g0b * 34 + col)
                xap = lambda Xb: Xb.ap([[16, SB], [0, 16], [1, 16]], g0b * 16)
                k1, k2 = kpre + "1", kpre + "2"
                d0 = dst[0].ap([[1, SB * 256]], doff * 256)
                d1 = dst[1].ap([[1, SB * 256]], doff * 256)
                V(eng, "tensor_tensor", ["TRE"] + xk, [k1], out=t1_.ap(o3), in0=tap(TRE), in1=xap(X[0]), op=ALU.mult)
                V(eng, "tensor_tensor", ["TIM"] + xk, [k2], out=t2_.ap(o3), in0=tap(TIM), in1=xap(X[1]), op=ALU.mult)
                V(eng, "tensor_tensor", [k1, k2], [which + "0" + ksuf], out=d0, in0=t1_[:], in1=t2_[:],
                  op=(ALU.subtract if which == "L" else ALU.add))
                V(eng, "tensor_tensor", ["TRE"] + xk, [k1], out=t1_.ap(o3), in0=tap(TRE), in1=xap(X[1]), op=ALU.mult)
                V(eng, "tensor_tensor", ["TIM"] + xk, [k2], out=t2_.ap(o3), in0=tap(TIM), in1=xap(X[0]), op=ALU.mult)
                V(eng, "tensor_tensor", [k1, k2], [which + "1" + ksuf], out=d1, in0=t1_[:], in1=t2_[:],
                  op=(ALU.add if which == "L" else ALU.subtract))

            for Q in range(64 // GQ):
                g0 = Q * GQ
                stop(3)
                P.barrier()
                def stage0(gl):
                    g = g0 + gl
                    LQ = LQ2[gl % 3]
                    lsuf = "_%d" % (gl % 3)
                    gen("dve", "L", g, LQ, g1, g2, "g", ksuf=lsuf)
                    if False:
                        gen("dve", "R", g, RQ, g1, g2, "g", doff=gl, ksuf="_%d" % gl)
                    else:
                        gen("pool", "R", g, RQ, h1, h2, "h", doff=gl, ksuf="_%d" % gl)

                def stage1(gl):
                    g = g0 + gl
                    LQ = LQ2[gl % 3]
                    lsuf = "_%d" % (gl % 3)
                    lt = LTg[gl % 2]; ltk = "LT%d" % (gl % 2)
                    xt = XT[gl % 3]; xtk = "XT%d" % (gl % 3)
                    p2 = PS(2, [4, 128], BF16)
                    for ri in range(2):
                        for jt in range(2):
                            V("pe", "transpose", ["L%d" % ri + lsuf, "identb"], ["ps2"], out=p2[:, ri * 2 + jt, :],
                              in_=LQ[ri][:, 0, jt * 128:(jt + 1) * 128], identity=identb[:])
                    V("act", "activation", ["ps2"], [ltk], out=lt[:], in_=p2[:], func=AF.Identity)
                    p3 = PS(3, [2, 128])
                    for jt in range(2):
                        V("pool", "tensor_copy", ["x"] + ["x%d" % (8 * jt + i) for i in range(8)], ["xstg%d" % jt],
                          out=xstg.ap([[16, 8], [1, 16]], jt * 128), in_=x.ap([[1024, 8], [1, 16]], 8 * jt * 1024 + g * 16))
                        V("pe", "transpose", ["xstg%d" % jt, "ident"], ["ps3"], out=p3[:, jt, :], in_=xstg[:, jt, :], identity=ident[:])
                    V("act", "activation", ["ps3", "mcolS"], [xtk], out=xt[:], in_=p3[:], func=AF.Identity,
                      scale=mcolS[:, 1, g:g + 1], bias=mcolS[:, 0, g:g + 1])

                def stage2(gl):
                    g = g0 + gl
                    LQ = LQ2[gl % 3]
                    lsuf = "_%d" % (gl % 3)
                    lt = LTg[gl % 2]; ltk = "LT%d" % (gl % 2)
                    xt = XT[gl % 3]; xtk = "XT%d" % (gl % 3)
                    p4 = PS(4, [2, 128])
                    for ri in range(2):
                        for jt in range(2):
                            P.op("pe", lambda e, p4=p4, ri=ri, jt=jt, lt=lt, xt=xt: e.matmul(
                                p4[:, ri, :], lt[:, ri * 2 + jt, :], xt[:, jt, :], start=(jt == 0), stop=(jt == 1)),
                                 reads=[ltk, xtk], writes=["ps4"])
                    V("act", "activation", ["ps4"], ["Bbuf"], out=Bbuf.ap([[GQ, 2], [2 * GQ, 128]], gl, 0, 64),
                      in_=p4.ap([[128, 2], [1, 128]], 0, 0, 64), func=AF.Identity)
                    V("act", "activation", ["ps4"], ["Bbuf"], out=Bbuf.ap([[GQ, 2], [-2 * GQ, 128]], 127 * 2 * GQ + gl, 64, 64),
                      in_=p4.ap([[128, 2], [1, 128]], 0, 64, 64), func=AF.Identity)
                    wg = Wg[gl % 2]; wk = "Wg%d" % (gl % 2)
                    for d in range(2):
                        pw_ = PS(d, [2, 256])
                        for jt in range(2):
                            for ri in range(2):
                                P.op("pe", lambda e, pw_=pw_, d=d, jt=jt, ri=ri, gl=gl, LQ=LQ: e.matmul(
                                    pw_[:, jt, :], LQ[ri].ap([[1, 128]], jt * 128, 64 * d, 64),
                                    RQ[ri].ap([[1, 256]], gl * 256, 64 * d, 64), start=(ri == 0), stop=(ri == 1)),
                                     reads=["L0" + lsuf, "L1" + lsuf, "R0_%d" % gl, "R1_%d" % gl], writes=["ps%d" % d])
                            V("dve", "tensor_tensor", ["ps%d" % d, "wmask"], [wk + "_%d" % (d * 2 + jt)], out=wg[:, d * 2 + jt, :],
                              in0=pw_[:, jt, :], in1=wmask[:, d * 2 + jt, :], op=ALU.mult)
                    for jt in range(2):
                        V("dve", "scalar_tensor_tensor", [wk + "_%d" % jt, "wmask", "Dcol"], [wk + "_%d" % jt], out=wg[:, jt, :],
                          in0=wmask[:, 4 + jt, :], scalar=Dcol[:, g:g + 1], in1=wg[:, jt, :], op0=ALU.mult, op1=ALU.add)

                def stage3(gl):
                    g = g0 + gl
                    xt = XT[gl % 3]; xtk = "XT%d" % (gl % 3)
                    wg = Wg[gl % 2]; wk = "Wg%d" % (gl % 2)
                    pb = 5 + gl % 2
                    py = PS(pb, [256])
                    n_mm = 0
                    for d in range(2):
                        for jt in range(2):
                            P.op("pe", lambda e, py=py, d=d, jt=jt, xt=xt, wg=wg, n_mm=n_mm: e.matmul(
                                py[:], xt[:, jt, :], wg[:, d * 2 + jt, :], start=(n_mm == 0), stop=(n_mm == 3)),
                                 reads=[xtk] + [wk + "_%d" % q for q in range(4)], writes=["ps%d" % pb])
                            n_mm += 1
                    V("act", "activation", ["ps%d" % pb], ["yR"], out=yR.ap([[1024, 16], [1, 16]], g * 16),
                      in_=py.ap([[16, 16], [1, 16]]), func=AF.Identity)

                for t in range(GQ + 3):
                    if 0 <= t - 2 < GQ:
                        stage2(t - 2)
                    if t < GQ:
                        stage0(t)
                    if 0 <= t - 1 < GQ:
                        stage1(t - 1)
                    if 0 <= t - 3 < GQ:
                        stage3(t - 3)
                stop(5)
                P.barrier()
                P.dma("sp", lambda e, Q=Q: e.dma_start(out=VV[0].ap([[GQ, 2], [1, GQ]]),
                                                       in_=h0_d[jj, :, :, Q * GQ:(Q + 1) * GQ]), writes=["VV0"])
                lam = lambda T: T.ap([[34, GQ]], g0 * 34 + 1)
                V("dve", "tensor_copy", ["TRE"], ["C4"], out=C4.ap([[G_, 2], [1, G_]]), in_=TRE.ap([[0, 2], [34, GQ]], g0 * 34 + 1))
                V("dve", "tensor_copy", ["TIM"], ["C4"], out=C4[:, 2 * G_:3 * G_], in_=lam(TIM))
                V("dve", "tensor_scalar", ["TIM"], ["C4"], out=C4[:, 3 * G_:4 * G_], in0=lam(TIM), scalar1=-1.0, scalar2=None, op0=ALU.mult)
                P.op("pool", lambda e: e.memset(ZC[:, 0, 0:W2], 1.0), writes=["ZC0"])
                P.op("pool", lambda e: e.memset(ZC[:, 0, W2:W4], 0.0), writes=["ZC0"])
                P.op("pool", lambda e: e.memset(Zc[0][:, 0:G_], 1.0), writes=["Zc0"])
                P.op("pool", lambda e: e.memset(Zc[0][:, G_:W2], 0.0), writes=["Zc0"])
                for j in range(1, 17):
                    zp, zn = Zc[(j - 1) % 2], Zc[j % 2]
                    kzp, kzn = "Zc%d" % ((j - 1) % 2), "Zc%d" % (j % 2)
                    V("pool", "tensor_tensor", [kzp, "C4"], ["prodz"], out=prodz.ap([[W2, 2], [1, W2]]),
                      in0=zp.ap([[0, 2], [1, W2]]), in1=C4.ap([[W2, 2], [1, W2]]), op=ALU.mult)
                    V("pool", "tensor_tensor", ["prodz"], [kzn], out=zn.ap([[G_, 2], [1, G_]]), in0=prodz.ap([[G_, 2], [1, G_]]),
                      in1=prodz.ap([[-G_, 2], [1, G_]], 3 * G_), op=ALU.add)
                    dst = C256 if j == 16 else None
                    dk = "C256" if j == 16 else "ZC%d" % j
                    oa = (lambda a, n: C256.ap([[1, n]], a)) if j == 16 else (lambda a, n, j=j: ZC.ap([[1, n]], j * W4 + a))
                    V("pool", "tensor_copy", [kzn], [dk], out=(C256.ap([[G_, 2], [1, G_]]) if j == 16 else ZC.ap([[G_, 2], [1, G_]], j * W4)),
                      in_=zn.ap([[0, 2], [1, G_]]))
                    V("pool", "tensor_copy", [kzn], [dk], out=oa(2 * G_, G_), in_=zn[:, G_:W2])
                    V("pool", "tensor_scalar", [kzn], [dk], out=oa(3 * G_, G_), in0=zn[:, G_:W2], scalar1=-1.0, scalar2=0.0,
                      op0=ALU.mult, op1=ALU.add)
                P.op("dve", lambda e: e.memset(UU[0][:], 0.0), writes=["UU0"])
                u3 = [[W2, 8], [G_, 2], [1, G_]]
                for j in range(16):
                    up, un = UU[j % 3], UU[(j + 1) % 3]
                    kp, kn = "UU%d" % (j % 3), "UU%d" % ((j + 1) % 3)
                    V("dve", "tensor_tensor", [kp, "Bbuf"], ["T2"], out=T2.ap([[W2, 8], [1, W2]]), in0=up.ap([[W2, 8], [1, W2]]),
                      in1=Bbuf.ap([[16 * W2, 8], [1, W2]], j * W2), op=ALU.add)
                    V("dve", "tensor_tensor", ["T2", "C4"], ["prod"], out=prod.ap([[W4, 8], [W2, 2], [1, W2]]),
                      in0=T2.ap([[W2, 8], [0, 2], [1, W2]]), in1=C4.ap([[0, 8], [W2, 2], [1, W2]]), op=ALU.mult)
                    V("dve", "tensor_tensor", ["prod"], [kn], out=un.ap(u3), in0=prod.ap([[W4, 8], [G_, 2], [1, G_]]),
                      in1=prod.ap([[W4, 8], [-G_, 2], [1, G_]], 3 * G_), op=ALU.add)
                    if j < 15:
                        V("act", "activation", [kn], ["Pbuf"], out=Pbuf.ap([[16, 8], [GQ * 128, 2], [128, GQ]], j + 1, 0, 64),
                          in_=un.ap(u3, 0, 0, 64), func=AF.Identity)
                        V("act", "activation", [kn], ["Pbuf"], out=Pbuf.ap([[-16, 8], [GQ * 128, 2], [128, GQ]], 127 - (j + 1), 64, 64),
                          in_=un.ap(u3, 0, 64, 64), func=AF.Identity)
                U15, kU15 = UU[16 % 3], "UU%d" % (16 % 3)
                for b in range(8):
                    vp, vn = VV[b % 3], VV[(b + 1) % 3]
                    kvp, kvn = "VV%d" % (b % 3), "VV%d" % ((b + 1) % 3)
                    V("dve", "tensor_scalar", [kvp, "mrow"], ["Pin"], out=Pin[:, b, :], in0=vp[:], scalar1=mrow[:, 16 * b:16 * b + 1],
                      scalar2=None, op0=ALU.mult)
                    V("dve", "tensor_tensor", ["Pin", "C256"], ["prodb"], out=prodb.ap([[W2, 2], [1, W2]]),
                      in0=Pin.ap([[0, 2], [1, W2]], b * W2), in1=C256.ap([[W2, 2], [1, W2]]), op=ALU.mult)
                    V("dve", "tensor_tensor", ["prodb"], ["tb"], out=tb.ap([[G_, 2], [1, G_]]), in0=prodb.ap([[G_, 2], [1, G_]]),
                      in1=prodb.ap([[-G_, 2], [1, G_]], 3 * G_), op=ALU.add)
                    V("dve", "tensor_tensor", ["tb", kU15], [kvn], out=vn[:], in0=tb[:], in1=U15[:, b, :], op=ALU.add)
                    for hh in range(2):
                        p0 = 64 * hh
                        bb = b if hh == 0 else 7 - b
                        V("act", "activation", [kvn], ["Sfin"], out=Sfin.ap([[64, 2], [1, GQ]], bb * 128 + g0, p0, 64),
                          in_=vn.ap([[GQ, 2], [1, GQ]], 0, p0, 64), func=AF.Identity)
                for j in range(16):
                    pj, cj = prodj[j % 2], corr[j % 2]
                    kpj, kcj = "prodj%d" % (j % 2), "corr%d" % (j % 2)
                    V("dve", "tensor_tensor", ["Pin", "ZC%d" % j], [kpj], out=pj.ap([[W4, 8], [W2, 2], [1, W2]]),
                      in0=Pin.ap([[W2, 8], [0, 2], [1, W2]]), in1=ZC.ap([[0, 8], [W2, 2], [1, W2]], j * W4), op=ALU.mult)
                    V("dve", "tensor_tensor", [kpj], [kcj], out=cj.ap(u3), in0=pj.ap([[W4, 8], [G_, 2], [1, G_]]),
                      in1=pj.ap([[W4, 8], [-G_, 2], [1, G_]], 3 * G_), op=ALU.add)
                    for hh in range(2):
                        p0 = 64 * hh
                        if hh == 0:
                            pap = Pbuf.ap([[16, 8], [GQ * 128, 2], [128, GQ]], j, 0, 64)
                        else:
                            pap = Pbuf.ap([[-16, 8], [GQ * 128, 2], [128, GQ]], 127 - j, 64, 64)
                        if j == 0:
                            V("act", "activation", [kcj], ["Pbuf"], out=pap, in_=cj.ap(u3, 0, p0, 64), func=AF.Identity)
                        else:
                            V("pool", "tensor_tensor", [kcj, "Pbuf"], ["Pbuf"], out=pap, in0=pap, in1=cj.ap(u3, 0, p0, 64), op=ALU.add)
                P.barrier()
                stop(6)
                for gl in range(GQ):
                    g = g0 + gl
                    pb = 5 + gl % 2
                    py = PS(pb, [256])
                    yt, yu, yv = ytl[gl % 2], yul[gl % 2], yvl[gl % 2]
                    ks = "_%d" % (gl % 2)
                    for ri in range(2):
                        P.op("pe", lambda e, py=py, ri=ri, gl=gl: e.matmul(
                            py[:], Pbuf[:, ri, gl, :], RQ[ri][:, gl, :], start=(ri == 0), stop=(ri == 1)),
                             reads=["Pbuf", "R0_%d" % gl, "R1_%d" % gl], writes=["ps%d" % pb])
                    V("dve", "tensor_tensor", ["ps%d" % pb, "yR"], ["yu" + ks], out=yu.ap([[16, 16], [1, 16]]),
                      in0=py.ap([[16, 16], [1, 16]]), in1=yR.ap([[1024, 16], [1, 16]], g * 16), op=ALU.add)
                    V("act", "activation", ["yu" + ks], ["yt" + ks], out=yt[:], in_=yu[:], func=AF.Square)
                    V("dve", "tensor_scalar", ["yt" + ks], ["yt" + ks], out=yt[:], in0=yt[:], scalar1=0.044715, scalar2=1.0, op0=ALU.mult, op1=ALU.add)
                    V("dve", "tensor_tensor", ["yt" + ks, "yu" + ks], ["yt" + ks], out=yt[:], in0=yt[:], in1=yu[:], op=ALU.mult)
                    V("act", "activation", ["yt" + ks], ["yv" + ks], out=yv[:], in_=yt[:], func=AF.Sigmoid, scale=1.5957691216057308)
                    V("dve", "tensor_tensor", ["yv" + ks, "yu" + ks], ["yR"], out=yR.ap([[1024, 16], [1, 16]], g * 16),
                      in0=yv.ap([[16, 16], [1, 16]]), in1=yu.ap([[16, 16], [1, 16]]), op=ALU.mult)
            A.release()
            P.barrier()
            stop(7)
            A.mark()
            SfT = A.alloc([16, 128])
            for bq in range(8):
                for ri in range(2):
                    pt = PS(ri, [128])
                    V("pe", "transpose", ["Sfin", "ident"], ["ps%d" % ri], out=pt[0:64, :],
                      in_=Sfin.ap([[1, 64]], bq * 128 + ri * 64), identity=ident[:])
                    V("act", "activation", ["ps%d" % ri], ["SfT"], out=SfT[0:64, bq * 2 + ri, :], in_=pt[0:64, :], func=AF.Identity)
            for ri in range(2):
                for d in range(2):
                    P.dma("sp", lambda e, ri=ri, d=d: e.dma_start(
                        out=st_d[ri][:, jj, d, :, :].rearrange("b g n -> g b n"),
                        in_=SfT.ap([[256, 8], [1, 64]], ri * 128 + d * 64, 0, 64)), reads=["SfT"], is_output=True)
            stop(8)
            gw = A.alloc([2, 8, 512], BF16)
            gb = A.alloc([2, 512], BF16)
            yT = [A.alloc([8, 128], BF16) for _ in range(2)]
            sgm = [A.alloc([512]) for _ in range(2)]
            load_ln(l, 0)
            for cbk in range(2):
                for vg in range(2):
                    c0 = vg * D + cbk * 512
                    P.dma("pool", lambda e, vg=vg, c0=c0: e.dma_start(
                        out=gw[:, vg, :, :], in_=gluw_d[jj, :, c0:c0 + 512].rearrange("(t p) n -> p t n", p=128)), writes=["gw"])
                    P.dma("pool", lambda e, vg=vg, c0=c0: e.dma_start(out=gb[0:1, vg, :], in_=glub_d[jj:jj + 1, c0:c0 + 512]),
                          writes=["gb"])
                def glu_tr(s):
                    yt_ = yT[s % 2]; ytk = "yT%d" % (s % 2)
                    ptb = PS(s % 2, [8, 128], BF16)
                    for ct in range(8):
                        P.op("pe", lambda e, ptb=ptb, ct=ct, s=s: e.transpose(ptb[:, ct, :], yR[:, s, ct * 128:(ct + 1) * 128], identb[:]),
                             reads=["yR", "identb"], writes=["ps%d" % (s % 2)], signal=(ct in (0, 7)))
                    V("act", "activation", ["ps%d" % (s % 2)], [ytk], out=yt_[:], in_=ptb[:], func=AF.Identity)

                def glu_mm(s, cbk=cbk):
                    yt_ = yT[s % 2]; ytk = "yT%d" % (s % 2)
                    pv, pg_ = PS(2 + 2 * (s % 2), [512]), PS(3 + 2 * (s % 2), [512])
                    kv, kg = "ps%d" % (2 + 2 * (s % 2)), "ps%d" % (3 + 2 * (s % 2))
                    for (pp, kq, vg) in ((pv, kv, 0), (pg_, kg, 1)):
                        for ct in range(8):
                            P.op("pe", lambda e, pp=pp, ct=ct, vg=vg, yt_=yt_: e.matmul(
                                pp[:], yt_[:, ct, :], gw[:, vg, ct, :], start=(ct == 0), stop=False),
                                 reads=[ytk, "gw"], writes=[kq], signal=(ct == 0))
                        P.op("pe", lambda e, pp=pp, vg=vg: e.matmul(pp[:], onesb[0:1, :], gb[0:1, vg, :], start=False, stop=True),
                             reads=[ytk, "gw", "gb", "onesb"], writes=[kq])
                    sg_ = sgm[s % 2]; sgk = "sgm%d" % (s % 2)
                    V("act", "activation", [kg], [sgk], out=sg_[:], in_=pg_[:], func=AF.Sigmoid)
                    V("dve", "tensor_tensor", [kv, sgk], [sgk], out=sg_[:], in0=pv[:], in1=sg_[:], op=ALU.mult)
                    V("dve", "tensor_tensor", [sgk, "grow0"], [sgk], out=sg_[:], in0=sg_[:], in1=grow[:, 0, cbk * 512:(cbk + 1) * 512], op=ALU.mult)
                    V("dve", "scalar_tensor_tensor", [sgk, "x%d" % s], ["x%d" % s], out=x[:, s, cbk * 512:(cbk + 1) * 512],
                      in0=x[:, s, cbk * 512:(cbk + 1) * 512], scalar=ALPHA, in1=sg_[:], op0=ALU.mult, op1=ALU.add)

                glu_tr(0)
                for s in range(16):
                    if s + 1 < 16:
                        glu_tr(s + 1)
                    glu_mm(s)
            layer_norm_batch(0, 16, pool_assist=True)
            A.release()
            A.release()
            P.barrier()

        ada_mod(0)
        for l in range(depth):
          try:
            mix = cfg.get("mix", "all")
            if l % 2 == 1 and mix in ("all", "pool"):
                mixer_pool(l)
            elif l % 2 == 0 and mix in ("all", "s5"):
                mixer_s5(l)
            else:
                mixer_none(l)
            ffn(l, l + 1 if l + 1 < depth else None)
          except StopBuild:
            P.barrier()
            break

        P.dma("sp", lambda e: e.dma_start(out=y_R, in_=x[:]), reads=["x"] + ["x%d" % s for s in range(16)],
              is_output=True)
        P.finish()
        P.replay()
    return nc


def _pool_consts(sample):
    pm = np.zeros((4, 3, 128, 128), np.float32)
    ic = np.zeros((128, 16, 4), np.float32)
    for g, k in enumerate(WINDOWS):
        for kq in range(128):
            if sample:
                row, cb = divmod(kq, 4)
                rows = range(max(0, row - k // 2), min(32, row - k // 2 + k))
                for a, off in ((0, 0), (1, 1), (2, -1)):
                    cbm = cb + off
                    if 0 <= cbm < 4:
                        for r in rows:
                            pm[g, a, r * 4 + cbm, kq] = 1.0
                for sq in range(16):
                    col = cb * 16 + sq
                    ccnt = min(64, col - k // 2 + k) - max(0, col - k // 2)
                    ic[kq, sq, g] = 1.0 / (len(rows) * ccnt)
            else:
                seq, cpos = divmod(kq, 16)
                for a, off in ((0, 0), (1, 1), (2, -1)):
                    m = cpos + off
                    if 0 <= m < 16:
                        pm[g, a, seq * 16 + m, kq] = 1.0
                for sq in range(16):
                    t = cpos * 16 + sq
                    cnt = min(256, t - k // 2 + k) - max(0, t - k // 2)
                    ic[kq, sq, g] = 1.0 / cnt
    return pm.reshape(12, 128, 128), ic


def _s5_consts():
    expo = np.zeros((128, 34), np.float32)
    expo[:, 0], expo[:, 1] = 1.0, 16.0
    sidx = np.arange(16, dtype=np.float32)
    expo[:64, 2:18] = -(sidx + 1)
    expo[64:, 2:18] = -(16 - sidx)
    expo[:64, 18:34] = sidx + 1
    expo[64:, 18:34] = 16 - sidx
    wm = np.zeros((128, 6, 256), np.float32)
    for p in range(128):
        s8, c = divmod(p, 16)
        for jt in range(2):
            sv = 8 * jt + s8
            for sp in range(16):
                wm[p, 0 * 2 + jt, sp * 16:(sp + 1) * 16] = 1.0 if sp >= sv else 0.0
                wm[p, 1 * 2 + jt, sp * 16:(sp + 1) * 16] = 1.0 if sp <= sv else 0.0
            wm[p, 4 + jt, sv * 16 + c] = 1.0
    return expo, wm


def kernel(**inputs):
    cfg = inputs.pop("_cfg", {})
    f = lambda a: np.ascontiguousarray(np.asarray(a, dtype=np.float32))
    x_prompt, x_sample = f(inputs["x_prompt"]), f(inputs["x_sample"])
    c, c_ctx = f(inputs["c"]), f(inputs["c_ctx"])
    shared = {k: f(inputs[k]) for k in ("ada_w", "ada_b", "ln_g", "ln_b", "ffn_w_gate", "ffn_w_up", "ffn_w_down",
                                         "pool_w", "pool_scale")}
    shared["ident"] = np.eye(128, dtype=np.float32)
    pidx = np.arange(128)
    shared["selT"] = (pidx[:, None] % 16 == pidx[None, :] % 16).astype(np.float32)
    shared["maskG"] = (pidx[:, None] // 16 == np.arange(8)[None, :]).astype(np.float32)
    in_maps = []
    for core in range(NCORES):
        m = dict(shared)
        if core < 4:
            m["x"] = x_sample[core]
            cond = c[core]
        else:
            m["x"] = x_prompt[(core - 4) * 8:(core - 3) * 8].reshape(2048, D)
            cond = c_ctx
        m["cond"] = np.ascontiguousarray(cond.reshape(8, 128).T)
        m["pmat"], m["icnt"] = _pool_consts(core < 4)
        for k in ("ssm_a_re", "ssm_a_im", "ssm_log_step", "ssm_b_re", "ssm_b_im", "ssm_c_re", "ssm_c_im", "ssm_d",
                  "ssm_glu_w", "ssm_glu_b"):
            m[k] = f(inputs[k])
        h0 = np.zeros((2, 128, 2, 64), np.float32)
        mrow = np.ones((128, 128), np.float32)
        if core < 4:
            for ri, key in enumerate(("state_ssm_re", "state_ssm_im")):
                stt = f(inputs[key])[core]
                h0[:, :, ri, :] = stt.transpose(0, 1, 3, 2).reshape(2, 128, 64)
        else:
            mrow[:, ::16] = 0.0
        m["h0"], m["mrow"] = h0, mrow
        m["expo"], m["wmask"] = _s5_consts()
        in_maps.append(m)
    nc = build(cfg)
    if cfg.get("trace"):
        res = run_bass_kernel_spmd(nc, in_maps, core_ids=list(range(NCORES)), trace=True)
        return res
    if cfg.get("cores"):
        in_maps = [in_maps[i] for i in cfg["cores"]]
        res = run_bass_kernel_spmd(nc, in_maps, core_ids=list(range(len(in_maps))))
        return res
    res = run_bass_kernel_spmd(nc, in_maps, core_ids=list(range(NCORES)))
    ys = [r["y"] for r in res.results]
    y_sample = np.stack(ys[:4], 0)
    y_prompt = np.concatenate(ys[4:], 0).reshape(32, 256, D)
    st_re = np.concatenate([r["st_re"] for r in res.results[4:]], 0)
    st_im = np.concatenate([r["st_im"] for r in res.results[4:]], 0)
    return (y_prompt, y_sample, st_re, st_im)
```
